# Optimizing a Trainium2 kernel written in Bass

```python
import math
import jax, jax.numpy as jnp
from jax import lax
import numpy as np

D_MODEL = 2048
BATCH = 1
SEQ = 8192
DEPTH = 1

HEAD_DIM = 64
ATTN_WIDTH = D_MODEL // 2
N_ATTN_HEADS = ATTN_WIDTH // HEAD_DIM
N_KV_HEADS = N_ATTN_HEADS // 8
KV_WIDTH = N_KV_HEADS * HEAD_DIM
WINDOW = 128
BLOCK = 128

SSM_WIDTH = D_MODEL - ATTN_WIDTH
SSM_GROUP = 16
SSM_GROUPS = SSM_WIDTH // SSM_GROUP
SSM_STATE = 64
DT_MIN = 0.001
DT_MAX = 0.1

MIX_WIDTH = ATTN_WIDTH + SSM_WIDTH
IN_WIDTH = ATTN_WIDTH + 2 * KV_WIDTH + ATTN_WIDTH + 2 * SSM_WIDTH
SPLITS = (ATTN_WIDTH,
          ATTN_WIDTH + KV_WIDTH,
          ATTN_WIDTH + 2 * KV_WIDTH,
          2 * ATTN_WIDTH + 2 * KV_WIDTH,
          2 * ATTN_WIDTH + 2 * KV_WIDTH + SSM_WIDTH)
NORM_EPS = 1e-6
NEG_INF = -1e30

kernel_name = "hymba_s5_swa_sink_alibi_layer"


def _rmsnorm(x, w):
    xf = x.astype(jnp.float32)
    r = lax.rsqrt(jnp.mean(xf * xf, axis=-1, keepdims=True) + NORM_EPS)
    return (xf * r).astype(x.dtype) * w


def _alibi_slopes(n_heads):
    h = jnp.arange(1, n_heads + 1, dtype=jnp.float32)
    return jnp.exp2(-8.0 * h / n_heads)


def _sliding_window_attention(q, k, v, sinks):
    b, l = q.shape[0], q.shape[1]
    nb = l // BLOCK
    g = N_ATTN_HEADS // N_KV_HEADS
    qb = q.reshape(b, nb, BLOCK, N_KV_HEADS, g, HEAD_DIM)
    kb = k.reshape(b, nb, BLOCK, N_KV_HEADS, HEAD_DIM)
    vb = v.reshape(b, nb, BLOCK, N_KV_HEADS, HEAD_DIM)

    def with_prev(t):
        prev = jnp.pad(t, ((0, 0), (1, 0), (0, 0), (0, 0), (0, 0)))[:, :-1]
        return jnp.concatenate([prev, t], axis=2)

    kw, vw = with_prev(kb), with_prev(vb)
    scores = jnp.einsum('bnqhgd,bnshd->bnhgqs', qb, kw).astype(jnp.float32) * (HEAD_DIM ** -0.5)

    qi = jnp.arange(BLOCK)[:, None]
    si = jnp.arange(2 * BLOCK)[None, :]
    dist = qi + BLOCK - si
    blk = jnp.arange(nb)[:, None, None]
    valid = (dist >= 0) & (dist < WINDOW) & (blk * BLOCK - BLOCK + si >= 0)

    slopes = _alibi_slopes(N_ATTN_HEADS).reshape(N_KV_HEADS, g)
    bias = -slopes[:, :, None, None] * dist.astype(jnp.float32)
    scores = jnp.where(valid[None, :, None, None], scores + bias, NEG_INF)

    sink = sinks.astype(jnp.float32).reshape(N_KV_HEADS, g)[None, None, :, :, None, None]
    sink = jnp.broadcast_to(sink, scores.shape[:-1] + (1,))
    probs = jax.nn.softmax(jnp.concatenate([scores, sink], axis=-1), axis=-1)[..., :-1]
    out = jnp.einsum('bnhgqs,bnshd->bnqhgd', probs.astype(v.dtype), vw)
    return out.reshape(b, l, N_ATTN_HEADS * HEAD_DIM)


def _complex_affine_combine(e1, e2):
    a1r, a1i, b1r, b1i = e1
    a2r, a2i, b2r, b2i = e2
    ar = a2r * a1r - a2i * a1i
    ai = a2r * a1i + a2i * a1r
    br = a2r * b1r - a2i * b1i + b2r
    bi = a2r * b1i + a2i * b1r + b2i
    return (ar, ai, br, bi)


def _s5(u, a_re, a_im, log_dt, b_re, b_im, c_re, c_im, d):
    f32 = jnp.float32
    bsz, l = u.shape[0], u.shape[1]
    ug = u.astype(f32).reshape(bsz, l, SSM_GROUPS, SSM_GROUP)
    a_re, a_im = a_re.astype(f32), a_im.astype(f32)
    dt = jnp.exp(log_dt.astype(f32))[:, None]
    mag = jnp.exp(a_re * dt)
    abar_re = mag * jnp.cos(a_im * dt)
    abar_im = mag * jnp.sin(a_im * dt)
    nr, ni = abar_re - 1.0, abar_im
    den = a_re * a_re + a_im * a_im
    coef_re = (nr * a_re + ni * a_im) / den
    coef_im = (ni * a_re - nr * a_im) / den
    b_re, b_im = b_re.astype(f32), b_im.astype(f32)
    bbar_re = coef_re[..., None] * b_re - coef_im[..., None] * b_im
    bbar_im = coef_re[..., None] * b_im + coef_im[..., None] * b_re
    bu_re = jnp.einsum('blgc,gpc->blgp', ug, bbar_re)
    bu_im = jnp.einsum('blgc,gpc->blgp', ug, bbar_im)
    ar = jnp.broadcast_to(abar_re, bu_re.shape)
    ai = jnp.broadcast_to(abar_im, bu_re.shape)
    _, _, s_re, s_im = lax.associative_scan(_complex_affine_combine, (ar, ai, bu_re, bu_im), axis=1)
    y = (jnp.einsum('gcp,blgp->blgc', c_re.astype(f32), s_re)
         - jnp.einsum('gcp,blgp->blgc', c_im.astype(f32), s_im)
         + d.astype(f32).reshape(SSM_GROUPS, SSM_GROUP) * ug)
    return y.reshape(bsz, l, SSM_WIDTH).astype(u.dtype)


def setup_inputs(seed: int = 0) -> dict:
    key = jax.random.key(seed)
    ks = jax.random.split(key, 20)
    f32 = jnp.float32
    n = jnp.arange(SSM_STATE, dtype=f32)
    x = jax.random.normal(ks[0], (BATCH, SEQ, D_MODEL), f32)
    norm_w = 1.0 + 0.02 * jax.random.normal(ks[1], (D_MODEL,), f32)
    w_in = jax.random.normal(ks[2], (D_MODEL, IN_WIDTH), f32) * D_MODEL ** -0.5
    q_norm_w = 1.0 + 0.02 * jax.random.normal(ks[3], (HEAD_DIM,), f32)
    k_norm_w = 1.0 + 0.02 * jax.random.normal(ks[4], (HEAD_DIM,), f32)
    attn_sinks = 0.5 * jax.random.normal(ks[5], (N_ATTN_HEADS,), f32)
    ssm_a_re = -0.5 + 0.01 * jax.random.normal(ks[6], (SSM_GROUPS, SSM_STATE), f32)
    ssm_a_im = math.pi * n[None, :] + 0.01 * jax.random.normal(ks[7], (SSM_GROUPS, SSM_STATE), f32)
    ssm_log_dt = jax.random.uniform(ks[8], (SSM_GROUPS,), f32, math.log(DT_MIN), math.log(DT_MAX))
    b_scale = (2.0 * SSM_GROUP) ** -0.5
    ssm_b_re = jax.random.normal(ks[9], (SSM_GROUPS, SSM_STATE, SSM_GROUP), f32) * b_scale
    ssm_b_im = jax.random.normal(ks[10], (SSM_GROUPS, SSM_STATE, SSM_GROUP), f32) * b_scale
    c_scale = (2.0 * SSM_STATE) ** -0.5
    ssm_c_re = jax.random.normal(ks[11], (SSM_GROUPS, SSM_GROUP, SSM_STATE), f32) * c_scale
    ssm_c_im = jax.random.normal(ks[12], (SSM_GROUPS, SSM_GROUP, SSM_STATE), f32) * c_scale
    ssm_d = 0.5 * jax.random.normal(ks[13], (SSM_WIDTH,), f32)
    w_glu = jax.random.normal(ks[14], (SSM_WIDTH, SSM_WIDTH), f32) * SSM_WIDTH ** -0.5
    b_glu = 0.01 * jax.random.normal(ks[15], (SSM_WIDTH,), f32)
    w_out = jax.random.normal(ks[16], (MIX_WIDTH, D_MODEL), f32) * MIX_WIDTH ** -0.5
    return {"x": x, "norm_w": norm_w, "w_in": w_in, "q_norm_w": q_norm_w, "k_norm_w": k_norm_w,
            "attn_sinks": attn_sinks, "ssm_a_re": ssm_a_re, "ssm_a_im": ssm_a_im,
            "ssm_log_dt": ssm_log_dt, "ssm_b_re": ssm_b_re, "ssm_b_im": ssm_b_im,
            "ssm_c_re": ssm_c_re, "ssm_c_im": ssm_c_im, "ssm_d": ssm_d,
            "w_glu": w_glu, "b_glu": b_glu, "w_out": w_out}


def reference(x, norm_w, w_in, q_norm_w, k_norm_w, attn_sinks, ssm_a_re, ssm_a_im, ssm_log_dt,
              ssm_b_re, ssm_b_im, ssm_c_re, ssm_c_im, ssm_d, w_glu, b_glu, w_out):
    bsz, l = x.shape[0], x.shape[1]
    for _ in range(DEPTH):
        h = _rmsnorm(x, norm_w)
        proj = h @ w_in
        q, k, v, z_attn, u, z_ssm = jnp.split(proj, SPLITS, axis=-1)

        q = _rmsnorm(q.reshape(bsz, l, N_ATTN_HEADS, HEAD_DIM), q_norm_w)
        k = _rmsnorm(k.reshape(bsz, l, N_KV_HEADS, HEAD_DIM), k_norm_w)
        v = v.reshape(bsz, l, N_KV_HEADS, HEAD_DIM)
        attn = _sliding_window_attention(q, k, v, attn_sinks) * jax.nn.silu(z_attn)

        y = _s5(u, ssm_a_re, ssm_a_im, ssm_log_dt, ssm_b_re, ssm_b_im, ssm_c_re, ssm_c_im, ssm_d)
        y = jax.nn.gelu(y)
        y = y * jax.nn.sigmoid(y @ w_glu + b_glu)
        ssm = y * jax.nn.silu(z_ssm)

        mixed = jnp.concatenate([attn, ssm], axis=-1) @ w_out
        x = x + mixed
    return x
```

```python
import math
from contextlib import ExitStack
import numpy as np
import concourse.bass as bass
import concourse.mybir as mybir
from concourse.bass_utils import run_bass_kernel_spmd

F32 = mybir.dt.float32
BF16 = mybir.dt.bfloat16
I32 = mybir.dt.int32
AF = mybir.ActivationFunctionType
ALU = mybir.AluOpType
AX = mybir.AxisListType

NCORES = 8
D_MODEL = 2048
SEQ = 8192
TOK = SEQ // NCORES
HALO = 128
NT = TOK // 128
NPRE = (SEQ - TOK) // 128
NBATCH = 4
KC = D_MODEL // 128
T = 4
NCH = TOK // T
IN_W = 4352
C_Q, C_K, C_V, C_ZA, C_U, C_ZS = 0, 1024, 1152, 1280, 2304, 3328
EPS = 1e-6
TWO_PI = 2.0 * math.pi
MAGIC = 12582912.0
ENGS = ("pe", "act", "dve", "pool", "sp")
DEBUG = {}
SAME_WAIT = ("dve", "act")
BIG_OP = 512


class Sched:
    def __init__(self, nc):
        self.nc = nc
        self.ops = []
        self.last_w = {}
        self.readers = {}
        self.seen = set()
        self.fence = set()
        self.capture = None
        self.pending = []

    def mark(self):
        if self.capture:
            self.pending.append(None)

    def replay(self, k):
        cap, self.capture = self.capture, None
        n = 0
        while self.pending:
            a = self.pending.pop(0)
            if a is None:
                if n >= k:
                    break
                continue
            self.op(*a)
            n += 1
        self.capture = cap

    def retire(self, keys):
        for k in keys:
            w = self.last_w.pop(k, None)
            if w is not None:
                self.fence.add(w)
            for r in self.readers.pop(k, ()):
                self.fence.add(r)

    def op(self, eng, fn, reads=(), writes=(), dsem=None, nfree=0):
        if self.capture:
            self.pending.append((eng, fn, list(reads), list(writes), dsem, nfree))
            return None
        bk = [k for k in reads if isinstance(k, str) and k.startswith("bank")]
        if bk:
            reads = [k for k in reads if k not in bk]
            writes = list(writes) + [k for k in bk if k not in writes]
        idx = len(self.ops)
        deps = set()
        for k in list(reads) + list(writes):
            if k not in self.seen:
                self.seen.add(k)
                deps |= self.fence
        for k in reads:
            w = self.last_w.get(k)
            if w is not None:
                deps.add(w)
        for k in writes:
            w = self.last_w.get(k)
            if w is not None:
                deps.add(w)
            for r in self.readers.get(k, ()):
                deps.add(r)
        deps.discard(idx)
        self.ops.append(dict(eng=eng, fn=fn, deps=deps, dsem=dsem, need_inc=False, nfree=nfree))
        for k in reads:
            self.readers.setdefault(k, []).append(idx)
        for k in writes:
            self.last_w[k] = idx
            self.readers[k] = []
        return idx

    def emit(self, stack, final_wait_ops=()):
        nc = self.nc
        ops = self.ops

        def skip(do, o):
            if not (do["dsem"] is None and o["dsem"] is None and do["eng"] == o["eng"] and do["eng"] != "sp"):
                return False
            if do["eng"] not in DEBUG.get("same_wait", SAME_WAIT):
                return True
            return do["eng"] == "dve" and do["nfree"] >= DEBUG.get("big", BIG_OP)

        for o in ops:
            for d in o["deps"]:
                if not skip(ops[d], o):
                    ops[d]["need_inc"] = True
        for d in final_wait_ops:
            ops[d]["need_inc"] = True
        esem = {e: stack.enter_context(nc.semaphore("s_" + e)) for e in ENGS}
        dsems = {}
        for o in ops:
            if o["dsem"] is not None and o["dsem"] not in dsems:
                dsems[o["dsem"]] = stack.enter_context(nc.semaphore("d_" + o["dsem"]))
        ecount = {e: 0 for e in ENGS}
        dcount = {k: 0 for k in dsems}
        for o in ops:
            if o["dsem"] is not None:
                dcount[o["dsem"]] += 16
                o["ticket"] = (o["dsem"], dsems[o["dsem"]], dcount[o["dsem"]])
            elif o["need_inc"]:
                ecount[o["eng"]] += 1
                o["ticket"] = (o["eng"], esem[o["eng"]], ecount[o["eng"]])
            else:
                o["ticket"] = None
        block = stack.enter_context(nc.Block())
        per_eng = {e: [o for o in ops if o["eng"] == e] for e in ENGS}

        def run(engname, eng):
            waited = {}
            for o in per_eng[engname]:
                need = {}
                for d in o["deps"]:
                    do = ops[d]
                    if skip(do, o):
                        continue
                    name, sem, val = do["ticket"]
                    if waited.get(name, 0) >= val:
                        continue
                    if need.get(name, (None, 0))[1] < val:
                        need[name] = (sem, val)
                for name, (sem, val) in need.items():
                    waited[name] = val
                    eng.wait_ge(sem, val)
                ins = o["fn"](eng)
                if o["dsem"] is not None:
                    ins.then_inc(o["ticket"][1], 16)
                elif o["need_inc"]:
                    ins.then_inc(o["ticket"][1], 1)
            if engname == "sp":
                for d in final_wait_ops:
                    _, sem, val = ops[d]["ticket"]
                    eng.wait_ge(sem, val)

        @block.tensor
        def _(e):
            run("pe", e)

        @block.scalar
        def _(e):
            run("act", e)

        @block.vector
        def _(e):
            run("dve", e)

        @block.gpsimd
        def _(e):
            run("pool", e)

        @block.sync
        def _(e):
            run("sp", e)


class _Stop(Exception):
    pass


def build_nc(debug=(), stop=None):
    nc = bass.Bass("TRN2", target_bir_lowering=False)
    S = Sched(nc)

    def din(name, shape):
        return nc.dram_tensor(name, list(shape), F32, kind="ExternalInput")

    xpre = din("xpre", [NPRE * 128, D_MODEL])
    xmain = din("xmain", [HALO + TOK, D_MODEL])
    w_in = din("w_in", [D_MODEL, IN_W])
    w_glu = din("w_glu", [1024, 1024])
    w_out = din("w_out", [2048, 2048])
    normw_row = din("normw_row", [1, D_MODEL])
    qw2_d = din("qw2", [128, 1]); kw2_d = din("kw2", [128, 1])
    sink_d = din("sinkcol", [128, 8])
    are2_d = din("are2", [128, 32]); aim2_d = din("aim2", [128, 32]); ldt2_d = din("ldt2", [128, 32])
    bre2_d = din("bre2", [128, 32, 16]); bim2_d = din("bim2", [128, 32, 16])
    cre2_d = din("cre2", [128, 32, 16]); cim2_d = din("cim2", [128, 32, 16])
    are_row = din("are_row", [1, 4096]); aim_row = din("aim_row", [1, 4096]); ldt_row = din("ldt_row", [1, 64])
    dskip_d = din("dskip", [128, 8]); bglu_d = din("bglu", [128, 8]); flag_d = din("flag", [128, 1])
    out_d = nc.dram_tensor("out", [TOK, D_MODEL], F32, kind="ExternalOutput")
    dbg_out = {}

    fin = []
    top = ExitStack()

    def sbuf(st, name, shape, dt=F32):
        return st.enter_context(nc.sbuf_tensor("sb_" + name, list(shape), dt))

    def _nf(ap):
        try:
            n = ap.free_size
            return int(n() if callable(n) else n)
        except Exception:
            return 0

    def dma(q, out, in_, r, w, dsem):
        return S.op(q, lambda e: e.dma_start(out=out, in_=in_), reads=r, writes=w, dsem=dsem)

    def tt(out, a, b, op, r, w, eng="dve"):
        return S.op(eng, lambda e: e.tensor_tensor(out=out, in0=a, in1=b, op=op), reads=r, writes=w, nfree=_nf(out))

    def ts(out, a, s1, s2, op0, op1, r, w, eng="dve"):
        if s2 is None:
            return S.op(eng, lambda e: e.tensor_scalar(out=out, in0=a, scalar1=s1, scalar2=None, op0=op0), reads=r, writes=w, nfree=_nf(out))
        return S.op(eng, lambda e: e.tensor_scalar(out=out, in0=a, scalar1=s1, scalar2=s2, op0=op0, op1=op1), reads=r, writes=w, nfree=_nf(out))

    def stt(out, a, scalar, b, op0, op1, r, w, eng="dve"):
        return S.op(eng, lambda e: e.scalar_tensor_tensor(out=out, in0=a, scalar=scalar, in1=b, op0=op0, op1=op1), reads=r, writes=w, nfree=_nf(out))

    def cp(out, a, r, w, eng="dve"):
        if eng == "act":
            return S.op("act", lambda e: e.copy(out=out, in_=a), reads=r, writes=w)
        return S.op(eng, lambda e: e.tensor_copy(out=out, in_=a), reads=r, writes=w, nfree=_nf(out))

    def act(out, a, func, r, w, bias=None, scale=None, accum=None):
        kw = {}
        if bias is not None:
            kw["bias"] = bias
        if scale is not None:
            kw["scale"] = scale
        if accum is not None:
            kw["accum_out"] = accum
        return S.op("act", lambda e: e.activation(out=out, in_=a, func=func, **kw), reads=r, writes=w)

    def mm(out, lhsT, rhs, start, stop, r, w, tp=None):
        kw = {}
        if tp is not None:
            kw["tile_position"] = tp
        return S.op("pe", lambda e: e.matmul(out, lhsT=lhsT, rhs=rhs, start=start, stop=stop, skip_group_check=True, **kw), reads=r, writes=w)

    def trn(out, a, ident, r, w):
        return S.op("pe", lambda e: e.matmul(out, lhsT=a, rhs=ident, start=True, stop=True, is_transpose=True, skip_group_check=True),
                    reads=r, writes=w)

    def memset(ap, v, w, eng="pool"):
        return S.op(eng, lambda e: e.memset(ap, v), writes=w)

    def recip(out, a, r, w):
        return S.op("dve", lambda e: e.reciprocal(out=out, in_=a), reads=r, writes=w)

    def dbg(name, ap, shape, r):
        if name not in debug:
            return
        d = nc.dram_tensor("dbg_" + name, list(shape), ap.dtype, kind="ExternalOutput")
        dbg_out[name] = d
        fin.append(dma("sp", d.ap(), ap, r, [], "dbg_" + name))

    bank = [top.enter_context(nc.psum_tensor("bank%d" % i, [128, 512], F32)) for i in range(8)]
    BK = ["bank%d" % i for i in range(8)]

    P = top
    ident = sbuf(P, "ident", [128, 128])
    blockones = sbuf(P, "blockones", [128, 128])
    ones_bf = sbuf(P, "ones_bf", [128, 64], BF16)
    qw2 = sbuf(P, "qw2", [128, 1]); kw2 = sbuf(P, "kw2", [128, 1])
    sinkx = sbuf(P, "sinkx", [128, 8])
    dskip = sbuf(P, "dskip", [128, 8]); bglu = sbuf(P, "bglu", [128, 8]); flag = sbuf(P, "flag", [128, 1])
    epsc = sbuf(P, "epsc", [128, 1])
    halfpi = sbuf(P, "halfpi", [128, 1])
    are2 = sbuf(P, "are2", [128, 32]); aim2 = sbuf(P, "aim2", [128, 32]); dt2 = sbuf(P, "dt2", [128, 32])
    lam = sbuf(P, "lam", [128, 32]); th = sbuf(P, "th", [128, 32])
    pwr = sbuf(P, "pwr", [128, 5, 32]); pwi = sbuf(P, "pwi", [128, 5, 32])
    bbr = sbuf(P, "bbr", [128, 32, 16]); bbi = sbuf(P, "bbi", [128, 32, 16])
    cre2 = sbuf(P, "cre2", [128, 32, 16]); cim2 = sbuf(P, "cim2", [128, 32, 16])
    Pr = sbuf(P, "Pr", [128, 32]); Pi = sbuf(P, "Pi", [128, 32])
    rho = sbuf(P, "rho", [128, 32]); phi = sbuf(P, "phi", [128, 32])
    sin_re = sbuf(P, "sin_re", [128, 32]); sin_im = sbuf(P, "sin_im", [128, 32])
    rs_re = sbuf(P, "rs_re", [128, 32]); rs_im = sbuf(P, "rs_im", [128, 32])
    sinb_re = sbuf(P, "sinb_re", [128, 32], BF16); sinb_im = sbuf(P, "sinb_im", [128, 32], BF16)
    kk1 = sbuf(P, "kk1", [128, NCH])
    Bmat = sbuf(P, "Bmat", [128, 8, T, 2, 128], BF16)
    Kblk = sbuf(P, "Kblk", [128, 8, T, 128], BF16)
    NXB = 3
    ssum = [sbuf(P, "ssum%d" % i, [128, 1]) for i in range(NXB)]
    rstd = [sbuf(P, "rstd%d" % i, [128, 1]) for i in range(NXB)]
    HB = {}

    def alloc_ht(st, tag):
        HB["tag"] = tag
        HB["nw"] = sbuf(st, "nw_row" + tag, [128, D_MODEL])
        HB["xb"] = [sbuf(st, "xb%d%s" % (i, tag), [128, D_MODEL]) for i in range(NXB)]
        HB["sqj"] = sbuf(st, "sqj" + tag, [128, D_MODEL], BF16)
        dma("sp", HB["nw"][:], bass.AP(normw_row, 0, [[0, 128], [1, D_MODEL]]), [], ["nw_row" + tag], "p_nw" + tag)

    def ht_keys():
        t_ = HB["tag"]
        return ["nw_row" + t_, "sqj" + t_] + ["xb%d%s" % (i, t_) for i in range(NXB)]

    def pload(dst, src, key):
        dma("sp", dst, src, [], [key], "p_" + key)

    pload(qw2[:], qw2_d.ap(), "qw2"); pload(kw2[:], kw2_d.ap(), "kw2")
    pload(sinkx[:], sink_d.ap(), "sinkx"); pload(dskip[:], dskip_d.ap(), "dskip")
    pload(bglu[:], bglu_d.ap(), "bglu"); pload(flag[:], flag_d.ap(), "flag")
    pload(are2[:], are2_d.ap(), "are2"); pload(aim2[:], aim2_d.ap(), "aim2"); pload(dt2[:], ldt2_d.ap(), "dt2")
    pload(cre2[:], cre2_d.ap(), "cre2"); pload(cim2[:], cim2_d.ap(), "cim2")

    memset(ident[:], 0.0, ["ident"])
    S.op("pool", lambda e: e.affine_select(out=ident[:], in_=ident[:], pattern=[[-1, 128]], compare_op=ALU.not_equal,
                                           fill=1.0, base=0, channel_multiplier=1), reads=["ident"], writes=["ident"])
    memset(blockones[:], 0.0, ["blockones"])
    memset(blockones[0:64, 0:64], 1.0, ["blockones"])
    memset(blockones[64:128, 64:128], 1.0, ["blockones"])
    memset(ones_bf[:], 1.0, ["ones_bf"])
    memset(epsc[:], EPS, ["epsc"])
    memset(halfpi[:], 0.5 * math.pi, ["halfpi"])
    kk_i = sbuf(P, "kk_i", [128, NCH], I32)
    S.op("pool", lambda e: e.iota(kk_i[:], pattern=[[1, NCH]], base=1, channel_multiplier=0), writes=["kk_i"])
    cp(kk1[:], kk_i[:], ["kk_i"], ["kk1"])
    act(sinkx[:], sinkx[:], AF.Exp, ["sinkx"], ["sinkx"])

    _sc_cache = {}

    def sincos(st, tag, x, shape, rk, sin_out, cos_out, wk_s, wk_c, keys=None, eng="dve"):
        if tag not in _sc_cache:
            _sc_cache[tag] = (sbuf(st, tag + "_n1", shape), sbuf(st, tag + "_r1", shape))
        n1, r1 = _sc_cache[tag]
        k1, k2 = keys if keys is not None else (tag + "_n1", tag + "_r1")
        ts(n1[:], x, 1.0 / TWO_PI, MAGIC, ALU.mult, ALU.add, rk, [k1], eng=eng)
        ts(n1[:], n1[:], -MAGIC, -TWO_PI, ALU.add, ALU.mult, [k1], [k1], eng=eng)
        tt(r1[:], x, n1[:], ALU.add, rk + [k1], [k2], eng=eng)
        ts(r1[:], r1[:], 3.1415925, -3.1415925, ALU.min, ALU.max, [k2], [k2], eng=eng)
        act(sin_out, r1[:], AF.Sin, [k2], wk_s)
        act(n1[:], r1[:], AF.Abs, [k2], [k1])
        act(cos_out, n1[:], AF.Sin, [k1, "halfpi"], wk_c, bias=halfpi[:], scale=-1.0)

    _p0h = []

    def body():
      p0 = ExitStack()
      _p0h.append(p0)
      if True:
          act(dt2[:], dt2[:], AF.Exp, ["dt2"], ["dt2"])
          tt(lam[:], are2[:], dt2[:], ALU.mult, ["are2", "dt2"], ["lam"])
          tt(th[:], aim2[:], dt2[:], ALU.mult, ["aim2", "dt2"], ["th"])
          memset(pwr[:, 0, :], 1.0, ["pwr0"]); memset(pwi[:, 0, :], 0.0, ["pwi0"])
          ang = sbuf(p0, "ang", [128, 32]); mag = sbuf(p0, "mag", [128, 32])
          sn = sbuf(p0, "sn", [128, 32]); cs = sbuf(p0, "cs", [128, 32])
          for m in list(range(1, T + 1)) + [256]:
              act(mag[:], lam[:], AF.Exp, ["lam"], ["mag"], scale=float(m))
              ts(ang[:], th[:], float(m), None, ALU.mult, None, ["th"], ["ang"])
              sincos(p0, "sc0", ang[:], [128, 32], ["ang"], sn[:], cs[:], ["sn"], ["cs"])
              if m == 256:
                  tt(Pr[:], mag[:], cs[:], ALU.mult, ["mag", "cs"], ["Pr"])
                  tt(Pi[:], mag[:], sn[:], ALU.mult, ["mag", "sn"], ["Pi"])
              else:
                  tt(pwr[:, m, :], mag[:], cs[:], ALU.mult, ["mag", "cs"], ["pwr%d" % m])
                  tt(pwi[:, m, :], mag[:], sn[:], ALU.mult, ["mag", "sn"], ["pwi%d" % m])
                  if m == T:
                      cp(rho[:], mag[:], ["mag"], ["rho"])
                      cp(phi[:], ang[:], ["ang"], ["phi"])
          nr = sbuf(p0, "nr", [128, 32]); den = sbuf(p0, "den", [128, 32]); t0 = sbuf(p0, "t0", [128, 32])
          cr = sbuf(p0, "cr", [128, 32]); ci = sbuf(p0, "ci", [128, 32])
          ts(nr[:], pwr[:, 1, :], -1.0, None, ALU.add, None, ["pwr1"], ["nr"])
          tt(den[:], are2[:], are2[:], ALU.mult, ["are2"], ["den"])
          tt(t0[:], aim2[:], aim2[:], ALU.mult, ["aim2"], ["t0"])
          tt(den[:], den[:], t0[:], ALU.add, ["den", "t0"], ["den"])
          recip(den[:], den[:], ["den"], ["den"])
          tt(cr[:], nr[:], are2[:], ALU.mult, ["nr", "are2"], ["cr"])
          tt(t0[:], pwi[:, 1, :], aim2[:], ALU.mult, ["pwi1", "aim2"], ["t0"])
          tt(cr[:], cr[:], t0[:], ALU.add, ["cr", "t0"], ["cr"])
          tt(cr[:], cr[:], den[:], ALU.mult, ["cr", "den"], ["cr"])
          tt(ci[:], pwi[:, 1, :], are2[:], ALU.mult, ["pwi1", "are2"], ["ci"])
          tt(t0[:], nr[:], aim2[:], ALU.mult, ["nr", "aim2"], ["t0"])
          tt(ci[:], ci[:], t0[:], ALU.subtract, ["ci", "t0"], ["ci"])
          tt(ci[:], ci[:], den[:], ALU.mult, ["ci", "den"], ["ci"])
          bre2 = sbuf(p0, "bre2", [128, 32, 16]); bim2 = sbuf(p0, "bim2", [128, 32, 16])
          tb = sbuf(p0, "tb", [128, 32, 16])
          pload(bre2[:], bre2_d.ap(), "bre2"); pload(bim2[:], bim2_d.ap(), "bim2")

          def bc(v):
              return v.unsqueeze(2).to_broadcast([128, 32, 16])

          tt(bbr[:], bre2[:], bc(cr[:]), ALU.mult, ["bre2", "cr"], ["bbr"])
          tt(tb[:], bim2[:], bc(ci[:]), ALU.mult, ["bim2", "ci"], ["tb"])
          tt(bbr[:], bbr[:], tb[:], ALU.subtract, ["bbr", "tb"], ["bbr"])
          tt(bbi[:], bim2[:], bc(cr[:]), ALU.mult, ["bim2", "cr"], ["bbi"])
          tt(tb[:], bre2[:], bc(ci[:]), ALU.mult, ["bre2", "ci"], ["tb"])
          tt(bbi[:], bbi[:], tb[:], ALU.add, ["bbi", "tb"], ["bbi"])

          Xrb = sbuf(p0, "Xrb", [128, 32, 2, 16], BF16); Xib = sbuf(p0, "Xib", [128, 32, 2, 16], BF16)
          Mr = bre2; Mi = bim2
          Cx = sbuf(p0, "Cx", [128, 32, 2, 2, 16], BF16)
          diagD = sbuf(p0, "diagD", [128, 128])
          ident_bf = sbuf(p0, "ident_bf", [128, 128], BF16)
          h1 = sbuf(p0, "h1", [128, 32, 16]); h2 = sbuf(p0, "h2", [128, 32, 16])
          Ct = h1; tb2 = h2
          cp(ident_bf[:], ident[:], ["ident"], ["ident_bf"], eng="pool")
          memset(Xrb[:], 0.0, ["Xrb"]); memset(Xib[:], 0.0, ["Xib"])
          memset(Cx[:], 0.0, ["Cx"])
          for e in range(2):
              ps_ = slice(64 * e, 64 * e + 64)
              cp(Cx[ps_, :, 0, e, :], cre2[ps_], ["cre2", "Cx"], ["Cx"], eng="pool")
              ts(Cx[ps_, :, 1, e, :], cim2[ps_], -1.0, None, ALU.mult, None, ["cim2", "Cx"], ["Cx"], eng="pool")

          def bcp(v):
              return v.unsqueeze(2).to_broadcast([128, 32, 16])

          p0_debug = (any(n_ in debug for n_ in ("Bmat", "Cmat", "Kblk")) and not DEBUG.get("force_capture")) or stop == "p0"
          S.capture = not p0_debug
          b7bf = bank[7][:].bitcast(BF16)
          for m in range(T):
              tt(Mr[:], bbr[:], bcp(pwr[:, m, :]), ALU.mult, ["bbr", "pwr%d" % m], ["bre2"])
              tt(tb[:], bbi[:], bcp(pwi[:, m, :]), ALU.mult, ["bbi", "pwi%d" % m], ["tb"])
              tt(Mr[:], Mr[:], tb[:], ALU.subtract, ["bre2", "tb"], ["bre2"])
              tt(Mi[:], bbi[:], bcp(pwr[:, m, :]), ALU.mult, ["bbi", "pwr%d" % m], ["bim2"])
              tt(tb[:], bbr[:], bcp(pwi[:, m, :]), ALU.mult, ["bbr", "pwi%d" % m], ["tb"])
              tt(Mi[:], Mi[:], tb[:], ALU.add, ["bim2", "tb"], ["bim2"])
              for e in range(2):
                  ps_ = slice(64 * e, 64 * e + 64)
                  cp(Xrb[ps_, :, e, :], Mr[ps_], ["bre2", "Xrb"], ["Xrb"])
                  cp(Xib[ps_, :, e, :], Mi[ps_], ["bim2", "Xib"], ["Xib"])
              S.mark()
              j = T - 1 - m
              for b in range(8):
                  for part, X, xk_ in ((0, Xrb, "Xrb"), (1, Xib, "Xib")):
                      trn(b7bf[:, 128 * part:128 * part + 128], X[:, 4 * b:4 * b + 4].rearrange("p a e c -> p (a e c)"), ident_bf[:],
                          [xk_, "ident_bf"], [BK[7]])
                  cp(Bmat[:, b, j, :, :], b7bf[:, 0:256].rearrange("p (t c) -> p t c", t=2), [BK[7]], ["Bmat"], eng="act")
                  S.mark()
                  pb = bank[6]
                  S.op("dve", lambda e: e.memset(pb[:, 0:128], 0.0), writes=[BK[6]])
                  for q in range(4):
                      pr_ = 4 * b + q
                      for part, Xb, xk_ in ((0, Xrb, "Xrb"), (1, Xib, "Xib")):
                          mm(pb[32 * q:32 * q + 32, 32 * q:32 * q + 32], Xb[:, pr_].rearrange("p e c -> p (e c)"),
                             Cx[:, pr_, part].rearrange("p e c -> p (e c)"), part == 0, part == 1, [xk_, "Cx"], [BK[6]], tp=(0, 32 * q))
                  if m == 0:
                      ts(diagD[:], ident[:], dskip[:, b:b + 1], None, ALU.mult, None, ["ident", "dskip"], ["diagD"])
                      tt(Kblk[:, b, m, :], pb[:, 0:128], diagD[:], ALU.add, [BK[6], "diagD"], ["Kblk"])
                  else:
                      cp(Kblk[:, b, m, :], pb[:, 0:128], [BK[6]], ["Kblk"])
                  S.mark()
          S.capture = None
          if p0_debug:
              dbg("Bmat", Bmat[:], [128, 8, T, 2, 128], ["Bmat"])
              dbg("Kblk", Kblk[:], [128, 8, T, 128], ["Kblk"])
          P0_KEYS = ["ang", "mag", "sn", "cs", "sc0_n1", "sc0_r1", "nr", "den", "t0", "cr", "ci", "bre2", "bim2", "tb",
                     "Xrb", "Xib", "Cx", "diagD", "h1", "h2", "ident_bf"]
      if stop == "p0":
          return

      def ht_load(src_ap, slot):
          xk = "xb%d%s" % (slot, HB["tag"])
          dma("sp", HB["xb"][slot][:], src_ap, [], [xk], xk)

      def ht_norm(slot):
          tg_ = HB["tag"]
          x_ = HB["xb"][slot]; xk = "xb%d%s" % (slot, tg_)
          act(HB["sqj"][:], x_[:], AF.Square, [xk], ["sqj" + tg_, "ssum%d" % slot], accum=ssum[slot][:])
          act(rstd[slot][:], ssum[slot][:], AF.Sqrt, ["ssum%d" % slot, "epsc"], ["rstd%d" % slot], bias=epsc[:], scale=1.0 / D_MODEL)
          recip(rstd[slot][:], rstd[slot][:], ["rstd%d" % slot], ["rstd%d" % slot])
          stt(x_[:], x_[:], rstd[slot][:, 0:1], HB["nw"][:], ALU.mult, ALU.mult, [xk, "rstd%d" % slot, "nw_row" + tg_], [xk])

      def ht_tr(slot, dst_fn, dst_keys):
          x_ = HB["xb"][slot]; xk = "xb%d%s" % (slot, HB["tag"])
          for h in range(4):
              for k4 in range(4):
                  kc = 4 * h + k4
                  trn(bank[h][:, k4 * 128:(k4 + 1) * 128], x_[:, kc * 128:(kc + 1) * 128], ident[:], [xk, "ident"], [BK[h]])
              cp(dst_fn(4 * h, 4 * h + 4), bank[h][:].rearrange("p (a b) -> p a b", b=128), [BK[h]], dst_keys, eng="act")

      with ExitStack() as p1:
          Wu = sbuf(p1, "Wu", [128, KC, 1024], BF16)
          for i in range(4):
              dma("pool", Wu[:, 4 * i:4 * i + 4, :],
                  w_in.ap()[512 * i:512 * (i + 1), C_U:C_U + 1024].rearrange("(kc p) c -> p kc c", p=128),
                  [], ["Wu%d" % i], "Wu%d" % i)
          DtW = [sbuf(p1, "Dt%d" % w, [128, 32, 2, 128], BF16) for w in range(2)]
          Dt = DtW[1]
          with ExitStack() as pd:
              jrevs = []
              for w, base in ((0, 255), (1, 127)):
                  jr_i = sbuf(pd, "jr_i%d" % w, [128, 1], I32); jrev = sbuf(pd, "jrev%d" % w, [128, 1])
                  S.op("pool", lambda e, jr_i=jr_i, base=base: e.iota(jr_i[:], pattern=[[0, 1]], base=base, channel_multiplier=-1),
                       writes=["jr_i%d" % w])
                  cp(jrev[:], jr_i[:], ["jr_i%d" % w], ["jrev%d" % w])
                  jrevs.append(jrev)
              dtr = sbuf(pd, "dtr", [128, 64])
              dma("sp", dtr[:], bass.AP(ldt_row, 0, [[0, 128], [1, 64]]), [], ["dtr"], "p_dtr")
              act(dtr[:], dtr[:], AF.Exp, ["dtr"], ["dtr"])
              ar_c = sbuf(pd, "ar_c", [128, 16, 64]); ai_c = sbuf(pd, "ai_c", [128, 16, 64])
              mgB = [sbuf(pd, "mg_c%d" % i, [128, 16, 64]) for i in range(2)]
              anB = [sbuf(pd, "an_c%d" % i, [128, 16, 64]) for i in range(2)]
              snB = [sbuf(pd, "sn_c%d" % i, [128, 16, 64]) for i in range(2)]
              csB = [sbuf(pd, "cs_c%d" % i, [128, 16, 64]) for i in range(2)]
              it_ = 0
              for c in range(4):
                  dma("sp", ar_c[:].rearrange("p a b -> p (a b)"), bass.AP(are_row, 1024 * c, [[0, 128], [1, 1024]]), [], ["ar_c"], "p_ar_c")
                  dma("sp", ai_c[:].rearrange("p a b -> p (a b)"), bass.AP(aim_row, 1024 * c, [[0, 128], [1, 1024]]), [], ["ai_c"], "p_ai_c")
                  dtb = dtr[:, 16 * c:16 * c + 16].unsqueeze(2).to_broadcast([128, 16, 64])
                  tt(ar_c[:], ar_c[:], dtb, ALU.mult, ["ar_c", "dtr"], ["ar_c"])
                  tt(ai_c[:], ai_c[:], dtb, ALU.mult, ["ai_c", "dtr"], ["ai_c"])
                  for w in range(2):
                      b_ = it_ % 2
                      it_ += 1
                      mg_c, an_c, sn_c, cs_c = mgB[b_], anB[b_], snB[b_], csB[b_]
                      kmg, kan, ksn, kcs = "mg_c%d" % b_, "an_c%d" % b_, "sn_c%d" % b_, "cs_c%d" % b_
                      jk = "jrev%d" % w
                      act(mg_c[:], ar_c[:], AF.Exp, ["ar_c", jk], [kmg], scale=jrevs[w][:, 0:1])
                      act(an_c[:], ai_c[:], AF.Identity, ["ai_c", jk], [kan], scale=jrevs[w][:, 0:1])
                      sincos(pd, "scd%d" % b_, an_c[:], [128, 16, 64], [kan], sn_c[:], cs_c[:], [ksn], [kcs])
                      dv = DtW[w][:, 8 * c:8 * c + 8, :, :].rearrange("p a t (e q) -> p a t e q", e=2)
                      mgv = mg_c[:].rearrange("p (a e) q -> p a e q", e=2)
                      tt(dv[:, :, 0, :, :], mgv, cs_c[:].rearrange("p (a e) q -> p a e q", e=2), ALU.mult, [kmg, kcs], ["Dt%d" % w])
                      tt(dv[:, :, 1, :, :], mgv, sn_c[:].rearrange("p (a e) q -> p a e q", e=2), ALU.mult, [kmg, ksn], ["Dt%d" % w])
              S.retire(["jr_i0", "jr_i1", "jrev0", "jrev1", "dtr", "ar_c", "ai_c"]
                       + ["%s%d" % (n_, i_) for n_ in ("mg_c", "an_c", "sn_c", "cs_c") for i_ in range(2)]
                       + ["scd%d_%s" % (i_, n_) for i_ in range(2) for n_ in ("n1", "r1")])
          dbg("Dt", Dt[:], [128, 32, 2, 128], ["Dt1"])
          if stop == "p1a":
              return

          alloc_ht(p1, "P")
          hTp = [sbuf(p1, "hTp%d" % i, [128, KC, 128], BF16) for i in range(2)]
          upre = [sbuf(p1, "upre%d" % i, [128, NBATCH, 1024], BF16) for i in range(2)]
          NTP = NBATCH // 2
          Gb = [sbuf(p1, "G%d" % i, [128, 2, NTP, 32, 16]) for i in range(2)]
          Sr = sbuf(p1, "Sr", [128, 32, 16]); Si = sbuf(p1, "Si", [128, 32, 16])
          memset(Sr[:], 0.0, ["Sr"]); memset(Si[:], 0.0, ["Si"])
          nbat = NPRE // NBATCH
          n_pending = len([a_ for a_ in S.pending if a_ is not None])
          per_step = (n_pending + NPRE - 9) // max(1, NPRE - 8) + 1

          GQ = []

          def pop_g(k):
              for _ in range(k):
                  if GQ:
                      GQ.pop(0)()

          def u_mm(n):
              hs = n % 2
              ub_ = upre[(n // NBATCH) % 2]; uk_ = "upre%d" % ((n // NBATCH) % 2)
              tl_ = n % NBATCH
              for half in range(2):
                  for kc in range(KC):
                      mm(bank[4 + half][:], hTp[hs][:, kc, :], Wu[:, kc, 512 * half:512 * half + 512], kc == 0, kc == KC - 1,
                         ["hTp%d" % hs, "Wu%d" % (kc // 4)], [BK[4 + half]])
                  cp(ub_[:, tl_, 512 * half:512 * half + 512], bank[4 + half][:], [BK[4 + half]], [uk_])
                  pop_g(2)

          def queue_g(bt):
              ub_ = upre[bt % 2]; uk_ = "upre%d" % (bt % 2)
              G = Gb[bt % 2]; Gk = "G%d" % (bt % 2)

              def g_round(pr0, sl):
                  gk = BK[6 + sl]
                  for prl in range(4):
                      for part in range(2):
                          for tp in range(NTP):
                              gi = (prl * 2 + part) * NTP + tp
                              for w in range(2):
                                  mm(bank[6 + sl][:, 32 * gi:32 * gi + 32], DtW[w][:, pr0 + prl, part, :],
                                     ub_[:, 2 * tp + w, 32 * (pr0 + prl):32 * (pr0 + prl) + 32], w == 0, w == 1,
                                     ["Dt%d" % w, uk_], [gk])
                  gv = bank[6 + sl][:].rearrange("p (a t b c) -> p t b a c", a=4, t=2, b=NTP, c=32)
                  for e in range(2):
                      ps_ = slice(64 * e, 64 * e + 64)
                      for part in range(2):
                          cp(G[ps_, part, :, pr0:pr0 + 4, :], gv[ps_, part, :, :, 16 * e:16 * e + 16], [gk], [Gk], eng="act")

              def horner():
                  for tl in range(NTP):
                      tt(h1[:], Sr[:], bc(Pr[:]), ALU.mult, ["Sr", "Pr"], ["h1"], eng="pool")
                      tt(h2[:], Si[:], bc(Pi[:]), ALU.mult, ["Si", "Pi"], ["h2"], eng="pool")
                      tt(h1[:], h1[:], h2[:], ALU.subtract, ["h1", "h2"], ["h1"], eng="pool")
                      tt(h2[:], Sr[:], bc(Pi[:]), ALU.mult, ["Sr", "Pi"], ["h2"], eng="pool")
                      tt(Sr[:], h1[:], G[:, 0, tl, :, :], ALU.add, ["h1", Gk], ["Sr"], eng="pool")
                      tt(h1[:], Si[:], bc(Pr[:]), ALU.mult, ["Si", "Pr"], ["h1"], eng="pool")
                      tt(h1[:], h1[:], h2[:], ALU.add, ["h1", "h2"], ["h1"], eng="pool")
                      tt(Si[:], h1[:], G[:, 1, tl, :, :], ALU.add, ["h1", Gk], ["Si"], eng="pool")

              cnt_ = 0
              for pr0 in range(0, 32, 4):
                  GQ.append(lambda pr0=pr0, sl=cnt_ % 2: g_round(pr0, sl))
                  cnt_ += 1
              GQ.append(horner)

          def g_and_horner(bt):
              pop_g(len(GQ))
              queue_g(bt)

          ht_load(xpre.ap()[0:128, :], 0)
          ht_load(xpre.ap()[128:256, :], 1)
          for s_ in range(NPRE + 2):
              if s_ < NPRE:
                  ht_norm(s_ % NXB)
              if 1 <= s_ <= NPRE:
                  n = s_ - 1
                  ht_tr(n % NXB, lambda a, b_, hs=n % 2: hTp[hs][:, a:b_, :], ["hTp%d" % (n % 2)])
              if s_ + 2 < NPRE:
                  ht_load(xpre.ap()[(s_ + 2) * 128:(s_ + 3) * 128, :], (s_ + 2) % NXB)
              if 2 <= s_:
                  n = s_ - 2
                  u_mm(n)
                  if n % NBATCH == NBATCH - 1:
                      g_and_horner(n // NBATCH)
              if s_ >= 4:
                  S.replay(per_step)
          pop_g(len(GQ))
          S.replay(len(S.pending))
          HK1 = ["h1"]; HK2 = ["h2"]; KSR = ["Sr"]; KSI = ["Si"]
          tt(h1[:], Sr[:], bbr[:], ALU.mult, KSR + ["bbr"], HK1)
          tt(h2[:], Si[:], bbi[:], ALU.mult, KSI + ["bbi"], HK2)
          tt(h1[:], h1[:], h2[:], ALU.subtract, HK1 + HK2, HK1)
          S.op("dve", lambda e: e.tensor_reduce(out=sin_re[:], in_=h1[:], axis=AX.X, op=ALU.add), reads=HK1, writes=["sin_re"])
          tt(h1[:], Sr[:], bbi[:], ALU.mult, KSR + ["bbi"], HK1)
          tt(h2[:], Si[:], bbr[:], ALU.mult, KSI + ["bbr"], HK2)
          tt(h1[:], h1[:], h2[:], ALU.add, HK1 + HK2, HK1)
          S.op("dve", lambda e: e.tensor_reduce(out=sin_im[:], in_=h1[:], axis=AX.X, op=ALU.add), reads=HK1, writes=["sin_im"])
          tt(rs_re[:], sin_re[:], rho[:], ALU.mult, ["sin_re", "rho"], ["rs_re"])
          tt(rs_im[:], sin_im[:], rho[:], ALU.mult, ["sin_im", "rho"], ["rs_im"])
          cp(sinb_re[:], sin_re[:], ["sin_re"], ["sinb_re"])
          cp(sinb_im[:], sin_im[:], ["sin_im"], ["sinb_im"])
          dbg("sin_re", sin_re[:], [128, 32], ["sin_re"])
          dbg("sin_im", sin_im[:], [128, 32], ["sin_im"])
          S.retire(ht_keys())
          S.retire(["Wu0", "Wu1", "Wu2", "Wu3", "Dt0", "Dt1", "hTp0", "hTp1", "upre0", "upre1", "G0", "G1", "Sr", "Si"])
      S.retire(P0_KEYS)
      p0.close()
      if DEBUG.get("force_capture"):
          dbg("Bmat", Bmat[:], [128, 8, T, 2, 128], ["Bmat"])
          dbg("Kblk", Kblk[:], [128, 8, T, 128], ["Kblk"])
      if stop == "p1":
          return

      Cmat = sbuf(P, "Cmat", [128, 32, T, 2, 32], BF16)
      mixa = sbuf(P, "mixa", [128, 8, TOK], BF16)
      ybuf = sbuf(P, "ybuf", [128, 8, TOK], BF16)
      zsg = sbuf(P, "zsg", [128, 8, TOK], BF16)
      with ExitStack() as p2:
          hTm = sbuf(p2, "hTm", [128, KC, HALO + TOK], BF16)
          NWS = 3
          wsl = [sbuf(p2, "wsl%d" % i, [128, KC, 256], BF16) for i in range(NWS)]
          with ExitStack() as ph:
              alloc_ht(ph, "M")
              ht_load(xmain.ap()[0:128, :], 0)
              ht_load(xmain.ap()[128:256, :], 1)
              for s_ in range(NT + 2):
                  if s_ < NT + 1:
                      ht_norm(s_ % NXB)
                  if 1 <= s_:
                      tl = s_ - 1
                      ht_tr(tl % NXB, lambda a, b_, tl=tl: hTm[:, a:b_, tl * 128:(tl + 1) * 128], ["hTm%d" % tl])
                  if s_ + 2 < NT + 1:
                      ht_load(xmain.ap()[(s_ + 2) * 128:(s_ + 3) * 128, :], (s_ + 2) % NXB)
              S.retire(ht_keys())
          HT_ALL = ["hTm%d" % t_ for t_ in range(NT + 1)]
          HT_MAIN = HT_ALL[1:]

          wctr = [0]

          def load_cols(col0, ncols, dup=False):
              s = wctr[0] % NWS
              wctr[0] += 1
              k = "wsl%d" % s
              if dup:
                  for hh in range(2):
                      dma("pool", wsl[s][:, :, 64 * hh:64 * hh + 64],
                          w_in.ap()[:, col0:col0 + 64].rearrange("(kc p) c -> p kc c", p=128), [], [k], k + ("a" if hh == 0 else "b"))
              else:
                  dma("pool", wsl[s][:, :, 0:ncols], w_in.ap()[:, col0:col0 + ncols].rearrange("(kc p) c -> p kc c", p=128), [], [k], k + "a")
              return wsl[s], k

          def proj_fm(wt, wk, coff, banks, ntok0, ntoks):
              for (bi, t0_, n_) in banks:
                  hk_ = ["hTm%d" % t_ for t_ in range(t0_ // 128, (t0_ + n_ + 127) // 128)]
                  for kc in range(KC):
                      mm(bank[bi][:, 0:n_], wt[:, kc, coff:coff + 128], hTm[:, kc, t0_:t0_ + n_], kc == 0, kc == KC - 1,
                         [wk] + hk_, [BK[bi]])

          pa = ExitStack()
          kT2 = [sbuf(pa, "kT2_%d" % kv, [128, HALO + TOK], BF16) for kv in range(2)]
          v_sb = sbuf(pa, "v_sb", [128, NT + 1, 128], BF16)
          qf = sbuf(pa, "qf", [128, HALO + TOK]); sq = sbuf(pa, "sq", [128, HALO + TOK])

          def qk_norm(nb_, ntok, wcol, scale_imm, dst, dkeys):
              per = ntok // nb_
              for i in range(nb_):
                  sl = slice(i * per, (i + 1) * per)
                  act(sq[:, sl], bank[i][:, 0:per], AF.Square, [BK[i]], ["sq"])
                  cp(qf[:, sl], bank[i][:, 0:per], [BK[i]], ["qf"], eng="act")
              for i in range(nb_):
                  sl = slice(i * per, (i + 1) * per)
                  mm(bank[i][:, 0:per], blockones[:], sq[:, sl], True, True, ["blockones", "sq"], [BK[i]])
              for i in range(nb_):
                  sl = slice(i * per, (i + 1) * per)
                  act(sq[:, sl], bank[i][:, 0:per], AF.Sqrt, [BK[i], "epsc"], ["sq"], bias=epsc[:], scale=1.0 / 64.0)
              recip(sq[:, 0:ntok], sq[:, 0:ntok], ["sq"], ["sq"])
              tt(qf[:, 0:ntok], qf[:, 0:ntok], sq[:, 0:ntok], ALU.mult, ["qf", "sq"], ["qf"])
              ts(dst, qf[:, 0:ntok], wcol, scale_imm, ALU.mult, ALU.mult, ["qf", "qw2", "kw2"], dkeys)

          for kv in range(2):
              wt, wk = load_cols(C_K + 64 * kv, 64, dup=True)
              proj_fm(wt, wk, 0, [(0, 0, 384), (1, 384, 384), (2, 768, 384)], 0, HALO + TOK)
              qk_norm(3, HALO + TOK, kw2[:, 0:1], 1.0, kT2[kv][:], ["kT2_%d" % kv])
          wt, wk = load_cols(C_V, 128)
          for tl in range(NT + 1):
              bi = 6 + (tl % 2)
              for kc in range(KC):
                  mm(bank[bi][:, 0:128], hTm[:, kc, tl * 128:(tl + 1) * 128], wt[:, kc, 0:128], kc == 0, kc == KC - 1,
                     [wk, "hTm%d" % tl], [BK[bi]])
              cp(v_sb[:, tl, :], bank[bi][:, 0:128], [BK[bi]], ["v_sb"])
          dbg("kT2_0", kT2[0][:], [128, HALO + TOK], ["kT2_0"])
          dbg("v_sb", v_sb[:], [128, NT + 1, 128], ["v_sb"])

          d_i = sbuf(pa, "d_i", [128, 128], I32); dmat = sbuf(pa, "dmat", [128, 128])
          S.op("pool", lambda e: e.iota(d_i[:], pattern=[[1, 128]], base=0, channel_multiplier=-1), writes=["d_i"])
          cp(dmat[:], d_i[:], ["d_i"], ["dmat"])
          mcur = sbuf(pa, "mcur", [128, 128]); mprev = sbuf(pa, "mprev", [128, 128])
          dcl = sbuf(pa, "dcl", [128, 128]); dpl = sbuf(pa, "dpl", [128, 128])
          S.op("dve", lambda e: e.tensor_single_scalar(out=mcur[:], in_=dmat[:], scalar=0.0, op=ALU.is_ge), reads=["dmat"], writes=["mcur"])
          S.op("dve", lambda e: e.tensor_single_scalar(out=mprev[:], in_=dmat[:], scalar=0.0, op=ALU.is_lt), reads=["dmat"], writes=["mprev"])
          ts(dcl[:], dmat[:], 0.0, None, ALU.max, None, ["dmat"], ["dcl"])
          ts(dpl[:], dmat[:], 128.0, None, ALU.add, None, ["dmat"], ["dpl"])
          Etab = [sbuf(pa, "Etab%d" % i, [128, 2, 2, 128]) for i in range(2)]
          qT = [sbuf(pa, "qT%d" % i, [128, TOK], BF16) for i in range(2)]
          zaT = [sbuf(pa, "zaT%d" % i, [128, TOK], BF16) for i in range(2)]
          pexp = [sbuf(pa, "pexp%d" % i, [128, 512]) for i in range(2)]
          Pt = [[sbuf(pa, "Pt%d%d" % (e, c), [128, 512], BF16) for c in range(2)] for e in range(2)]
          rec = sbuf(pa, "rec", [128, 512]); onum = sbuf(pa, "onum", [128, 512])
          CtX = ybuf[:, 0, :].bitcast(F32).rearrange("p (a c) -> p a c", c=16)
          tb2X = ybuf[:, 1, :].bitcast(F32).rearrange("p (a c) -> p a c", c=16)
          memset(Cmat[:], 0.0, ["Cmat"])
          CMQ = []

          def cm_piece(i, part):
              m = i + 1
              if part == 0:
                  tt(CtX, cre2[:], bcp(pwr[:, m, :]), ALU.mult, ["cre2", "pwr%d" % m], ["ybuf0"], eng="pool")
                  tt(tb2X, cim2[:], bcp(pwi[:, m, :]), ALU.mult, ["cim2", "pwi%d" % m], ["ybuf1"], eng="pool")
                  tt(CtX, CtX, tb2X, ALU.subtract, ["ybuf0", "ybuf1"], ["ybuf0"], eng="pool")
                  for e in range(2):
                      ps_ = slice(64 * e, 64 * e + 64)
                      cp(Cmat[ps_, :, i, 0, 16 * e:16 * e + 16], CtX[ps_], ["ybuf0", "Cmat"], ["Cmat"], eng="pool")
              else:
                  tt(CtX, cre2[:], bcp(pwi[:, m, :]), ALU.mult, ["cre2", "pwi%d" % m], ["ybuf0"], eng="pool")
                  tt(tb2X, cim2[:], bcp(pwr[:, m, :]), ALU.mult, ["cim2", "pwr%d" % m], ["ybuf1"], eng="pool")
                  tt(CtX, CtX, tb2X, ALU.add, ["ybuf0", "ybuf1"], ["ybuf0"], eng="pool")
                  for e in range(2):
                      ps_ = slice(64 * e, 64 * e + 64)
                      ts(Cmat[ps_, :, i, 1, 16 * e:16 * e + 16], CtX[ps_], -1.0, None, ALU.mult, None, ["ybuf0", "Cmat"], ["Cmat"], eng="pool")

          for i_c in range(T):
              for part_c in range(2):
                  CMQ.append((i_c, part_c))

          WQ = {}

          def attn_prep(i):
              kv = i // 4
              s2 = i % 2
              for e in range(2):
                  h = 2 * i + e
                  slope = 2.0 ** (-8.0 * (h + 1) / 16.0)
                  act(Etab[s2][:, e, 0, :], dcl[:], AF.Exp, ["dcl"], ["Etab%d" % s2], scale=-slope)
                  tt(Etab[s2][:, e, 0, :], Etab[s2][:, e, 0, :], mcur[:], ALU.mult, ["Etab%d" % s2, "mcur"], ["Etab%d" % s2])
                  act(Etab[s2][:, e, 1, :], dpl[:], AF.Exp, ["dpl"], ["Etab%d" % s2], scale=-slope)
                  tt(Etab[s2][:, e, 1, :], Etab[s2][:, e, 1, :], mprev[:], ALU.mult, ["Etab%d" % s2, "mprev"], ["Etab%d" % s2])
              if i % 2 == 0:
                  WQ['q'] = load_cols(C_Q + 128 * i, 256)
                  if CMQ:
                      cm_piece(*CMQ.pop(0))
                  WQ['z'] = load_cols(C_ZA + 128 * i, 256)
                  if CMQ:
                      cm_piece(*CMQ.pop(0))
              coff = 128 * (i % 2)
              proj_fm(WQ['q'][0], WQ['q'][1], coff, [(0, HALO, 512), (1, HALO + 512, 512)], HALO, TOK)
              qk_norm(2, TOK, qw2[:, 0:1], 0.125, qT[s2][:], ["qT%d" % s2])
              proj_fm(WQ['z'][0], WQ['z'][1], coff, [(0, HALO, 512), (1, HALO + 512, 512)], HALO, TOK)
              for hb in range(2):
                  act(zaT[s2][:, 512 * hb:512 * hb + 512], bank[hb][:], AF.Silu, [BK[hb]], ["zaT%d" % s2])
              if i == 0:
                  dbg("qT0", qT[0][:], [128, TOK], ["qT0"])

          def attn_core(i):
              kv = i // 4
              s2 = i % 2
              for hq in range(2):
                  for e in range(2):
                      ps_ = slice(64 * e, 64 * e + 64)
                      for c in range(2):
                          bi = 2 + 2 * e + c
                          for qb in range(4):
                              qabs = 4 * hq + qb
                              tok0 = (HALO + 128 * qabs) if c == 0 else (128 * qabs)
                              mm(bank[bi][:, 128 * qb:128 * qb + 128], kT2[kv][ps_, tok0:tok0 + 128],
                                 qT[s2][ps_, 128 * qabs:128 * qabs + 128], True, True,
                                 ["kT2_%d" % kv, "qT%d" % s2], [BK[bi]], tp=(64 * e, 0))
                  for e in range(2):
                      for c in range(2):
                          bi = 2 + 2 * e + c
                          px = pexp[c]; pk = "pexp%d" % c
                          act(px[:], bank[bi][:], AF.Exp, [BK[bi]], [pk])
                          tt(Pt[e][c][:].rearrange("p (a b) -> p a b", b=128), px[:].rearrange("p (a b) -> p a b", b=128),
                             Etab[s2][:, e, c, :].unsqueeze(1).to_broadcast([128, 4, 128]), ALU.mult,
                             [pk, "Etab%d" % s2], ["Pt%d%d" % (e, c)])
                          if c == 1 and hq == 0:
                              ts(Pt[e][1][:, 0:128], Pt[e][1][:, 0:128], flag[:, 0:1], None, ALU.mult, None,
                                 ["Pt%d1" % e, "flag"], ["Pt%d1" % e])
                  for e in range(2):
                      ps_ = slice(64 * e, 64 * e + 64)
                      for qb in range(4):
                          qabs = 4 * hq + qb
                          cs_ = slice(128 * qb, 128 * qb + 128)
                          for c, blk in ((1, qabs), (0, qabs + 1)):
                              mm(bank[6][ps_, cs_], v_sb[:, blk, 64 * kv:64 * kv + 64], Pt[e][c][:, cs_], c == 1, c == 0,
                                 ["v_sb", "Pt%d%d" % (e, c)], [BK[6]], tp=(0, 64 * e))
                              mm(bank[7][ps_, cs_], ones_bf[:], Pt[e][c][:, cs_], c == 1, c == 0,
                                 ["ones_bf", "Pt%d%d" % (e, c)], [BK[7]], tp=(0, 64 * e))
                  ts(rec[:], bank[7][:], sinkx[:, i:i + 1], None, ALU.add, None, [BK[7], "sinkx"], ["rec"])
                  recip(rec[:], rec[:], ["rec"], ["rec"])
                  tt(onum[:], bank[6][:], rec[:], ALU.mult, [BK[6], "rec"], ["onum"])
                  tt(mixa[:, i, 512 * hq:512 * hq + 512], onum[:], zaT[s2][:, 512 * hq:512 * hq + 512], ALU.mult,
                     ["onum", "zaT%d" % s2], ["mixa%d" % i])

          attn_prep(0)
          for i in range(8):
              if i + 1 < 8:
                  attn_prep(i + 1)
              attn_core(i)
          while CMQ:
              cm_piece(*CMQ.pop(0))
          dbg("Cmat", Cmat[:], [128, 32, T, 2, 32], ["Cmat"])
          dbg("mixa", mixa[:], [128, 8, TOK], ["mixa%d" % i for i in range(8)])
          S.retire(["kT2_0", "kT2_1", "v_sb", "qf", "sq", "d_i", "dmat", "mcur", "mprev", "dcl", "dpl", "Etab0", "Etab1",
                    "qT0", "qT1", "zaT0", "zaT1", "pexp0", "pexp1", "Pt00", "Pt01", "Pt10", "Pt11", "rec", "onum"])
          pa.close()
          if stop == "pa":
              return

          with ExitStack() as p3:
              uT = [sbuf(p3, "uT%d" % i, [128, TOK], BF16) for i in range(2)]
              cosT = sbuf(p3, "cosT", [128, 4, NCH]); sinT = sbuf(p3, "sinT", [128, 4, NCH])
              coefT = sbuf(p3, "coefT", [128, 4, NCH])
              Btr = sbuf(p3, "Btr", [128, 4, NCH]); Bti = sbuf(p3, "Bti", [128, 4, NCH])
              rr = sbuf(p3, "rr", [128, 4, NCH]); ri = sbuf(p3, "ri", [128, 4, NCH])
              m1 = sbuf(p3, "m1", [128, 4, NCH]); m2 = sbuf(p3, "m2", [128, 4, NCH])
              psi = m1
              Spr = sbuf(p3, "Spr", [128, 4, NCH], BF16); Spi = sbuf(p3, "Spi", [128, 4, NCH], BF16)
              g1 = Btr[:].rearrange("p a b -> p (a b)"); g2 = Bti[:].rearrange("p a b -> p (a b)")
              _sc_cache["scm"] = (rr, ri)
              WU = {}

              def ssm_proj(b):
                  us = uT[b % 2]; uk = "uT%d" % (b % 2)
                  if b % 2 == 0:
                      WU["u"] = load_cols(C_U + 128 * b, 256)
                      WU["z"] = load_cols(C_ZS + 128 * b, 256)
                  coff = 128 * (b % 2)
                  proj_fm(WU["u"][0], WU["u"][1], coff, [(0, HALO, 512), (1, HALO + 512, 512)], HALO, TOK)
                  for hb in range(2):
                      cp(us[:, 512 * hb:512 * hb + 512], bank[hb][:], [BK[hb]], [uk], eng="act")
                  proj_fm(WU["z"][0], WU["z"][1], coff, [(0, HALO, 512), (1, HALO + 512, 512)], HALO, TOK)
                  for hb in range(2):
                      act(zsg[:, b, 512 * hb:512 * hb + 512], bank[hb][:], AF.Silu, [BK[hb]], ["zsg%d" % b])

              ssm_proj(0)
              for b in range(8):
                  us = uT[b % 2]; uk = "uT%d" % (b % 2)
                  for q in range(4):
                      pr_ = 4 * b + q
                      ts(psi[:, q, :], kk1[:], phi[:, pr_:pr_ + 1], None, ALU.mult, None, ["kk1", "phi"], ["m1"])
                      ts(coefT[:, q, :], kk1[:], 0.0, rho[:, pr_:pr_ + 1], ALU.mult, ALU.add, ["kk1", "rho"], ["coefT"])
                  memset(coefT[:, :, 0:1], 0.0, ["coefT"])
                  sincos(p3, "scm", psi[:], [128, 4, NCH], ["m1"], sinT[:], cosT[:], ["sinT"], ["cosT"], keys=("rr", "ri"))
                  for half in range(2):
                      yb = bank[2 + half][:].rearrange("p (k i) -> p k i", i=T)
                      uv = us[:, 512 * half:512 * half + 512].rearrange("p (k i) -> p k i", i=T)
                      for m in range(T):
                          mm(yb[:, :, m:T], Kblk[:, b, m, :], uv[:, :, 0:T - m], m == 0, False, ["Kblk", uk], [BK[2 + half]])
                  uv4 = us[:].rearrange("p (k i) -> p k i", i=T)
                  for part in range(2):
                      for j in range(T):
                          for q in range(4):
                              mm(bank[4 + q][:, 256 * part:256 * part + 256], Bmat[32 * q:32 * q + 32, b, j, part, :],
                                 uv4[32 * q:32 * q + 32, :, j], j == 0, j == T - 1, ["Bmat", uk], [BK[4 + q]], tp=(32 * q, 0))
                  for q in range(4):
                      cp(Btr[:, q, :], bank[4 + q][:, 0:256], [BK[4 + q]], ["Btr"], eng="act")
                      cp(Bti[:, q, :], bank[4 + q][:, 256:512], [BK[4 + q]], ["Bti"], eng="act")
                  tt(m1[:], Btr[:], cosT[:], ALU.mult, ["Btr", "cosT"], ["m1"])
                  tt(m2[:], Bti[:], sinT[:], ALU.mult, ["Bti", "sinT"], ["m2"])
                  tt(rr[:], m1[:], m2[:], ALU.add, ["m1", "m2"], ["rr"])
                  tt(m1[:], Bti[:], cosT[:], ALU.mult, ["Bti", "cosT"], ["m1"])
                  tt(m2[:], Btr[:], sinT[:], ALU.mult, ["Btr", "sinT"], ["m2"])
                  tt(ri[:], m1[:], m2[:], ALU.subtract, ["m1", "m2"], ["ri"])
                  tt(rr[:, :, 0], rr[:, :, 0], rs_re[:, 4 * b:4 * b + 4], ALU.add, ["rr", "rs_re"], ["rr"])
                  tt(ri[:, :, 0], ri[:, :, 0], rs_im[:, 4 * b:4 * b + 4], ALU.add, ["ri", "rs_im"], ["ri"])
                  fl = lambda t_: t_[:].rearrange("p a b -> p (a b)")
                  S.op("dve", lambda e: e.tensor_tensor_scan(out=fl(Btr), data0=fl(coefT), data1=fl(rr), initial=0.0,
                                                             op0=ALU.mult, op1=ALU.add), reads=["coefT", "rr"], writes=["Btr"])
                  S.op("dve", lambda e: e.tensor_tensor_scan(out=fl(Bti), data0=fl(coefT), data1=fl(ri), initial=0.0,
                                                             op0=ALU.mult, op1=ALU.add), reads=["coefT", "ri"], writes=["Bti"])
                  tt(m1[:], Btr[:], cosT[:], ALU.mult, ["Btr", "cosT"], ["m1"])
                  tt(m2[:], Bti[:], sinT[:], ALU.mult, ["Bti", "sinT"], ["m2"])
                  tt(Spr[:, :, 1:NCH], m1[:, :, 0:NCH - 1], m2[:, :, 0:NCH - 1], ALU.subtract, ["m1", "m2"], ["Spr"])
                  tt(m1[:], Btr[:], sinT[:], ALU.mult, ["Btr", "sinT"], ["m1"])
                  tt(m2[:], Bti[:], cosT[:], ALU.mult, ["Bti", "cosT"], ["m2"])
                  tt(Spi[:, :, 1:NCH], m1[:, :, 0:NCH - 1], m2[:, :, 0:NCH - 1], ALU.add, ["m1", "m2"], ["Spi"])
                  cp(Spr[:, :, 0], sinb_re[:, 4 * b:4 * b + 4], ["sinb_re", "Spr"], ["Spr"])
                  cp(Spi[:, :, 0], sinb_im[:, 4 * b:4 * b + 4], ["sinb_im", "Spi"], ["Spi"])
                  if b + 1 < 8:
                      ssm_proj(b + 1)
                  for half in range(2):
                      yb = bank[2 + half][:].rearrange("p (k i) -> p k i", i=T)
                      for q in range(4):
                          pr_ = 4 * b + q
                          for i_ in range(T):
                              for part, Sp, sk in ((0, Spr, "Spr"), (1, Spi, "Spi")):
                                  mm(yb[32 * q:32 * q + 32, :, i_], Cmat[:, pr_, i_, part, :], Sp[:, q, 128 * half:128 * half + 128],
                                     False, (i_ == T - 1 and part == 1), ["Cmat", sk], [BK[2 + half]], tp=(0, 32 * q))
                  for half in range(2):
                      sl = slice(512 * half, 512 * half + 512)
                      yb = bank[2 + half][:]
                      act(g1[:, sl], yb, AF.Square, [BK[2 + half]], ["Btr"])
                      ts(g1[:, sl], g1[:, sl], 0.044715, 1.0, ALU.mult, ALU.add, ["Btr"], ["Btr"])
                      tt(g1[:, sl], g1[:, sl], yb, ALU.mult, ["Btr", BK[2 + half]], ["Btr"])
                      act(g2[:, sl], g1[:, sl], AF.Sigmoid, ["Btr"], ["Bti"], scale=1.5957691216057308)
                      tt(ybuf[:, b, sl], g2[:, sl], yb, ALU.mult, ["Bti", BK[2 + half]], ["ybuf%d" % b])
                      if b == 0:
                          pass
              dbg("ybuf", ybuf[:], [128, 8, TOK], ["ybuf%d" % b for b in range(8)])
              S.retire(["uT0", "uT1", "cosT", "sinT", "coefT", "Btr", "Bti", "rr", "ri", "m1", "m2", "Spr", "Spi"])
          S.retire(HT_ALL + ["wsl%d" % i for i in range(NWS)])
          if stop == "p3":
              return

      with ExitStack() as p4:
          mixs = sbuf(p4, "mixs", [128, 8, TOK], BF16)
          wg = [sbuf(p4, "wg%d" % i, [128, 8, 128], BF16) for i in range(3)]
          sg = sbuf(p4, "sg", [128, TOK]); tg = sbuf(p4, "tg", [128, TOK])
          YB = ["ybuf%d" % b for b in range(8)]
          wo = [sbuf(p4, "wo%d" % i, [128, KC, 512], BF16) for i in range(2)]
          xr = [sbuf(p4, "xr%d" % i, [128, 512]) for i in range(3)]
          ot = [sbuf(p4, "ot%d" % i, [128, 512]) for i in range(3)]

          def load_wo(c):
              s_ = c % 2
              for hh in range(2):
                  dma("pool", wo[s_][:, 8 * hh:8 * hh + 8, :],
                      w_out.ap()[1024 * hh:1024 * hh + 1024, 512 * c:512 * c + 512].rearrange("(kc p) c -> p kc c", p=128),
                      [], ["wo%d_%d" % (s_, hh)], "wo%d_%d" % (s_, hh))

          for f in range(8):
              if f == 3:
                  load_wo(0)
              if f == 6:
                  load_wo(1)
              s = f % 3
              dma("pool", wg[s][:], w_glu.ap()[:, 128 * f:128 * f + 128].rearrange("(kc p) c -> p kc c", p=128), [], ["wg%d" % s], "wg%d" % s)
              b0 = 2 * (f % 2)
              for half in range(2):
                  for kc in range(8):
                      mm(bank[b0 + half][:], wg[s][:, kc, :], ybuf[:, kc, 512 * half:512 * half + 512], kc == 0, kc == 7,
                         ["wg%d" % s] + YB, [BK[b0 + half]])
              for half in range(2):
                  sl = slice(512 * half, 512 * half + 512)
                  act(sg[:, sl], bank[b0 + half][:], AF.Sigmoid, [BK[b0 + half], "bglu"], ["sg"], bias=bglu[:, f:f + 1], scale=1.0)
              tt(tg[:], sg[:], ybuf[:, f, :], ALU.mult, ["sg", "ybuf%d" % f], ["tg"])
              tt(mixs[:, f, :], tg[:], zsg[:, f, :], ALU.mult, ["tg", "zsg%d" % f], ["mixs%d" % f])
          dbg("mixs", mixs[:], [128, 8, TOK], ["mixs%d" % f for f in range(8)])
          MIXA = ["mixa%d" % i for i in range(8)]; MIXS = ["mixs%d" % i for i in range(8)]
          cnt = 0
          for c in range(4):
              s = c % 2
              if c >= 2:
                  load_wo(c)
              for tl in range(NT):
                  bi = cnt % 8
                  xs = cnt % 3
                  cnt += 1
                  dma("sp", xr[xs][:], xmain.ap()[HALO + 128 * tl:HALO + 128 * tl + 128, 512 * c:512 * c + 512], [], ["xr%d" % xs], "xr%d" % xs)
                  for kc in range(KC):
                      lhs = mixa[:, kc, 128 * tl:128 * tl + 128] if kc < 8 else mixs[:, kc - 8, 128 * tl:128 * tl + 128]
                      mm(bank[bi][:], lhs, wo[s][:, kc, :], kc == 0, kc == KC - 1,
                         ["wo%d_%d" % (s, kc // 8)] + (MIXA if kc < 8 else MIXS), [BK[bi]])
                  tt(ot[xs][:], bank[bi][:], xr[xs][:], ALU.add, [BK[bi], "xr%d" % xs], ["ot%d" % xs])
                  fin.append(dma("sp", out_d.ap()[128 * tl:128 * tl + 128, 512 * c:512 * c + 512], ot[xs][:], ["ot%d" % xs], [], "ot%d" % xs))


    body()
    for st_ in _p0h:
        st_.close()
    S.emit(top, final_wait_ops=fin)
    top.close()
    return nc, dbg_out


def make_in_maps(inputs):
    f32 = np.float32
    x = np.asarray(inputs["x"], f32)[0]
    g = lambda k: np.asarray(inputs[k], f32)
    a_re, a_im, ldt = g("ssm_a_re"), g("ssm_a_im"), g("ssm_log_dt")

    def st2(a):
        return np.ascontiguousarray(a.reshape(32, 2, 64).transpose(1, 2, 0).reshape(128, 32))

    def b2(a):
        return np.ascontiguousarray(a.reshape(32, 2, 64, 16).transpose(1, 2, 0, 3).reshape(128, 32, 16))

    def c2(a):
        return np.ascontiguousarray(a.reshape(32, 2, 16, 64).transpose(1, 3, 0, 2).reshape(128, 32, 16))

    shared = {
        "w_in": g("w_in"), "w_glu": g("w_glu"), "w_out": g("w_out"),
        "normw_row": g("norm_w").reshape(1, D_MODEL),
        "qw2": np.ascontiguousarray(np.tile(g("q_norm_w"), 2).reshape(128, 1)),
        "kw2": np.ascontiguousarray(np.tile(g("k_norm_w"), 2).reshape(128, 1)),
        "sinkcol": np.ascontiguousarray(g("attn_sinks").reshape(8, 2).T.repeat(64, axis=0)),
        "are2": st2(a_re), "aim2": st2(a_im),
        "ldt2": st2(np.broadcast_to(ldt[:, None], (64, 64))),
        "bre2": b2(g("ssm_b_re")), "bim2": b2(g("ssm_b_im")),
        "cre2": c2(g("ssm_c_re")), "cim2": c2(g("ssm_c_im")),
        "are_row": a_re.reshape(1, 4096), "aim_row": a_im.reshape(1, 4096), "ldt_row": ldt.reshape(1, 64),
        "dskip": np.ascontiguousarray(g("ssm_d").reshape(8, 128).T),
        "bglu": np.ascontiguousarray(g("b_glu").reshape(8, 128).T),
    }
    maps = []
    for c in range(NCORES):
        t0 = c * TOK
        xm = np.zeros((HALO + TOK, D_MODEL), f32)
        xm[HALO:] = x[t0:t0 + TOK]
        if c > 0:
            xm[:HALO] = x[t0 - HALO:t0]
        xp = np.zeros((NPRE * 128, D_MODEL), f32)
        if c > 0:
            xp[NPRE * 128 - t0:] = x[:t0]
        m = dict(shared)
        m["xpre"] = xp
        m["xmain"] = xm
        m["flag"] = np.full((128, 1), 0.0 if c == 0 else 1.0, f32)
        maps.append(m)
    return maps


_NC_CACHE = {}


def kernel(**inputs):
    dbg = tuple(sorted(DEBUG.get("names", ())))
    if dbg not in _NC_CACHE:
        _NC_CACHE[dbg] = build_nc(debug=dbg, stop=DEBUG.get("stop"))
    nc, dbg_out = _NC_CACHE[dbg]
    maps = make_in_maps(inputs)
    cores = DEBUG.get("cores", list(range(NCORES)))
    res = run_bass_kernel_spmd(nc, [maps[c] for c in cores], core_ids=list(range(len(cores))))
    if dbg:
        DEBUG["results"] = res.results
    out = np.concatenate([r["out"] for r in res.results], axis=0)
    return out.reshape(1, -1, D_MODEL).astype(np.float32)
```

```python
import math
from contextlib import ExitStack
import numpy as np
import concourse.bass as bass
import concourse.mybir as mybir
from concourse.bass_utils import run_bass_kernel_spmd

F32 = mybir.dt.float32
BF16 = mybir.dt.bfloat16
I32 = mybir.dt.int32
AF = mybir.ActivationFunctionType
ALU = mybir.AluOpType
AX = mybir.AxisListType

NCORES = 8
D_MODEL = 2048
SEQ = 8192
TOK = SEQ // NCORES
HALO = 128
NT = TOK // 128
NPRE = (SEQ - TOK) // 128
NBATCH = 4
KC = D_MODEL // 128
T = 4
NCH = TOK // T
IN_W = 4352
C_Q, C_K, C_V, C_ZA, C_U, C_ZS = 0, 1024, 1152, 1280, 2304, 3328
EPS = 1e-6
TWO_PI = 2.0 * math.pi
MAGIC = 12582912.0
ENGS = ("pe", "act", "dve", "pool", "sp")
DEBUG = {}
SAME_WAIT = ("dve", "act")
BIG_OP = 512


class Sched:
    def __init__(self, nc):
        self.nc = nc
        self.ops = []
        self.last_w = {}
        self.readers = {}
        self.seen = set()
        self.fence = set()
        self.capture = None
        self.pending = []

    def mark(self):
        if self.capture:
            self.pending.append(None)

    def replay(self, k):
        cap, self.capture = self.capture, None
        n = 0
        while self.pending:
            a = self.pending.pop(0)
            if a is None:
                if n >= k:
                    break
                continue
            self.op(*a)
            n += 1
        self.capture = cap

    def retire(self, keys):
        for k in keys:
            w = self.last_w.pop(k, None)
            if w is not None:
                self.fence.add(w)
            for r in self.readers.pop(k, ()):
                self.fence.add(r)

    def op(self, eng, fn, reads=(), writes=(), dsem=None, nfree=0):
        if self.capture:
            self.pending.append((eng, fn, list(reads), list(writes), dsem, nfree))
            return None
        bk = [k for k in reads if isinstance(k, str) and k.startswith("bank")]
        if bk:
            reads = [k for k in reads if k not in bk]
            writes = list(writes) + [k for k in bk if k not in writes]
        idx = len(self.ops)
        deps = set()
        for k in list(reads) + list(writes):
            if k not in self.seen:
                self.seen.add(k)
                deps |= self.fence
        for k in reads:
            w = self.last_w.get(k)
            if w is not None:
                deps.add(w)
        for k in writes:
            w = self.last_w.get(k)
            if w is not None:
                deps.add(w)
            for r in self.readers.get(k, ()):
                deps.add(r)
        deps.discard(idx)
        self.ops.append(dict(eng=eng, fn=fn, deps=deps, dsem=dsem, need_inc=False, nfree=nfree))
        for k in reads:
            self.readers.setdefault(k, []).append(idx)
        for k in writes:
            self.last_w[k] = idx
            self.readers[k] = []
        return idx

    def emit(self, stack, final_wait_ops=()):
        nc = self.nc
        ops = self.ops

        def skip(do, o):
            if not (do["dsem"] is None and o["dsem"] is None and do["eng"] == o["eng"] and do["eng"] != "sp"):
                return False
            if do["eng"] not in DEBUG.get("same_wait", SAME_WAIT):
                return True
            return do["eng"] == "dve" and do["nfree"] >= DEBUG.get("big", BIG_OP)

        for o in ops:
            for d in o["deps"]:
                if not skip(ops[d], o):
                    ops[d]["need_inc"] = True
        for d in final_wait_ops:
            ops[d]["need_inc"] = True
        esem = {e: stack.enter_context(nc.semaphore("s_" + e)) for e in ENGS}
        dsems = {}
        for o in ops:
            if o["dsem"] is not None and o["dsem"] not in dsems:
                dsems[o["dsem"]] = stack.enter_context(nc.semaphore("d_" + o["dsem"]))
        ecount = {e: 0 for e in ENGS}
        dcount = {k: 0 for k in dsems}
        for o in ops:
            if o["dsem"] is not None:
                dcount[o["dsem"]] += 16
                o["ticket"] = (o["dsem"], dsems[o["dsem"]], dcount[o["dsem"]])
            elif o["need_inc"]:
                ecount[o["eng"]] += 1
                o["ticket"] = (o["eng"], esem[o["eng"]], ecount[o["eng"]])
            else:
                o["ticket"] = None
        block = stack.enter_context(nc.Block())
        per_eng = {e: [o for o in ops if o["eng"] == e] for e in ENGS}

        def run(engname, eng):
            waited = {}
            for o in per_eng[engname]:
                need = {}
                for d in o["deps"]:
                    do = ops[d]
                    if skip(do, o):
                        continue
                    name, sem, val = do["ticket"]
                    if waited.get(name, 0) >= val:
                        continue
                    if need.get(name, (None, 0))[1] < val:
                        need[name] = (sem, val)
                for name, (sem, val) in need.items():
                    waited[name] = val
                    eng.wait_ge(sem, val)
                ins = o["fn"](eng)
                if o["dsem"] is not None:
                    ins.then_inc(o["ticket"][1], 16)
                elif o["need_inc"]:
                    ins.then_inc(o["ticket"][1], 1)
            if engname == "sp":
                for d in final_wait_ops:
                    _, sem, val = ops[d]["ticket"]
                    eng.wait_ge(sem, val)

        @block.tensor
        def _(e):
            run("pe", e)

        @block.scalar
        def _(e):
            run("act", e)

        @block.vector
        def _(e):
            run("dve", e)

        @block.gpsimd
        def _(e):
            run("pool", e)

        @block.sync
        def _(e):
            run("sp", e)


class _Stop(Exception):
    pass


def build_nc(debug=(), stop=None):
    nc = bass.Bass("TRN2", target_bir_lowering=False)
    S = Sched(nc)

    def din(name, shape):
        return nc.dram_tensor(name, list(shape), F32, kind="ExternalInput")

    xpre = din("xpre", [NPRE * 128, D_MODEL])
    xmain = din("xmain", [HALO + TOK, D_MODEL])
    w_in = din("w_in", [D_MODEL, IN_W])
    w_glu = din("w_glu", [1024, 1024])
    w_out = din("w_out", [2048, 2048])
    normw_row = din("normw_row", [1, D_MODEL])
    qw2_d = din("qw2", [128, 1]); kw2_d = din("kw2", [128, 1])
    sink_d = din("sinkcol", [128, 8])
    are2_d = din("are2", [128, 32]); aim2_d = din("aim2", [128, 32]); ldt2_d = din("ldt2", [128, 32])
    bre2_d = din("bre2", [128, 32, 16]); bim2_d = din("bim2", [128, 32, 16])
    cre2_d = din("cre2", [128, 32, 16]); cim2_d = din("cim2", [128, 32, 16])
    are_row = din("are_row", [1, 4096]); aim_row = din("aim_row", [1, 4096]); ldt_row = din("ldt_row", [1, 64])
    dskip_d = din("dskip", [128, 8]); bglu_d = din("bglu", [128, 8]); flag_d = din("flag", [128, 1])
    out_d = nc.dram_tensor("out", [TOK, D_MODEL], F32, kind="ExternalOutput")
    dbg_out = {}

    fin = []
    top = ExitStack()

    def sbuf(st, name, shape, dt=F32):
        return st.enter_context(nc.sbuf_tensor("sb_" + name, list(shape), dt))

    def _nf(ap):
        try:
            n = ap.free_size
            return int(n() if callable(n) else n)
        except Exception:
            return 0

    def dma(q, out, in_, r, w, dsem):
        return S.op(q, lambda e: e.dma_start(out=out, in_=in_), reads=r, writes=w, dsem=dsem)

    def tt(out, a, b, op, r, w, eng="dve"):
        return S.op(eng, lambda e: e.tensor_tensor(out=out, in0=a, in1=b, op=op), reads=r, writes=w, nfree=_nf(out))

    def ts(out, a, s1, s2, op0, op1, r, w, eng="dve"):
        if s2 is None:
            return S.op(eng, lambda e: e.tensor_scalar(out=out, in0=a, scalar1=s1, scalar2=None, op0=op0), reads=r, writes=w, nfree=_nf(out))
        return S.op(eng, lambda e: e.tensor_scalar(out=out, in0=a, scalar1=s1, scalar2=s2, op0=op0, op1=op1), reads=r, writes=w, nfree=_nf(out))

    def stt(out, a, scalar, b, op0, op1, r, w, eng="dve"):
        return S.op(eng, lambda e: e.scalar_tensor_tensor(out=out, in0=a, scalar=scalar, in1=b, op0=op0, op1=op1), reads=r, writes=w, nfree=_nf(out))

    def cp(out, a, r, w, eng="dve"):
        if eng == "act":
            return S.op("act", lambda e: e.copy(out=out, in_=a), reads=r, writes=w)
        return S.op(eng, lambda e: e.tensor_copy(out=out, in_=a), reads=r, writes=w, nfree=_nf(out))

    def act(out, a, func, r, w, bias=None, scale=None, accum=None):
        kw = {}
        if bias is not None:
            kw["bias"] = bias
        if scale is not None:
            kw["scale"] = scale
        if accum is not None:
            kw["accum_out"] = accum
        return S.op("act", lambda e: e.activation(out=out, in_=a, func=func, **kw), reads=r, writes=w)

    def mm(out, lhsT, rhs, start, stop, r, w, tp=None):
        kw = {}
        if tp is not None:
            kw["tile_position"] = tp
        return S.op("pe", lambda e: e.matmul(out, lhsT=lhsT, rhs=rhs, start=start, stop=stop, skip_group_check=True, **kw), reads=r, writes=w)

    def trn(out, a, ident, r, w):
        return S.op("pe", lambda e: e.matmul(out, lhsT=a, rhs=ident, start=True, stop=True, is_transpose=True, skip_group_check=True),
                    reads=r, writes=w)

    def memset(ap, v, w, eng="pool"):
        return S.op(eng, lambda e: e.memset(ap, v), writes=w)

    def recip(out, a, r, w):
        return S.op("dve", lambda e: e.reciprocal(out=out, in_=a), reads=r, writes=w)

    def dbg(name, ap, shape, r):
        if name not in debug:
            return
        d = nc.dram_tensor("dbg_" + name, list(shape), ap.dtype, kind="ExternalOutput")
        dbg_out[name] = d
        fin.append(dma("sp", d.ap(), ap, r, [], "dbg_" + name))

    bank = [top.enter_context(nc.psum_tensor("bank%d" % i, [128, 512], F32)) for i in range(8)]
    BK = ["bank%d" % i for i in range(8)]

    P = top
    ident = sbuf(P, "ident", [128, 128])
    blockones = sbuf(P, "blockones", [128, 128])
    ones_bf = sbuf(P, "ones_bf", [128, 64], BF16)
    qw2 = sbuf(P, "qw2", [128, 1]); kw2 = sbuf(P, "kw2", [128, 1])
    sinkx = sbuf(P, "sinkx", [128, 8])
    dskip = sbuf(P, "dskip", [128, 8]); bglu = sbuf(P, "bglu", [128, 8]); flag = sbuf(P, "flag", [128, 1])
    epsc = sbuf(P, "epsc", [128, 1])
    halfpi = sbuf(P, "halfpi", [128, 1])
    are2 = sbuf(P, "are2", [128, 32]); aim2 = sbuf(P, "aim2", [128, 32]); dt2 = sbuf(P, "dt2", [128, 32])
    lam = sbuf(P, "lam", [128, 32]); th = sbuf(P, "th", [128, 32])
    pwr = sbuf(P, "pwr", [128, 5, 32]); pwi = sbuf(P, "pwi", [128, 5, 32])
    bbr = sbuf(P, "bbr", [128, 32, 16]); bbi = sbuf(P, "bbi", [128, 32, 16])
    cre2 = sbuf(P, "cre2", [128, 32, 16]); cim2 = sbuf(P, "cim2", [128, 32, 16])
    Pr = sbuf(P, "Pr", [128, 32]); Pi = sbuf(P, "Pi", [128, 32])
    rho = sbuf(P, "rho", [128, 32]); phi = sbuf(P, "phi", [128, 32])
    sin_re = sbuf(P, "sin_re", [128, 32]); sin_im = sbuf(P, "sin_im", [128, 32])
    rs_re = sbuf(P, "rs_re", [128, 32]); rs_im = sbuf(P, "rs_im", [128, 32])
    sinb_re = sbuf(P, "sinb_re", [128, 32], BF16); sinb_im = sbuf(P, "sinb_im", [128, 32], BF16)
    kk1 = sbuf(P, "kk1", [128, NCH])
    Bmat = sbuf(P, "Bmat", [128, 8, T, 2, 128], BF16)
    Kblk = sbuf(P, "Kblk", [128, 8, T, 128], BF16)
    NXB = 3
    ssum = [sbuf(P, "ssum%d" % i, [128, 1]) for i in range(NXB)]
    rstd = [sbuf(P, "rstd%d" % i, [128, 1]) for i in range(NXB)]
    HB = {}

    def alloc_ht(st, tag):
        HB["tag"] = tag
        HB["nw"] = sbuf(st, "nw_row" + tag, [128, D_MODEL])
        HB["xb"] = [sbuf(st, "xb%d%s" % (i, tag), [128, D_MODEL]) for i in range(NXB)]
        HB["sqj"] = sbuf(st, "sqj" + tag, [128, D_MODEL], BF16)
        dma("sp", HB["nw"][:], bass.AP(normw_row, 0, [[0, 128], [1, D_MODEL]]), [], ["nw_row" + tag], "p_nw" + tag)

    def ht_keys():
        t_ = HB["tag"]
        return ["nw_row" + t_, "sqj" + t_] + ["xb%d%s" % (i, t_) for i in range(NXB)]

    def pload(dst, src, key):
        dma("sp", dst, src, [], [key], "p_" + key)

    pload(qw2[:], qw2_d.ap(), "qw2"); pload(kw2[:], kw2_d.ap(), "kw2")
    pload(sinkx[:], sink_d.ap(), "sinkx"); pload(dskip[:], dskip_d.ap(), "dskip")
    pload(bglu[:], bglu_d.ap(), "bglu"); pload(flag[:], flag_d.ap(), "flag")
    pload(are2[:], are2_d.ap(), "are2"); pload(aim2[:], aim2_d.ap(), "aim2"); pload(dt2[:], ldt2_d.ap(), "dt2")
    pload(cre2[:], cre2_d.ap(), "cre2"); pload(cim2[:], cim2_d.ap(), "cim2")

    memset(ident[:], 0.0, ["ident"])
    S.op("pool", lambda e: e.affine_select(out=ident[:], in_=ident[:], pattern=[[-1, 128]], compare_op=ALU.not_equal,
                                           fill=1.0, base=0, channel_multiplier=1), reads=["ident"], writes=["ident"])
    memset(blockones[:], 0.0, ["blockones"])
    memset(blockones[0:64, 0:64], 1.0, ["blockones"])
    memset(blockones[64:128, 64:128], 1.0, ["blockones"])
    memset(ones_bf[:], 1.0, ["ones_bf"])
    memset(epsc[:], EPS, ["epsc"])
    memset(halfpi[:], 0.5 * math.pi, ["halfpi"])
    kk_i = sbuf(P, "kk_i", [128, NCH], I32)
    S.op("pool", lambda e: e.iota(kk_i[:], pattern=[[1, NCH]], base=1, channel_multiplier=0), writes=["kk_i"])
    cp(kk1[:], kk_i[:], ["kk_i"], ["kk1"])
    act(sinkx[:], sinkx[:], AF.Exp, ["sinkx"], ["sinkx"])

    _sc_cache = {}

    def sincos(st, tag, x, shape, rk, sin_out, cos_out, wk_s, wk_c, keys=None, eng="dve"):
        if tag not in _sc_cache:
            _sc_cache[tag] = (sbuf(st, tag + "_n1", shape), sbuf(st, tag + "_r1", shape))
        n1, r1 = _sc_cache[tag]
        k1, k2 = keys if keys is not None else (tag + "_n1", tag + "_r1")
        ts(n1[:], x, 1.0 / TWO_PI, MAGIC, ALU.mult, ALU.add, rk, [k1], eng=eng)
        ts(n1[:], n1[:], -MAGIC, -TWO_PI, ALU.add, ALU.mult, [k1], [k1], eng=eng)
        tt(r1[:], x, n1[:], ALU.add, rk + [k1], [k2], eng=eng)
        ts(r1[:], r1[:], 3.1415925, -3.1415925, ALU.min, ALU.max, [k2], [k2], eng=eng)
        act(sin_out, r1[:], AF.Sin, [k2], wk_s)
        act(n1[:], r1[:], AF.Abs, [k2], [k1])
        act(cos_out, n1[:], AF.Sin, [k1, "halfpi"], wk_c, bias=halfpi[:], scale=-1.0)

    _p0h = []

    def body():
      p0 = ExitStack()
      _p0h.append(p0)
      if True:
          act(dt2[:], dt2[:], AF.Exp, ["dt2"], ["dt2"])
          tt(lam[:], are2[:], dt2[:], ALU.mult, ["are2", "dt2"], ["lam"])
          tt(th[:], aim2[:], dt2[:], ALU.mult, ["aim2", "dt2"], ["th"])
          memset(pwr[:, 0, :], 1.0, ["pwr0"]); memset(pwi[:, 0, :], 0.0, ["pwi0"])
          ang = sbuf(p0, "ang", [128, 32]); mag = sbuf(p0, "mag", [128, 32])
          sn = sbuf(p0, "sn", [128, 32]); cs = sbuf(p0, "cs", [128, 32])
          for m in list(range(1, T + 1)) + [256]:
              act(mag[:], lam[:], AF.Exp, ["lam"], ["mag"], scale=float(m))
              ts(ang[:], th[:], float(m), None, ALU.mult, None, ["th"], ["ang"])
              sincos(p0, "sc0", ang[:], [128, 32], ["ang"], sn[:], cs[:], ["sn"], ["cs"])
              if m == 256:
                  tt(Pr[:], mag[:], cs[:], ALU.mult, ["mag", "cs"], ["Pr"])
                  tt(Pi[:], mag[:], sn[:], ALU.mult, ["mag", "sn"], ["Pi"])
              else:
                  tt(pwr[:, m, :], mag[:], cs[:], ALU.mult, ["mag", "cs"], ["pwr%d" % m])
                  tt(pwi[:, m, :], mag[:], sn[:], ALU.mult, ["mag", "sn"], ["pwi%d" % m])
                  if m == T:
                      cp(rho[:], mag[:], ["mag"], ["rho"])
                      cp(phi[:], ang[:], ["ang"], ["phi"])
          nr = sbuf(p0, "nr", [128, 32]); den = sbuf(p0, "den", [128, 32]); t0 = sbuf(p0, "t0", [128, 32])
          cr = sbuf(p0, "cr", [128, 32]); ci = sbuf(p0, "ci", [128, 32])
          ts(nr[:], pwr[:, 1, :], -1.0, None, ALU.add, None, ["pwr1"], ["nr"])
          tt(den[:], are2[:], are2[:], ALU.mult, ["are2"], ["den"])
          tt(t0[:], aim2[:], aim2[:], ALU.mult, ["aim2"], ["t0"])
          tt(den[:], den[:], t0[:], ALU.add, ["den", "t0"], ["den"])
          recip(den[:], den[:], ["den"], ["den"])
          tt(cr[:], nr[:], are2[:], ALU.mult, ["nr", "are2"], ["cr"])
          tt(t0[:], pwi[:, 1, :], aim2[:], ALU.mult, ["pwi1", "aim2"], ["t0"])
          tt(cr[:], cr[:], t0[:], ALU.add, ["cr", "t0"], ["cr"])
          tt(cr[:], cr[:], den[:], ALU.mult, ["cr", "den"], ["cr"])
          tt(ci[:], pwi[:, 1, :], are2[:], ALU.mult, ["pwi1", "are2"], ["ci"])
          tt(t0[:], nr[:], aim2[:], ALU.mult, ["nr", "aim2"], ["t0"])
          tt(ci[:], ci[:], t0[:], ALU.subtract, ["ci", "t0"], ["ci"])
          tt(ci[:], ci[:], den[:], ALU.mult, ["ci", "den"], ["ci"])
          bre2 = sbuf(p0, "bre2", [128, 32, 16]); bim2 = sbuf(p0, "bim2", [128, 32, 16])
          tb = sbuf(p0, "tb", [128, 32, 16])
          pload(bre2[:], bre2_d.ap(), "bre2"); pload(bim2[:], bim2_d.ap(), "bim2")

          def bc(v):
              return v.unsqueeze(2).to_broadcast([128, 32, 16])

          tt(bbr[:], bre2[:], bc(cr[:]), ALU.mult, ["bre2", "cr"], ["bbr"])
          tt(tb[:], bim2[:], bc(ci[:]), ALU.mult, ["bim2", "ci"], ["tb"])
          tt(bbr[:], bbr[:], tb[:], ALU.subtract, ["bbr", "tb"], ["bbr"])
          tt(bbi[:], bim2[:], bc(cr[:]), ALU.mult, ["bim2", "cr"], ["bbi"])
          tt(tb[:], bre2[:], bc(ci[:]), ALU.mult, ["bre2", "ci"], ["tb"])
          tt(bbi[:], bbi[:], tb[:], ALU.add, ["bbi", "tb"], ["bbi"])

          Xrb = sbuf(p0, "Xrb", [128, 32, 2, 16], BF16); Xib = sbuf(p0, "Xib", [128, 32, 2, 16], BF16)
          Mr = bre2; Mi = bim2
          Cx = sbuf(p0, "Cx", [128, 32, 2, 2, 16], BF16)
          diagD = sbuf(p0, "diagD", [128, 128])
          ident_bf = sbuf(p0, "ident_bf", [128, 128], BF16)
          h1 = sbuf(p0, "h1", [128, 32, 16]); h2 = sbuf(p0, "h2", [128, 32, 16])
          Ct = h1; tb2 = h2
          cp(ident_bf[:], ident[:], ["ident"], ["ident_bf"], eng="pool")
          memset(Xrb[:], 0.0, ["Xrb"]); memset(Xib[:], 0.0, ["Xib"])
          memset(Cx[:], 0.0, ["Cx"])
          for e in range(2):
              ps_ = slice(64 * e, 64 * e + 64)
              cp(Cx[ps_, :, 0, e, :], cre2[ps_], ["cre2", "Cx"], ["Cx"], eng="pool")
              ts(Cx[ps_, :, 1, e, :], cim2[ps_], -1.0, None, ALU.mult, None, ["cim2", "Cx"], ["Cx"], eng="pool")

          def bcp(v):
              return v.unsqueeze(2).to_broadcast([128, 32, 16])

          p0_debug = (any(n_ in debug for n_ in ("Bmat", "Cmat", "Kblk")) and not DEBUG.get("force_capture")) or stop == "p0"
          S.capture = not p0_debug
          b7bf = bank[7][:].bitcast(BF16)
          for m in range(T):
              tt(Mr[:], bbr[:], bcp(pwr[:, m, :]), ALU.mult, ["bbr", "pwr%d" % m], ["bre2"])
              tt(tb[:], bbi[:], bcp(pwi[:, m, :]), ALU.mult, ["bbi", "pwi%d" % m], ["tb"])
              tt(Mr[:], Mr[:], tb[:], ALU.subtract, ["bre2", "tb"], ["bre2"])
              tt(Mi[:], bbi[:], bcp(pwr[:, m, :]), ALU.mult, ["bbi", "pwr%d" % m], ["bim2"])
              tt(tb[:], bbr[:], bcp(pwi[:, m, :]), ALU.mult, ["bbr", "pwi%d" % m], ["tb"])
              tt(Mi[:], Mi[:], tb[:], ALU.add, ["bim2", "tb"], ["bim2"])
              for e in range(2):
                  ps_ = slice(64 * e, 64 * e + 64)
                  cp(Xrb[ps_, :, e, :], Mr[ps_], ["bre2", "Xrb"], ["Xrb"])
                  cp(Xib[ps_, :, e, :], Mi[ps_], ["bim2", "Xib"], ["Xib"])
              S.mark()
              j = T - 1 - m
              for b in range(8):
                  for part, X, xk_ in ((0, Xrb, "Xrb"), (1, Xib, "Xib")):
                      trn(b7bf[:, 128 * part:128 * part + 128], X[:, 4 * b:4 * b + 4].rearrange("p a e c -> p (a e c)"), ident_bf[:],
                          [xk_, "ident_bf"], [BK[7]])
                  cp(Bmat[:, b, j, :, :], b7bf[:, 0:256].rearrange("p (t c) -> p t c", t=2), [BK[7]], ["Bmat"], eng="act")
                  S.mark()
                  pb = bank[6]
                  S.op("dve", lambda e: e.memset(pb[:, 0:128], 0.0), writes=[BK[6]])
                  for q in range(4):
                      pr_ = 4 * b + q
                      for part, Xb, xk_ in ((0, Xrb, "Xrb"), (1, Xib, "Xib")):
                          mm(pb[32 * q:32 * q + 32, 32 * q:32 * q + 32], Xb[:, pr_].rearrange("p e c -> p (e c)"),
                             Cx[:, pr_, part].rearrange("p e c -> p (e c)"), part == 0, part == 1, [xk_, "Cx"], [BK[6]], tp=(0, 32 * q))
                  if m == 0:
                      ts(diagD[:], ident[:], dskip[:, b:b + 1], None, ALU.mult, None, ["ident", "dskip"], ["diagD"])
                      tt(Kblk[:, b, m, :], pb[:, 0:128], diagD[:], ALU.add, [BK[6], "diagD"], ["Kblk"])
                  else:
                      cp(Kblk[:, b, m, :], pb[:, 0:128], [BK[6]], ["Kblk"])
                  S.mark()
          S.capture = None
          if p0_debug:
              dbg("Bmat", Bmat[:], [128, 8, T, 2, 128], ["Bmat"])
              dbg("Kblk", Kblk[:], [128, 8, T, 128], ["Kblk"])
          P0_KEYS = ["ang", "mag", "sn", "cs", "sc0_n1", "sc0_r1", "nr", "den", "t0", "cr", "ci", "bre2", "bim2", "tb",
                     "Xrb", "Xib", "Cx", "diagD", "h1", "h2", "ident_bf"]
      if stop == "p0":
          return

      def ht_load(src_ap, slot):
          xk = "xb%d%s" % (slot, HB["tag"])
          dma("sp", HB["xb"][slot][:], src_ap, [], [xk], xk)

      def ht_norm(slot):
          tg_ = HB["tag"]
          x_ = HB["xb"][slot]; xk = "xb%d%s" % (slot, tg_)
          act(HB["sqj"][:], x_[:], AF.Square, [xk], ["sqj" + tg_, "ssum%d" % slot], accum=ssum[slot][:])
          act(rstd[slot][:], ssum[slot][:], AF.Sqrt, ["ssum%d" % slot, "epsc"], ["rstd%d" % slot], bias=epsc[:], scale=1.0 / D_MODEL)
          recip(rstd[slot][:], rstd[slot][:], ["rstd%d" % slot], ["rstd%d" % slot])
          stt(x_[:], x_[:], rstd[slot][:, 0:1], HB["nw"][:], ALU.mult, ALU.mult, [xk, "rstd%d" % slot, "nw_row" + tg_], [xk])

      def ht_tr(slot, dst_fn, dst_keys):
          x_ = HB["xb"][slot]; xk = "xb%d%s" % (slot, HB["tag"])
          for h in range(4):
              for k4 in range(4):
                  kc = 4 * h + k4
                  trn(bank[h][:, k4 * 128:(k4 + 1) * 128], x_[:, kc * 128:(kc + 1) * 128], ident[:], [xk, "ident"], [BK[h]])
              cp(dst_fn(4 * h, 4 * h + 4), bank[h][:].rearrange("p (a b) -> p a b", b=128), [BK[h]], dst_keys, eng="act")

      with ExitStack() as p1:
          Wu = sbuf(p1, "Wu", [128, KC, 1024], BF16)
          for i in range(4):
              dma("pool", Wu[:, 4 * i:4 * i + 4, :],
                  w_in.ap()[512 * i:512 * (i + 1), C_U:C_U + 1024].rearrange("(kc p) c -> p kc c", p=128),
                  [], ["Wu%d" % i], "Wu%d" % i)
          DtW = [sbuf(p1, "Dt%d" % w, [128, 32, 2, 128], BF16) for w in range(2)]
          Dt = DtW[1]
          with ExitStack() as pd:
              jrevs = []
              for w, base in ((0, 255), (1, 127)):
                  jr_i = sbuf(pd, "jr_i%d" % w, [128, 1], I32); jrev = sbuf(pd, "jrev%d" % w, [128, 1])
                  S.op("pool", lambda e, jr_i=jr_i, base=base: e.iota(jr_i[:], pattern=[[0, 1]], base=base, channel_multiplier=-1),
                       writes=["jr_i%d" % w])
                  cp(jrev[:], jr_i[:], ["jr_i%d" % w], ["jrev%d" % w])
                  jrevs.append(jrev)
              dtr = sbuf(pd, "dtr", [128, 64])
              dma("sp", dtr[:], bass.AP(ldt_row, 0, [[0, 128], [1, 64]]), [], ["dtr"], "p_dtr")
              act(dtr[:], dtr[:], AF.Exp, ["dtr"], ["dtr"])
              ar_c = sbuf(pd, "ar_c", [128, 16, 64]); ai_c = sbuf(pd, "ai_c", [128, 16, 64])
              mgB = [sbuf(pd, "mg_c%d" % i, [128, 16, 64]) for i in range(2)]
              anB = [sbuf(pd, "an_c%d" % i, [128, 16, 64]) for i in range(2)]
              snB = [sbuf(pd, "sn_c%d" % i, [128, 16, 64]) for i in range(2)]
              csB = [sbuf(pd, "cs_c%d" % i, [128, 16, 64]) for i in range(2)]
              arB = [ar_c, sbuf(pd, "ar_c1", [128, 16, 64])]
              aiB = [ai_c, sbuf(pd, "ai_c1", [128, 16, 64])]

              def load_chunk(c):
                  a_, i_ = arB[c % 2], aiB[c % 2]
                  ka, ki_ = "ar_c%d" % (c % 2), "ai_c%d" % (c % 2)
                  dma("sp", a_[:].rearrange("p a b -> p (a b)"), bass.AP(are_row, 1024 * c, [[0, 128], [1, 1024]]), [], [ka], "p_" + ka)
                  dma("sp", i_[:].rearrange("p a b -> p (a b)"), bass.AP(aim_row, 1024 * c, [[0, 128], [1, 1024]]), [], [ki_], "p_" + ki_)
                  dtb = dtr[:, 16 * c:16 * c + 16].unsqueeze(2).to_broadcast([128, 16, 64])
                  tt(a_[:], a_[:], dtb, ALU.mult, [ka, "dtr"], [ka])
                  tt(i_[:], i_[:], dtb, ALU.mult, [ki_, "dtr"], [ki_])

              def stage_a(it):
                  c, w = it // 2, it % 2
                  b_ = it % 2
                  a_, i_ = arB[c % 2], aiB[c % 2]
                  ka, ki_ = "ar_c%d" % (c % 2), "ai_c%d" % (c % 2)
                  jk = "jrev%d" % w
                  act(mgB[b_][:], a_[:], AF.Exp, [ka, jk], ["mg_c%d" % b_], scale=jrevs[w][:, 0:1])
                  act(anB[b_][:], i_[:], AF.Identity, [ki_, jk], ["an_c%d" % b_], scale=jrevs[w][:, 0:1])
                  sincos(pd, "scd%d" % b_, anB[b_][:], [128, 16, 64], ["an_c%d" % b_], snB[b_][:], csB[b_][:],
                         ["sn_c%d" % b_], ["cs_c%d" % b_])

              def stage_b(it):
                  c, w = it // 2, it % 2
                  b_ = it % 2
                  dv = DtW[w][:, 8 * c:8 * c + 8, :, :].rearrange("p a t (e q) -> p a t e q", e=2)
                  mgv = mgB[b_][:].rearrange("p (a e) q -> p a e q", e=2)
                  tt(dv[:, :, 0, :, :], mgv, csB[b_][:].rearrange("p (a e) q -> p a e q", e=2), ALU.mult,
                     ["mg_c%d" % b_, "cs_c%d" % b_], ["Dt%d" % w])
                  tt(dv[:, :, 1, :, :], mgv, snB[b_][:].rearrange("p (a e) q -> p a e q", e=2), ALU.mult,
                     ["mg_c%d" % b_, "sn_c%d" % b_], ["Dt%d" % w])

              load_chunk(0)
              load_chunk(1)
              stage_a(0)
              for it in range(8):
                  if it + 1 < 8:
                      stage_a(it + 1)
                  stage_b(it)
                  if it % 2 == 1 and it // 2 + 2 < 4:
                      load_chunk(it // 2 + 2)
              S.retire(["jr_i0", "jr_i1", "jrev0", "jrev1", "dtr", "ar_c0", "ai_c0", "ar_c1", "ai_c1"]
                       + ["%s%d" % (n_, i_) for n_ in ("mg_c", "an_c", "sn_c", "cs_c") for i_ in range(2)]
                       + ["scd%d_%s" % (i_, n_) for i_ in range(2) for n_ in ("n1", "r1")])
          dbg("Dt", Dt[:], [128, 32, 2, 128], ["Dt1"])
          if stop == "p1a":
              return

          alloc_ht(p1, "P")
          hTp = [sbuf(p1, "hTp%d" % i, [128, KC, 128], BF16) for i in range(2)]
          upre = [sbuf(p1, "upre%d" % i, [128, NBATCH, 1024], BF16) for i in range(2)]
          NTP = NBATCH // 2
          Gb = [sbuf(p1, "G%d" % i, [128, 2, NTP, 32, 16]) for i in range(2)]
          Sr = sbuf(p1, "Sr", [128, 32, 16]); Si = sbuf(p1, "Si", [128, 32, 16])
          memset(Sr[:], 0.0, ["Sr"]); memset(Si[:], 0.0, ["Si"])
          nbat = NPRE // NBATCH
          n_pending = len([a_ for a_ in S.pending if a_ is not None])
          per_step = (n_pending + NPRE - 9) // max(1, NPRE - 8) + 1

          GQ = []

          def pop_g(k):
              for _ in range(k):
                  if GQ:
                      GQ.pop(0)()

          def u_mm(n):
              hs = n % 2
              ub_ = upre[(n // NBATCH) % 2]; uk_ = "upre%d" % ((n // NBATCH) % 2)
              tl_ = n % NBATCH
              for half in range(2):
                  for kc in range(KC):
                      mm(bank[4 + half][:], hTp[hs][:, kc, :], Wu[:, kc, 512 * half:512 * half + 512], kc == 0, kc == KC - 1,
                         ["hTp%d" % hs, "Wu%d" % (kc // 4)], [BK[4 + half]])
                  cp(ub_[:, tl_, 512 * half:512 * half + 512], bank[4 + half][:], [BK[4 + half]], [uk_])
                  pop_g(2)

          def queue_g(bt):
              ub_ = upre[bt % 2]; uk_ = "upre%d" % (bt % 2)
              G = Gb[bt % 2]; Gk = "G%d" % (bt % 2)

              def g_round(pr0, sl):
                  gk = BK[6 + sl]
                  for prl in range(4):
                      for part in range(2):
                          for tp in range(NTP):
                              gi = (prl * 2 + part) * NTP + tp
                              for w in range(2):
                                  mm(bank[6 + sl][:, 32 * gi:32 * gi + 32], DtW[w][:, pr0 + prl, part, :],
                                     ub_[:, 2 * tp + w, 32 * (pr0 + prl):32 * (pr0 + prl) + 32], w == 0, w == 1,
                                     ["Dt%d" % w, uk_], [gk])
                  gv = bank[6 + sl][:].rearrange("p (a t b c) -> p t b a c", a=4, t=2, b=NTP, c=32)
                  for e in range(2):
                      ps_ = slice(64 * e, 64 * e + 64)
                      for part in range(2):
                          cp(G[ps_, part, :, pr0:pr0 + 4, :], gv[ps_, part, :, :, 16 * e:16 * e + 16], [gk], [Gk], eng="act")

              def horner():
                  for tl in range(NTP):
                      tt(h1[:], Sr[:], bc(Pr[:]), ALU.mult, ["Sr", "Pr"], ["h1"], eng="pool")
                      tt(h2[:], Si[:], bc(Pi[:]), ALU.mult, ["Si", "Pi"], ["h2"], eng="pool")
                      tt(h1[:], h1[:], h2[:], ALU.subtract, ["h1", "h2"], ["h1"], eng="pool")
                      tt(h2[:], Sr[:], bc(Pi[:]), ALU.mult, ["Sr", "Pi"], ["h2"], eng="pool")
                      tt(Sr[:], h1[:], G[:, 0, tl, :, :], ALU.add, ["h1", Gk], ["Sr"], eng="pool")
                      tt(h1[:], Si[:], bc(Pr[:]), ALU.mult, ["Si", "Pr"], ["h1"], eng="pool")
                      tt(h1[:], h1[:], h2[:], ALU.add, ["h1", "h2"], ["h1"], eng="pool")
                      tt(Si[:], h1[:], G[:, 1, tl, :, :], ALU.add, ["h1", Gk], ["Si"], eng="pool")

              cnt_ = 0
              for pr0 in range(0, 32, 4):
                  GQ.append(lambda pr0=pr0, sl=cnt_ % 2: g_round(pr0, sl))
                  cnt_ += 1
              GQ.append(horner)

          def g_and_horner(bt):
              pop_g(len(GQ))
              queue_g(bt)

          ht_load(xpre.ap()[0:128, :], 0)
          ht_load(xpre.ap()[128:256, :], 1)
          for s_ in range(NPRE + 2):
              if s_ < NPRE:
                  ht_norm(s_ % NXB)
              if 1 <= s_ <= NPRE:
                  n = s_ - 1
                  ht_tr(n % NXB, lambda a, b_, hs=n % 2: hTp[hs][:, a:b_, :], ["hTp%d" % (n % 2)])
              if s_ + 2 < NPRE:
                  ht_load(xpre.ap()[(s_ + 2) * 128:(s_ + 3) * 128, :], (s_ + 2) % NXB)
              if 2 <= s_:
                  n = s_ - 2
                  u_mm(n)
                  if n % NBATCH == NBATCH - 1:
                      g_and_horner(n // NBATCH)
              if s_ >= 4:
                  S.replay(per_step)
          pop_g(len(GQ))
          S.replay(len(S.pending))
          HK1 = ["h1"]; HK2 = ["h2"]; KSR = ["Sr"]; KSI = ["Si"]
          tt(h1[:], Sr[:], bbr[:], ALU.mult, KSR + ["bbr"], HK1)
          tt(h2[:], Si[:], bbi[:], ALU.mult, KSI + ["bbi"], HK2)
          tt(h1[:], h1[:], h2[:], ALU.subtract, HK1 + HK2, HK1)
          S.op("dve", lambda e: e.tensor_reduce(out=sin_re[:], in_=h1[:], axis=AX.X, op=ALU.add), reads=HK1, writes=["sin_re"])
          tt(h1[:], Sr[:], bbi[:], ALU.mult, KSR + ["bbi"], HK1)
          tt(h2[:], Si[:], bbr[:], ALU.mult, KSI + ["bbr"], HK2)
          tt(h1[:], h1[:], h2[:], ALU.add, HK1 + HK2, HK1)
          S.op("dve", lambda e: e.tensor_reduce(out=sin_im[:], in_=h1[:], axis=AX.X, op=ALU.add), reads=HK1, writes=["sin_im"])
          tt(rs_re[:], sin_re[:], rho[:], ALU.mult, ["sin_re", "rho"], ["rs_re"])
          tt(rs_im[:], sin_im[:], rho[:], ALU.mult, ["sin_im", "rho"], ["rs_im"])
          cp(sinb_re[:], sin_re[:], ["sin_re"], ["sinb_re"])
          cp(sinb_im[:], sin_im[:], ["sin_im"], ["sinb_im"])
          dbg("sin_re", sin_re[:], [128, 32], ["sin_re"])
          dbg("sin_im", sin_im[:], [128, 32], ["sin_im"])
          S.retire(ht_keys())
          S.retire(["Wu0", "Wu1", "Wu2", "Wu3", "Dt0", "Dt1", "hTp0", "hTp1", "upre0", "upre1", "G0", "G1", "Sr", "Si"])
      S.retire(P0_KEYS)
      p0.close()
      if DEBUG.get("force_capture"):
          dbg("Bmat", Bmat[:], [128, 8, T, 2, 128], ["Bmat"])
          dbg("Kblk", Kblk[:], [128, 8, T, 128], ["Kblk"])
      if stop == "p1":
          return

      Cmat = sbuf(P, "Cmat", [128, 32, T, 2, 32], BF16)
      mixa = sbuf(P, "mixa", [128, 8, TOK], BF16)
      ybuf = sbuf(P, "ybuf", [128, 8, TOK], BF16)
      zsg = sbuf(P, "zsg", [128, 8, TOK], BF16)
      with ExitStack() as p2:
          hTm = sbuf(p2, "hTm", [128, KC, HALO + TOK], BF16)
          NWS = 3
          wsl = [sbuf(p2, "wsl%d" % i, [128, KC, 256], BF16) for i in range(NWS)]
          with ExitStack() as ph:
              alloc_ht(ph, "M")
              ht_load(xmain.ap()[0:128, :], 0)
              ht_load(xmain.ap()[128:256, :], 1)
              for s_ in range(NT + 2):
                  if s_ < NT + 1:
                      ht_norm(s_ % NXB)
                  if 1 <= s_:
                      tl = s_ - 1
                      ht_tr(tl % NXB, lambda a, b_, tl=tl: hTm[:, a:b_, tl * 128:(tl + 1) * 128], ["hTm%d" % tl])
                  if s_ + 2 < NT + 1:
                      ht_load(xmain.ap()[(s_ + 2) * 128:(s_ + 3) * 128, :], (s_ + 2) % NXB)
              S.retire(ht_keys())
          HT_ALL = ["hTm%d" % t_ for t_ in range(NT + 1)]
          HT_MAIN = HT_ALL[1:]

          wctr = [0]

          def load_cols(col0, ncols, dup=False):
              s = wctr[0] % NWS
              wctr[0] += 1
              k = "wsl%d" % s
              if dup:
                  for hh in range(2):
                      dma("pool", wsl[s][:, :, 64 * hh:64 * hh + 64],
                          w_in.ap()[:, col0:col0 + 64].rearrange("(kc p) c -> p kc c", p=128), [], [k], k + ("a" if hh == 0 else "b"))
              else:
                  dma("pool", wsl[s][:, :, 0:ncols], w_in.ap()[:, col0:col0 + ncols].rearrange("(kc p) c -> p kc c", p=128), [], [k], k + "a")
              return wsl[s], k

          def proj_fm(wt, wk, coff, banks, ntok0, ntoks):
              for (bi, t0_, n_) in banks:
                  hk_ = ["hTm%d" % t_ for t_ in range(t0_ // 128, (t0_ + n_ + 127) // 128)]
                  for kc in range(KC):
                      mm(bank[bi][:, 0:n_], wt[:, kc, coff:coff + 128], hTm[:, kc, t0_:t0_ + n_], kc == 0, kc == KC - 1,
                         [wk] + hk_, [BK[bi]])

          pa = ExitStack()
          kT2 = [sbuf(pa, "kT2_%d" % kv, [128, HALO + TOK], BF16) for kv in range(2)]
          v_sb = sbuf(pa, "v_sb", [128, NT + 1, 128], BF16)
          qf = sbuf(pa, "qf", [128, HALO + TOK]); sq = sbuf(pa, "sq", [128, HALO + TOK])

          def qk_norm(nb_, ntok, wcol, scale_imm, dst, dkeys):
              per = ntok // nb_
              for i in range(nb_):
                  sl = slice(i * per, (i + 1) * per)
                  act(sq[:, sl], bank[i][:, 0:per], AF.Square, [BK[i]], ["sq"])
                  cp(qf[:, sl], bank[i][:, 0:per], [BK[i]], ["qf"], eng="act")
              for i in range(nb_):
                  sl = slice(i * per, (i + 1) * per)
                  mm(bank[i][:, 0:per], blockones[:], sq[:, sl], True, True, ["blockones", "sq"], [BK[i]])
              for i in range(nb_):
                  sl = slice(i * per, (i + 1) * per)
                  act(sq[:, sl], bank[i][:, 0:per], AF.Sqrt, [BK[i], "epsc"], ["sq"], bias=epsc[:], scale=1.0 / 64.0)
              recip(sq[:, 0:ntok], sq[:, 0:ntok], ["sq"], ["sq"])
              tt(qf[:, 0:ntok], qf[:, 0:ntok], sq[:, 0:ntok], ALU.mult, ["qf", "sq"], ["qf"])
              ts(dst, qf[:, 0:ntok], wcol, scale_imm, ALU.mult, ALU.mult, ["qf", "qw2", "kw2"], dkeys)

          for kv in range(2):
              wt, wk = load_cols(C_K + 64 * kv, 64, dup=True)
              proj_fm(wt, wk, 0, [(0, 0, 384), (1, 384, 384), (2, 768, 384)], 0, HALO + TOK)
              qk_norm(3, HALO + TOK, kw2[:, 0:1], 1.0, kT2[kv][:], ["kT2_%d" % kv])
          wt, wk = load_cols(C_V, 128)
          for tl in range(NT + 1):
              bi = 6 + (tl % 2)
              for kc in range(KC):
                  mm(bank[bi][:, 0:128], hTm[:, kc, tl * 128:(tl + 1) * 128], wt[:, kc, 0:128], kc == 0, kc == KC - 1,
                     [wk, "hTm%d" % tl], [BK[bi]])
              cp(v_sb[:, tl, :], bank[bi][:, 0:128], [BK[bi]], ["v_sb"])
          dbg("kT2_0", kT2[0][:], [128, HALO + TOK], ["kT2_0"])
          dbg("v_sb", v_sb[:], [128, NT + 1, 128], ["v_sb"])

          d_i = sbuf(pa, "d_i", [128, 128], I32); dmat = sbuf(pa, "dmat", [128, 128])
          S.op("pool", lambda e: e.iota(d_i[:], pattern=[[1, 128]], base=0, channel_multiplier=-1), writes=["d_i"])
          cp(dmat[:], d_i[:], ["d_i"], ["dmat"])
          mcur = sbuf(pa, "mcur", [128, 128]); mprev = sbuf(pa, "mprev", [128, 128])
          dcl = sbuf(pa, "dcl", [128, 128]); dpl = sbuf(pa, "dpl", [128, 128])
          S.op("dve", lambda e: e.tensor_single_scalar(out=mcur[:], in_=dmat[:], scalar=0.0, op=ALU.is_ge), reads=["dmat"], writes=["mcur"])
          S.op("dve", lambda e: e.tensor_single_scalar(out=mprev[:], in_=dmat[:], scalar=0.0, op=ALU.is_lt), reads=["dmat"], writes=["mprev"])
          ts(dcl[:], dmat[:], 0.0, None, ALU.max, None, ["dmat"], ["dcl"])
          ts(dpl[:], dmat[:], 128.0, None, ALU.add, None, ["dmat"], ["dpl"])
          Etab = [sbuf(pa, "Etab%d" % i, [128, 2, 2, 128]) for i in range(2)]
          qT = [sbuf(pa, "qT%d" % i, [128, TOK], BF16) for i in range(2)]
          zaT = [sbuf(pa, "zaT%d" % i, [128, TOK], BF16) for i in range(2)]
          pexp = [sbuf(pa, "pexp%d" % i, [128, 512]) for i in range(2)]
          Pt = [[sbuf(pa, "Pt%d%d" % (e, c), [128, 512], BF16) for c in range(2)] for e in range(2)]
          rec = sbuf(pa, "rec", [128, 512]); onum = sbuf(pa, "onum", [128, 512])
          CtX = ybuf[:, 0, :].bitcast(F32).rearrange("p (a c) -> p a c", c=16)
          tb2X = ybuf[:, 1, :].bitcast(F32).rearrange("p (a c) -> p a c", c=16)
          memset(Cmat[:], 0.0, ["Cmat"])
          CMQ = []

          def cm_piece(i, part):
              m = i + 1
              if part == 0:
                  tt(CtX, cre2[:], bcp(pwr[:, m, :]), ALU.mult, ["cre2", "pwr%d" % m], ["ybuf0"], eng="pool")
                  tt(tb2X, cim2[:], bcp(pwi[:, m, :]), ALU.mult, ["cim2", "pwi%d" % m], ["ybuf1"], eng="pool")
                  tt(CtX, CtX, tb2X, ALU.subtract, ["ybuf0", "ybuf1"], ["ybuf0"], eng="pool")
                  for e in range(2):
                      ps_ = slice(64 * e, 64 * e + 64)
                      cp(Cmat[ps_, :, i, 0, 16 * e:16 * e + 16], CtX[ps_], ["ybuf0", "Cmat"], ["Cmat"], eng="pool")
              else:
                  tt(CtX, cre2[:], bcp(pwi[:, m, :]), ALU.mult, ["cre2", "pwi%d" % m], ["ybuf0"], eng="pool")
                  tt(tb2X, cim2[:], bcp(pwr[:, m, :]), ALU.mult, ["cim2", "pwr%d" % m], ["ybuf1"], eng="pool")
                  tt(CtX, CtX, tb2X, ALU.add, ["ybuf0", "ybuf1"], ["ybuf0"], eng="pool")
                  for e in range(2):
                      ps_ = slice(64 * e, 64 * e + 64)
                      ts(Cmat[ps_, :, i, 1, 16 * e:16 * e + 16], CtX[ps_], -1.0, None, ALU.mult, None, ["ybuf0", "Cmat"], ["Cmat"], eng="pool")

          for i_c in range(T):
              for part_c in range(2):
                  CMQ.append((i_c, part_c))

          WQ = {}

          def attn_prep(i):
              kv = i // 4
              s2 = i % 2
              for e in range(2):
                  h = 2 * i + e
                  slope = 2.0 ** (-8.0 * (h + 1) / 16.0)
                  act(Etab[s2][:, e, 0, :], dcl[:], AF.Exp, ["dcl"], ["Etab%d" % s2], scale=-slope)
                  tt(Etab[s2][:, e, 0, :], Etab[s2][:, e, 0, :], mcur[:], ALU.mult, ["Etab%d" % s2, "mcur"], ["Etab%d" % s2])
                  act(Etab[s2][:, e, 1, :], dpl[:], AF.Exp, ["dpl"], ["Etab%d" % s2], scale=-slope)
                  tt(Etab[s2][:, e, 1, :], Etab[s2][:, e, 1, :], mprev[:], ALU.mult, ["Etab%d" % s2, "mprev"], ["Etab%d" % s2])
              if i % 2 == 0:
                  WQ['q'] = load_cols(C_Q + 128 * i, 256)
                  if CMQ:
                      cm_piece(*CMQ.pop(0))
                  WQ['z'] = load_cols(C_ZA + 128 * i, 256)
                  if CMQ:
                      cm_piece(*CMQ.pop(0))
              coff = 128 * (i % 2)
              proj_fm(WQ['q'][0], WQ['q'][1], coff, [(0, HALO, 512), (1, HALO + 512, 512)], HALO, TOK)
              qk_norm(2, TOK, qw2[:, 0:1], 0.125, qT[s2][:], ["qT%d" % s2])
              proj_fm(WQ['z'][0], WQ['z'][1], coff, [(0, HALO, 512), (1, HALO + 512, 512)], HALO, TOK)
              for hb in range(2):
                  act(zaT[s2][:, 512 * hb:512 * hb + 512], bank[hb][:], AF.Silu, [BK[hb]], ["zaT%d" % s2])
              if i == 0:
                  dbg("qT0", qT[0][:], [128, TOK], ["qT0"])

          def attn_core(i):
              kv = i // 4
              s2 = i % 2
              for hq in range(2):
                  for e in range(2):
                      ps_ = slice(64 * e, 64 * e + 64)
                      for c in range(2):
                          bi = 2 + 2 * e + c
                          for qb in range(4):
                              qabs = 4 * hq + qb
                              tok0 = (HALO + 128 * qabs) if c == 0 else (128 * qabs)
                              mm(bank[bi][:, 128 * qb:128 * qb + 128], kT2[kv][ps_, tok0:tok0 + 128],
                                 qT[s2][ps_, 128 * qabs:128 * qabs + 128], True, True,
                                 ["kT2_%d" % kv, "qT%d" % s2], [BK[bi]], tp=(64 * e, 0))
                  for e in range(2):
                      for c in range(2):
                          bi = 2 + 2 * e + c
                          px = pexp[c]; pk = "pexp%d" % c
                          act(px[:], bank[bi][:], AF.Exp, [BK[bi]], [pk])
                          tt(Pt[e][c][:].rearrange("p (a b) -> p a b", b=128), px[:].rearrange("p (a b) -> p a b", b=128),
                             Etab[s2][:, e, c, :].unsqueeze(1).to_broadcast([128, 4, 128]), ALU.mult,
                             [pk, "Etab%d" % s2], ["Pt%d%d" % (e, c)])
                          if c == 1 and hq == 0:
                              ts(Pt[e][1][:, 0:128], Pt[e][1][:, 0:128], flag[:, 0:1], None, ALU.mult, None,
                                 ["Pt%d1" % e, "flag"], ["Pt%d1" % e])
                  for e in range(2):
                      ps_ = slice(64 * e, 64 * e + 64)
                      for qb in range(4):
                          qabs = 4 * hq + qb
                          cs_ = slice(128 * qb, 128 * qb + 128)
                          for c, blk in ((1, qabs), (0, qabs + 1)):
                              mm(bank[6][ps_, cs_], v_sb[:, blk, 64 * kv:64 * kv + 64], Pt[e][c][:, cs_], c == 1, c == 0,
                                 ["v_sb", "Pt%d%d" % (e, c)], [BK[6]], tp=(0, 64 * e))
                              mm(bank[7][ps_, cs_], ones_bf[:], Pt[e][c][:, cs_], c == 1, c == 0,
                                 ["ones_bf", "Pt%d%d" % (e, c)], [BK[7]], tp=(0, 64 * e))
                  ts(rec[:], bank[7][:], sinkx[:, i:i + 1], None, ALU.add, None, [BK[7], "sinkx"], ["rec"])
                  recip(rec[:], rec[:], ["rec"], ["rec"])
                  tt(onum[:], bank[6][:], rec[:], ALU.mult, [BK[6], "rec"], ["onum"])
                  tt(mixa[:, i, 512 * hq:512 * hq + 512], onum[:], zaT[s2][:, 512 * hq:512 * hq + 512], ALU.mult,
                     ["onum", "zaT%d" % s2], ["mixa%d" % i])

          attn_prep(0)
          for i in range(8):
              if i + 1 < 8:
                  attn_prep(i + 1)
              attn_core(i)
          while CMQ:
              cm_piece(*CMQ.pop(0))
          dbg("Cmat", Cmat[:], [128, 32, T, 2, 32], ["Cmat"])
          dbg("mixa", mixa[:], [128, 8, TOK], ["mixa%d" % i for i in range(8)])
          S.retire(["kT2_0", "kT2_1", "v_sb", "qf", "sq", "d_i", "dmat", "mcur", "mprev", "dcl", "dpl", "Etab0", "Etab1",
                    "qT0", "qT1", "zaT0", "zaT1", "pexp0", "pexp1", "Pt00", "Pt01", "Pt10", "Pt11", "rec", "onum"])
          pa.close()
          if stop == "pa":
              return

          with ExitStack() as p3:
              uT = [sbuf(p3, "uT%d" % i, [128, TOK], BF16) for i in range(2)]
              cosT = sbuf(p3, "cosT", [128, 4, NCH]); sinT = sbuf(p3, "sinT", [128, 4, NCH])
              coefT = sbuf(p3, "coefT", [128, 4, NCH])
              Btr = sbuf(p3, "Btr", [128, 4, NCH]); Bti = sbuf(p3, "Bti", [128, 4, NCH])
              rr = sbuf(p3, "rr", [128, 4, NCH]); ri = sbuf(p3, "ri", [128, 4, NCH])
              m1 = sbuf(p3, "m1", [128, 4, NCH]); m2 = sbuf(p3, "m2", [128, 4, NCH])
              psi = m1
              Spr = sbuf(p3, "Spr", [128, 4, NCH], BF16); Spi = sbuf(p3, "Spi", [128, 4, NCH], BF16)
              g1 = Btr[:].rearrange("p a b -> p (a b)"); g2 = Bti[:].rearrange("p a b -> p (a b)")
              _sc_cache["scm"] = (rr, ri)
              WU = {}

              def ssm_proj(b):
                  us = uT[b % 2]; uk = "uT%d" % (b % 2)
                  if b % 2 == 0:
                      WU["u"] = load_cols(C_U + 128 * b, 256)
                      WU["z"] = load_cols(C_ZS + 128 * b, 256)
                  coff = 128 * (b % 2)
                  proj_fm(WU["u"][0], WU["u"][1], coff, [(0, HALO, 512), (1, HALO + 512, 512)], HALO, TOK)
                  for hb in range(2):
                      cp(us[:, 512 * hb:512 * hb + 512], bank[hb][:], [BK[hb]], [uk], eng="act")
                  proj_fm(WU["z"][0], WU["z"][1], coff, [(0, HALO, 512), (1, HALO + 512, 512)], HALO, TOK)
                  for hb in range(2):
                      act(zsg[:, b, 512 * hb:512 * hb + 512], bank[hb][:], AF.Silu, [BK[hb]], ["zsg%d" % b])

              ssm_proj(0)
              for b in range(8):
                  us = uT[b % 2]; uk = "uT%d" % (b % 2)
                  for q in range(4):
                      pr_ = 4 * b + q
                      ts(psi[:, q, :], kk1[:], phi[:, pr_:pr_ + 1], None, ALU.mult, None, ["kk1", "phi"], ["m1"])
                      ts(coefT[:, q, :], kk1[:], 0.0, rho[:, pr_:pr_ + 1], ALU.mult, ALU.add, ["kk1", "rho"], ["coefT"])
                  memset(coefT[:, :, 0:1], 0.0, ["coefT"])
                  sincos(p3, "scm", psi[:], [128, 4, NCH], ["m1"], sinT[:], cosT[:], ["sinT"], ["cosT"], keys=("rr", "ri"))
                  for half in range(2):
                      yb = bank[2 + half][:].rearrange("p (k i) -> p k i", i=T)
                      uv = us[:, 512 * half:512 * half + 512].rearrange("p (k i) -> p k i", i=T)
                      for m in range(T):
                          mm(yb[:, :, m:T], Kblk[:, b, m, :], uv[:, :, 0:T - m], m == 0, False, ["Kblk", uk], [BK[2 + half]])
                  uv4 = us[:].rearrange("p (k i) -> p k i", i=T)
                  for part in range(2):
                      for j in range(T):
                          for q in range(4):
                              mm(bank[4 + q][:, 256 * part:256 * part + 256], Bmat[32 * q:32 * q + 32, b, j, part, :],
                                 uv4[32 * q:32 * q + 32, :, j], j == 0, j == T - 1, ["Bmat", uk], [BK[4 + q]], tp=(32 * q, 0))
                  for q in range(4):
                      cp(Btr[:, q, :], bank[4 + q][:, 0:256], [BK[4 + q]], ["Btr"], eng="act")
                      cp(Bti[:, q, :], bank[4 + q][:, 256:512], [BK[4 + q]], ["Bti"], eng="act")
                  tt(m1[:], Btr[:], cosT[:], ALU.mult, ["Btr", "cosT"], ["m1"])
                  tt(m2[:], Bti[:], sinT[:], ALU.mult, ["Bti", "sinT"], ["m2"])
                  tt(rr[:], m1[:], m2[:], ALU.add, ["m1", "m2"], ["rr"])
                  tt(m1[:], Bti[:], cosT[:], ALU.mult, ["Bti", "cosT"], ["m1"])
                  tt(m2[:], Btr[:], sinT[:], ALU.mult, ["Btr", "sinT"], ["m2"])
                  tt(ri[:], m1[:], m2[:], ALU.subtract, ["m1", "m2"], ["ri"])
                  tt(rr[:, :, 0], rr[:, :, 0], rs_re[:, 4 * b:4 * b + 4], ALU.add, ["rr", "rs_re"], ["rr"])
                  tt(ri[:, :, 0], ri[:, :, 0], rs_im[:, 4 * b:4 * b + 4], ALU.add, ["ri", "rs_im"], ["ri"])
                  fl = lambda t_: t_[:].rearrange("p a b -> p (a b)")
                  S.op("dve", lambda e: e.tensor_tensor_scan(out=fl(Btr), data0=fl(coefT), data1=fl(rr), initial=0.0,
                                                             op0=ALU.mult, op1=ALU.add), reads=["coefT", "rr"], writes=["Btr"])
                  S.op("dve", lambda e: e.tensor_tensor_scan(out=fl(Bti), data0=fl(coefT), data1=fl(ri), initial=0.0,
                                                             op0=ALU.mult, op1=ALU.add), reads=["coefT", "ri"], writes=["Bti"])
                  tt(m1[:], Btr[:], cosT[:], ALU.mult, ["Btr", "cosT"], ["m1"])
                  tt(m2[:], Bti[:], sinT[:], ALU.mult, ["Bti", "sinT"], ["m2"])
                  tt(Spr[:, :, 1:NCH], m1[:, :, 0:NCH - 1], m2[:, :, 0:NCH - 1], ALU.subtract, ["m1", "m2"], ["Spr"])
                  tt(m1[:], Btr[:], sinT[:], ALU.mult, ["Btr", "sinT"], ["m1"])
                  tt(m2[:], Bti[:], cosT[:], ALU.mult, ["Bti", "cosT"], ["m2"])
                  tt(Spi[:, :, 1:NCH], m1[:, :, 0:NCH - 1], m2[:, :, 0:NCH - 1], ALU.add, ["m1", "m2"], ["Spi"])
                  cp(Spr[:, :, 0], sinb_re[:, 4 * b:4 * b + 4], ["sinb_re", "Spr"], ["Spr"])
                  cp(Spi[:, :, 0], sinb_im[:, 4 * b:4 * b + 4], ["sinb_im", "Spi"], ["Spi"])
                  if b + 1 < 8:
                      ssm_proj(b + 1)
                  for half in range(2):
                      yb = bank[2 + half][:].rearrange("p (k i) -> p k i", i=T)
                      for q in range(4):
                          pr_ = 4 * b + q
                          for i_ in range(T):
                              for part, Sp, sk in ((0, Spr, "Spr"), (1, Spi, "Spi")):
                                  mm(yb[32 * q:32 * q + 32, :, i_], Cmat[:, pr_, i_, part, :], Sp[:, q, 128 * half:128 * half + 128],
                                     False, (i_ == T - 1 and part == 1), ["Cmat", sk], [BK[2 + half]], tp=(0, 32 * q))
                  for half in range(2):
                      sl = slice(512 * half, 512 * half + 512)
                      yb = bank[2 + half][:]
                      act(g1[:, sl], yb, AF.Square, [BK[2 + half]], ["Btr"])
                      ts(g1[:, sl], g1[:, sl], 0.044715, 1.0, ALU.mult, ALU.add, ["Btr"], ["Btr"])
                      tt(g1[:, sl], g1[:, sl], yb, ALU.mult, ["Btr", BK[2 + half]], ["Btr"])
                      act(g2[:, sl], g1[:, sl], AF.Sigmoid, ["Btr"], ["Bti"], scale=1.5957691216057308)
                      tt(ybuf[:, b, sl], g2[:, sl], yb, ALU.mult, ["Bti", BK[2 + half]], ["ybuf%d" % b])
                      if b == 0:
                          pass
              dbg("ybuf", ybuf[:], [128, 8, TOK], ["ybuf%d" % b for b in range(8)])
              S.retire(["uT0", "uT1", "cosT", "sinT", "coefT", "Btr", "Bti", "rr", "ri", "m1", "m2", "Spr", "Spi"])
          S.retire(HT_ALL + ["wsl%d" % i for i in range(NWS)])
          if stop == "p3":
              return

      with ExitStack() as p4:
          mixs = sbuf(p4, "mixs", [128, 8, TOK], BF16)
          wg = [sbuf(p4, "wg%d" % i, [128, 8, 128], BF16) for i in range(3)]
          sg = sbuf(p4, "sg", [128, TOK]); tg = sbuf(p4, "tg", [128, TOK])
          YB = ["ybuf%d" % b for b in range(8)]
          wo = [sbuf(p4, "wo%d" % i, [128, KC, 512], BF16) for i in range(2)]
          xr = [sbuf(p4, "xr%d" % i, [128, 512]) for i in range(3)]
          ot = [sbuf(p4, "ot%d" % i, [128, 512]) for i in range(3)]

          def load_wo(c):
              s_ = c % 2
              for hh in range(2):
                  dma("pool", wo[s_][:, 8 * hh:8 * hh + 8, :],
                      w_out.ap()[1024 * hh:1024 * hh + 1024, 512 * c:512 * c + 512].rearrange("(kc p) c -> p kc c", p=128),
                      [], ["wo%d_%d" % (s_, hh)], "wo%d_%d" % (s_, hh))

          for f in range(8):
              if f == 3:
                  load_wo(0)
              if f == 6:
                  load_wo(1)
              s = f % 3
              dma("pool", wg[s][:], w_glu.ap()[:, 128 * f:128 * f + 128].rearrange("(kc p) c -> p kc c", p=128), [], ["wg%d" % s], "wg%d" % s)
              b0 = 2 * (f % 2)
              for half in range(2):
                  for kc in range(8):
                      mm(bank[b0 + half][:], wg[s][:, kc, :], ybuf[:, kc, 512 * half:512 * half + 512], kc == 0, kc == 7,
                         ["wg%d" % s] + YB, [BK[b0 + half]])
              for half in range(2):
                  sl = slice(512 * half, 512 * half + 512)
                  act(sg[:, sl], bank[b0 + half][:], AF.Sigmoid, [BK[b0 + half], "bglu"], ["sg"], bias=bglu[:, f:f + 1], scale=1.0)
              tt(tg[:], sg[:], ybuf[:, f, :], ALU.mult, ["sg", "ybuf%d" % f], ["tg"])
              tt(mixs[:, f, :], tg[:], zsg[:, f, :], ALU.mult, ["tg", "zsg%d" % f], ["mixs%d" % f])
          dbg("mixs", mixs[:], [128, 8, TOK], ["mixs%d" % f for f in range(8)])
          MIXA = ["mixa%d" % i for i in range(8)]; MIXS = ["mixs%d" % i for i in range(8)]
          cnt = 0
          for c in range(4):
              s = c % 2
              if c >= 2:
                  load_wo(c)
              for tl in range(NT):
                  bi = cnt % 8
                  xs = cnt % 3
                  cnt += 1
                  dma("sp", xr[xs][:], xmain.ap()[HALO + 128 * tl:HALO + 128 * tl + 128, 512 * c:512 * c + 512], [], ["xr%d" % xs], "xr%d" % xs)
                  for kc in range(KC):
                      lhs = mixa[:, kc, 128 * tl:128 * tl + 128] if kc < 8 else mixs[:, kc - 8, 128 * tl:128 * tl + 128]
                      mm(bank[bi][:], lhs, wo[s][:, kc, :], kc == 0, kc == KC - 1,
                         ["wo%d_%d" % (s, kc // 8)] + (MIXA if kc < 8 else MIXS), [BK[bi]])
                  tt(ot[xs][:], bank[bi][:], xr[xs][:], ALU.add, [BK[bi], "xr%d" % xs], ["ot%d" % xs])
                  fin.append(dma("sp", out_d.ap()[128 * tl:128 * tl + 128, 512 * c:512 * c + 512], ot[xs][:], ["ot%d" % xs], [], "ot%d" % xs))


    body()
    for st_ in _p0h:
        st_.close()
    S.emit(top, final_wait_ops=fin)
    top.close()
    return nc, dbg_out


def make_in_maps(inputs):
    f32 = np.float32
    x = np.asarray(inputs["x"], f32)[0]
    g = lambda k: np.asarray(inputs[k], f32)
    a_re, a_im, ldt = g("ssm_a_re"), g("ssm_a_im"), g("ssm_log_dt")

    def st2(a):
        return np.ascontiguousarray(a.reshape(32, 2, 64).transpose(1, 2, 0).reshape(128, 32))

    def b2(a):
        return np.ascontiguousarray(a.reshape(32, 2, 64, 16).transpose(1, 2, 0, 3).reshape(128, 32, 16))

    def c2(a):
        return np.ascontiguousarray(a.reshape(32, 2, 16, 64).transpose(1, 3, 0, 2).reshape(128, 32, 16))

    shared = {
        "w_in": g("w_in"), "w_glu": g("w_glu"), "w_out": g("w_out"),
        "normw_row": g("norm_w").reshape(1, D_MODEL),
        "qw2": np.ascontiguousarray(np.tile(g("q_norm_w"), 2).reshape(128, 1)),
        "kw2": np.ascontiguousarray(np.tile(g("k_norm_w"), 2).reshape(128, 1)),
        "sinkcol": np.ascontiguousarray(g("attn_sinks").reshape(8, 2).T.repeat(64, axis=0)),
        "are2": st2(a_re), "aim2": st2(a_im),
        "ldt2": st2(np.broadcast_to(ldt[:, None], (64, 64))),
        "bre2": b2(g("ssm_b_re")), "bim2": b2(g("ssm_b_im")),
        "cre2": c2(g("ssm_c_re")), "cim2": c2(g("ssm_c_im")),
        "are_row": a_re.reshape(1, 4096), "aim_row": a_im.reshape(1, 4096), "ldt_row": ldt.reshape(1, 64),
        "dskip": np.ascontiguousarray(g("ssm_d").reshape(8, 128).T),
        "bglu": np.ascontiguousarray(g("b_glu").reshape(8, 128).T),
    }
    maps = []
    for c in range(NCORES):
        t0 = c * TOK
        xm = np.zeros((HALO + TOK, D_MODEL), f32)
        xm[HALO:] = x[t0:t0 + TOK]
        if c > 0:
            xm[:HALO] = x[t0 - HALO:t0]
        xp = np.zeros((NPRE * 128, D_MODEL), f32)
        if c > 0:
            xp[NPRE * 128 - t0:] = x[:t0]
        m = dict(shared)
        m["xpre"] = xp
        m["xmain"] = xm
        m["flag"] = np.full((128, 1), 0.0 if c == 0 else 1.0, f32)
        maps.append(m)
    return maps


_NC_CACHE = {}


def kernel(**inputs):
    dbg = tuple(sorted(DEBUG.get("names", ())))
    if dbg not in _NC_CACHE:
        _NC_CACHE[dbg] = build_nc(debug=dbg, stop=DEBUG.get("stop"))
    nc, dbg_out = _NC_CACHE[dbg]
    maps = make_in_maps(inputs)
    cores = DEBUG.get("cores", list(range(NCORES)))
    res = run_bass_kernel_spmd(nc, [maps[c] for c in cores], core_ids=list(range(len(cores))))
    if dbg:
        DEBUG["results"] = res.results
    out = np.concatenate([r["out"] for r in res.results], axis=0)
    return out.reshape(1, -1, D_MODEL).astype(np.float32)
```

```python
import math
from contextlib import ExitStack
import numpy as np
import concourse.bass as bass
import concourse.mybir as mybir
from concourse.bass_utils import run_bass_kernel_spmd

F32 = mybir.dt.float32
BF16 = mybir.dt.bfloat16
I32 = mybir.dt.int32
AF = mybir.ActivationFunctionType
ALU = mybir.AluOpType
AX = mybir.AxisListType

NCORES = 8
D_MODEL = 2048
SEQ = 8192
TOK = SEQ // NCORES
HALO = 128
NT = TOK // 128
NPRE = (SEQ - TOK) // 128
NBATCH = 4
KC = D_MODEL // 128
T = 4
NCH = TOK // T
IN_W = 4352
C_Q, C_K, C_V, C_ZA, C_U, C_ZS = 0, 1024, 1152, 1280, 2304, 3328
EPS = 1e-6
TWO_PI = 2.0 * math.pi
MAGIC = 12582912.0
ENGS = ("pe", "act", "dve", "pool", "sp")
DEBUG = {}
SAME_WAIT = ("dve", "act")
BIG_OP = 512


class Sched:
    def __init__(self, nc):
        self.nc = nc
        self.ops = []
        self.last_w = {}
        self.readers = {}
        self.seen = set()
        self.fence = set()
        self.capture = None
        self.pending = []

    def mark(self):
        if self.capture:
            self.pending.append(None)

    def replay(self, k):
        cap, self.capture = self.capture, None
        n = 0
        while self.pending:
            a = self.pending.pop(0)
            if a is None:
                if n >= k:
                    break
                continue
            self.op(*a)
            n += 1
        self.capture = cap

    def retire(self, keys):
        for k in keys:
            w = self.last_w.pop(k, None)
            if w is not None:
                self.fence.add(w)
            for r in self.readers.pop(k, ()):
                self.fence.add(r)

    def op(self, eng, fn, reads=(), writes=(), dsem=None, nfree=0):
        if self.capture:
            self.pending.append((eng, fn, list(reads), list(writes), dsem, nfree))
            return None
        bk = [k for k in reads if isinstance(k, str) and k.startswith("bank")]
        if bk:
            reads = [k for k in reads if k not in bk]
            writes = list(writes) + [k for k in bk if k not in writes]
        idx = len(self.ops)
        deps = set()
        for k in list(reads) + list(writes):
            if k not in self.seen:
                self.seen.add(k)
                deps |= self.fence
        for k in reads:
            w = self.last_w.get(k)
            if w is not None:
                deps.add(w)
        for k in writes:
            w = self.last_w.get(k)
            if w is not None:
                deps.add(w)
            for r in self.readers.get(k, ()):
                deps.add(r)
        deps.discard(idx)
        self.ops.append(dict(eng=eng, fn=fn, deps=deps, dsem=dsem, need_inc=False, nfree=nfree))
        for k in reads:
            self.readers.setdefault(k, []).append(idx)
        for k in writes:
            self.last_w[k] = idx
            self.readers[k] = []
        return idx

    def emit(self, stack, final_wait_ops=()):
        nc = self.nc
        ops = self.ops

        def skip(do, o):
            if not (do["dsem"] is None and o["dsem"] is None and do["eng"] == o["eng"] and do["eng"] != "sp"):
                return False
            if do["eng"] not in DEBUG.get("same_wait", SAME_WAIT):
                return True
            return do["eng"] == "dve" and do["nfree"] >= DEBUG.get("big", BIG_OP)

        for o in ops:
            for d in o["deps"]:
                if not skip(ops[d], o):
                    ops[d]["need_inc"] = True
        for d in final_wait_ops:
            ops[d]["need_inc"] = True
        esem = {e: stack.enter_context(nc.semaphore("s_" + e)) for e in ENGS}
        dsems = {}
        for o in ops:
            if o["dsem"] is not None and o["dsem"] not in dsems:
                dsems[o["dsem"]] = stack.enter_context(nc.semaphore("d_" + o["dsem"]))
        ecount = {e: 0 for e in ENGS}
        dcount = {k: 0 for k in dsems}
        for o in ops:
            if o["dsem"] is not None:
                dcount[o["dsem"]] += 16
                o["ticket"] = (o["dsem"], dsems[o["dsem"]], dcount[o["dsem"]])
            elif o["need_inc"]:
                ecount[o["eng"]] += 1
                o["ticket"] = (o["eng"], esem[o["eng"]], ecount[o["eng"]])
            else:
                o["ticket"] = None
        block = stack.enter_context(nc.Block())
        per_eng = {e: [o for o in ops if o["eng"] == e] for e in ENGS}

        def run(engname, eng):
            waited = {}
            for o in per_eng[engname]:
                need = {}
                for d in o["deps"]:
                    do = ops[d]
                    if skip(do, o):
                        continue
                    name, sem, val = do["ticket"]
                    if waited.get(name, 0) >= val:
                        continue
                    if need.get(name, (None, 0))[1] < val:
                        need[name] = (sem, val)
                for name, (sem, val) in need.items():
                    waited[name] = val
                    eng.wait_ge(sem, val)
                ins = o["fn"](eng)
                if o["dsem"] is not None:
                    ins.then_inc(o["ticket"][1], 16)
                elif o["need_inc"]:
                    ins.then_inc(o["ticket"][1], 1)
            if engname == "sp":
                for d in final_wait_ops:
                    _, sem, val = ops[d]["ticket"]
                    eng.wait_ge(sem, val)

        @block.tensor
        def _(e):
            run("pe", e)

        @block.scalar
        def _(e):
            run("act", e)

        @block.vector
        def _(e):
            run("dve", e)

        @block.gpsimd
        def _(e):
            run("pool", e)

        @block.sync
        def _(e):
            run("sp", e)


class _Stop(Exception):
    pass


def build_nc(debug=(), stop=None):
    nc = bass.Bass("TRN2", target_bir_lowering=False)
    S = Sched(nc)

    def din(name, shape):
        return nc.dram_tensor(name, list(shape), F32, kind="ExternalInput")

    xpre = din("xpre", [NPRE * 128, D_MODEL])
    xmain = din("xmain", [HALO + TOK, D_MODEL])
    w_in = din("w_in", [D_MODEL, IN_W])
    w_glu = din("w_glu", [1024, 1024])
    w_out = din("w_out", [2048, 2048])
    normw_row = din("normw_row", [1, D_MODEL])
    qw2_d = din("qw2", [128, 1]); kw2_d = din("kw2", [128, 1])
    sink_d = din("sinkcol", [128, 8])
    are2_d = din("are2", [128, 32]); aim2_d = din("aim2", [128, 32]); ldt2_d = din("ldt2", [128, 32])
    bre2_d = din("bre2", [128, 32, 16]); bim2_d = din("bim2", [128, 32, 16])
    cre2_d = din("cre2", [128, 32, 16]); cim2_d = din("cim2", [128, 32, 16])
    are_row = din("are_row", [1, 4096]); aim_row = din("aim_row", [1, 4096]); ldt_row = din("ldt_row", [1, 64])
    dskip_d = din("dskip", [128, 8]); bglu_d = din("bglu", [128, 8]); flag_d = din("flag", [128, 1])
    out_d = nc.dram_tensor("out", [TOK, D_MODEL], F32, kind="ExternalOutput")
    dbg_out = {}

    fin = []
    top = ExitStack()

    def sbuf(st, name, shape, dt=F32):
        return st.enter_context(nc.sbuf_tensor("sb_" + name, list(shape), dt))

    def _nf(ap):
        try:
            n = ap.free_size
            return int(n() if callable(n) else n)
        except Exception:
            return 0

    def dma(q, out, in_, r, w, dsem):
        return S.op(q, lambda e: e.dma_start(out=out, in_=in_), reads=r, writes=w, dsem=dsem)

    def tt(out, a, b, op, r, w, eng="dve"):
        return S.op(eng, lambda e: e.tensor_tensor(out=out, in0=a, in1=b, op=op), reads=r, writes=w, nfree=_nf(out))

    def ts(out, a, s1, s2, op0, op1, r, w, eng="dve"):
        if s2 is None:
            return S.op(eng, lambda e: e.tensor_scalar(out=out, in0=a, scalar1=s1, scalar2=None, op0=op0), reads=r, writes=w, nfree=_nf(out))
        return S.op(eng, lambda e: e.tensor_scalar(out=out, in0=a, scalar1=s1, scalar2=s2, op0=op0, op1=op1), reads=r, writes=w, nfree=_nf(out))

    def stt(out, a, scalar, b, op0, op1, r, w, eng="dve"):
        return S.op(eng, lambda e: e.scalar_tensor_tensor(out=out, in0=a, scalar=scalar, in1=b, op0=op0, op1=op1), reads=r, writes=w, nfree=_nf(out))

    def cp(out, a, r, w, eng="dve"):
        if eng == "act":
            return S.op("act", lambda e: e.copy(out=out, in_=a), reads=r, writes=w)
        return S.op(eng, lambda e: e.tensor_copy(out=out, in_=a), reads=r, writes=w, nfree=_nf(out))

    def act(out, a, func, r, w, bias=None, scale=None, accum=None):
        kw = {}
        if bias is not None:
            kw["bias"] = bias
        if scale is not None:
            kw["scale"] = scale
        if accum is not None:
            kw["accum_out"] = accum
        return S.op("act", lambda e: e.activation(out=out, in_=a, func=func, **kw), reads=r, writes=w)

    def mm(out, lhsT, rhs, start, stop, r, w, tp=None):
        kw = {}
        if tp is not None:
            kw["tile_position"] = tp
        return S.op("pe", lambda e: e.matmul(out, lhsT=lhsT, rhs=rhs, start=start, stop=stop, skip_group_check=True, **kw), reads=r, writes=w)

    def trn(out, a, ident, r, w):
        return S.op("pe", lambda e: e.matmul(out, lhsT=a, rhs=ident, start=True, stop=True, is_transpose=True, skip_group_check=True),
                    reads=r, writes=w)

    def memset(ap, v, w, eng="pool"):
        return S.op(eng, lambda e: e.memset(ap, v), writes=w)

    def recip(out, a, r, w):
        return S.op("dve", lambda e: e.reciprocal(out=out, in_=a), reads=r, writes=w)

    def dbg(name, ap, shape, r):
        if name not in debug:
            return
        d = nc.dram_tensor("dbg_" + name, list(shape), ap.dtype, kind="ExternalOutput")
        dbg_out[name] = d
        fin.append(dma("sp", d.ap(), ap, r, [], "dbg_" + name))

    bank = [top.enter_context(nc.psum_tensor("bank%d" % i, [128, 512], F32)) for i in range(8)]
    BK = ["bank%d" % i for i in range(8)]

    P = top
    ident = sbuf(P, "ident", [128, 128])
    blockones = sbuf(P, "blockones", [128, 128])
    ones_bf = sbuf(P, "ones_bf", [128, 64], BF16)
    qw2 = sbuf(P, "qw2", [128, 1]); kw2 = sbuf(P, "kw2", [128, 1])
    sinkx = sbuf(P, "sinkx", [128, 8])
    dskip = sbuf(P, "dskip", [128, 8]); bglu = sbuf(P, "bglu", [128, 8]); flag = sbuf(P, "flag", [128, 1])
    epsc = sbuf(P, "epsc", [128, 1])
    halfpi = sbuf(P, "halfpi", [128, 1])
    are2 = sbuf(P, "are2", [128, 32]); aim2 = sbuf(P, "aim2", [128, 32]); dt2 = sbuf(P, "dt2", [128, 32])
    lam = sbuf(P, "lam", [128, 32]); th = sbuf(P, "th", [128, 32])
    pwr = sbuf(P, "pwr", [128, 5, 32]); pwi = sbuf(P, "pwi", [128, 5, 32])
    bbr = sbuf(P, "bbr", [128, 32, 16]); bbi = sbuf(P, "bbi", [128, 32, 16])
    cre2 = sbuf(P, "cre2", [128, 32, 16]); cim2 = sbuf(P, "cim2", [128, 32, 16])
    Pr = sbuf(P, "Pr", [128, 32]); Pi = sbuf(P, "Pi", [128, 32])
    rho = sbuf(P, "rho", [128, 32]); phi = sbuf(P, "phi", [128, 32])
    sin_re = sbuf(P, "sin_re", [128, 32]); sin_im = sbuf(P, "sin_im", [128, 32])
    rs_re = sbuf(P, "rs_re", [128, 32]); rs_im = sbuf(P, "rs_im", [128, 32])
    sinb_re = sbuf(P, "sinb_re", [128, 32], BF16); sinb_im = sbuf(P, "sinb_im", [128, 32], BF16)
    kk1 = sbuf(P, "kk1", [128, NCH])
    Bmat = sbuf(P, "Bmat", [128, 8, T, 2, 128], BF16)
    Kblk = sbuf(P, "Kblk", [128, 8, T, 128], BF16)
    NXB = 3
    ssum = [sbuf(P, "ssum%d" % i, [128, 1]) for i in range(NXB)]
    rstd = [sbuf(P, "rstd%d" % i, [128, 1]) for i in range(NXB)]
    HB = {}

    def alloc_ht(st, tag):
        HB["tag"] = tag
        HB["nw"] = sbuf(st, "nw_row" + tag, [128, D_MODEL])
        HB["xb"] = [sbuf(st, "xb%d%s" % (i, tag), [128, D_MODEL]) for i in range(NXB)]
        HB["sqj"] = sbuf(st, "sqj" + tag, [128, D_MODEL], BF16)
        dma("sp", HB["nw"][:], bass.AP(normw_row, 0, [[0, 128], [1, D_MODEL]]), [], ["nw_row" + tag], "p_nw" + tag)

    def ht_keys():
        t_ = HB["tag"]
        return ["nw_row" + t_, "sqj" + t_] + ["xb%d%s" % (i, t_) for i in range(NXB)]

    def pload(dst, src, key):
        dma("sp", dst, src, [], [key], "p_" + key)

    pload(qw2[:], qw2_d.ap(), "qw2"); pload(kw2[:], kw2_d.ap(), "kw2")
    pload(sinkx[:], sink_d.ap(), "sinkx"); pload(dskip[:], dskip_d.ap(), "dskip")
    pload(bglu[:], bglu_d.ap(), "bglu"); pload(flag[:], flag_d.ap(), "flag")
    pload(are2[:], are2_d.ap(), "are2"); pload(aim2[:], aim2_d.ap(), "aim2"); pload(dt2[:], ldt2_d.ap(), "dt2")
    pload(cre2[:], cre2_d.ap(), "cre2"); pload(cim2[:], cim2_d.ap(), "cim2")

    memset(ident[:], 0.0, ["ident"])
    S.op("pool", lambda e: e.affine_select(out=ident[:], in_=ident[:], pattern=[[-1, 128]], compare_op=ALU.not_equal,
                                           fill=1.0, base=0, channel_multiplier=1), reads=["ident"], writes=["ident"])
    memset(blockones[:], 0.0, ["blockones"])
    memset(blockones[0:64, 0:64], 1.0, ["blockones"])
    memset(blockones[64:128, 64:128], 1.0, ["blockones"])
    memset(ones_bf[:], 1.0, ["ones_bf"])
    memset(epsc[:], EPS, ["epsc"])
    memset(halfpi[:], 0.5 * math.pi, ["halfpi"])
    kk_i = sbuf(P, "kk_i", [128, NCH], I32)
    S.op("pool", lambda e: e.iota(kk_i[:], pattern=[[1, NCH]], base=1, channel_multiplier=0), writes=["kk_i"])
    cp(kk1[:], kk_i[:], ["kk_i"], ["kk1"])
    act(sinkx[:], sinkx[:], AF.Exp, ["sinkx"], ["sinkx"])

    _sc_cache = {}

    def sincos(st, tag, x, shape, rk, sin_out, cos_out, wk_s, wk_c, keys=None, eng="dve"):
        if tag not in _sc_cache:
            _sc_cache[tag] = (sbuf(st, tag + "_n1", shape), sbuf(st, tag + "_r1", shape))
        n1, r1 = _sc_cache[tag]
        k1, k2 = keys if keys is not None else (tag + "_n1", tag + "_r1")
        ts(n1[:], x, 1.0 / TWO_PI, MAGIC, ALU.mult, ALU.add, rk, [k1], eng=eng)
        ts(n1[:], n1[:], -MAGIC, -TWO_PI, ALU.add, ALU.mult, [k1], [k1], eng=eng)
        tt(r1[:], x, n1[:], ALU.add, rk + [k1], [k2], eng=eng)
        ts(r1[:], r1[:], 3.1415925, -3.1415925, ALU.min, ALU.max, [k2], [k2], eng=eng)
        act(sin_out, r1[:], AF.Sin, [k2], wk_s)
        act(n1[:], r1[:], AF.Abs, [k2], [k1])
        act(cos_out, n1[:], AF.Sin, [k1, "halfpi"], wk_c, bias=halfpi[:], scale=-1.0)

    _p0h = []

    def body():
      p0 = ExitStack()
      _p0h.append(p0)
      if True:
          act(dt2[:], dt2[:], AF.Exp, ["dt2"], ["dt2"])
          tt(lam[:], are2[:], dt2[:], ALU.mult, ["are2", "dt2"], ["lam"])
          tt(th[:], aim2[:], dt2[:], ALU.mult, ["aim2", "dt2"], ["th"])
          memset(pwr[:, 0, :], 1.0, ["pwr0"]); memset(pwi[:, 0, :], 0.0, ["pwi0"])
          MS = list(range(1, T + 1)) + [256]
          NM = len(MS)
          mvec = sbuf(p0, "mvec", [128, NM, 32]); ang5 = sbuf(p0, "ang5", [128, NM, 32]); mag5 = sbuf(p0, "mag5", [128, NM, 32])
          sn5 = sbuf(p0, "sn5", [128, NM, 32]); cs5 = sbuf(p0, "cs5", [128, NM, 32])
          for k_, m_ in enumerate(MS):
              memset(mvec[:, k_, :], float(m_), ["mvec"])
          lam_b = lam[:].unsqueeze(1).to_broadcast([128, NM, 32])
          th_b = th[:].unsqueeze(1).to_broadcast([128, NM, 32])
          tt(mag5[:], mvec[:], lam_b, ALU.mult, ["mvec", "lam"], ["mag5"])
          act(mag5[:], mag5[:], AF.Exp, ["mag5"], ["mag5"])
          tt(ang5[:], mvec[:], th_b, ALU.mult, ["mvec", "th"], ["ang5"])
          sincos(p0, "sc0", ang5[:], [128, NM, 32], ["ang5"], sn5[:], cs5[:], ["sn5"], ["cs5"])
          PWK = ["pwr%d" % m_ for m_ in range(1, T + 1)]; PWIK = ["pwi%d" % m_ for m_ in range(1, T + 1)]
          tt(pwr[:, 1:T + 1, :], mag5[:, 0:T, :], cs5[:, 0:T, :], ALU.mult, ["mag5", "cs5"], PWK)
          tt(pwi[:, 1:T + 1, :], mag5[:, 0:T, :], sn5[:, 0:T, :], ALU.mult, ["mag5", "sn5"], PWIK)
          tt(Pr[:], mag5[:, T, :], cs5[:, T, :], ALU.mult, ["mag5", "cs5"], ["Pr"])
          tt(Pi[:], mag5[:, T, :], sn5[:, T, :], ALU.mult, ["mag5", "sn5"], ["Pi"])
          cp(rho[:], mag5[:, T - 1, :], ["mag5"], ["rho"])
          cp(phi[:], ang5[:, T - 1, :], ["ang5"], ["phi"])
          nr = sbuf(p0, "nr", [128, 32]); den = sbuf(p0, "den", [128, 32]); t0 = sbuf(p0, "t0", [128, 32])
          cr = sbuf(p0, "cr", [128, 32]); ci = sbuf(p0, "ci", [128, 32])
          ts(nr[:], pwr[:, 1, :], -1.0, None, ALU.add, None, ["pwr1"], ["nr"])
          tt(den[:], are2[:], are2[:], ALU.mult, ["are2"], ["den"])
          tt(t0[:], aim2[:], aim2[:], ALU.mult, ["aim2"], ["t0"])
          tt(den[:], den[:], t0[:], ALU.add, ["den", "t0"], ["den"])
          recip(den[:], den[:], ["den"], ["den"])
          tt(cr[:], nr[:], are2[:], ALU.mult, ["nr", "are2"], ["cr"])
          tt(t0[:], pwi[:, 1, :], aim2[:], ALU.mult, ["pwi1", "aim2"], ["t0"])
          tt(cr[:], cr[:], t0[:], ALU.add, ["cr", "t0"], ["cr"])
          tt(cr[:], cr[:], den[:], ALU.mult, ["cr", "den"], ["cr"])
          tt(ci[:], pwi[:, 1, :], are2[:], ALU.mult, ["pwi1", "are2"], ["ci"])
          tt(t0[:], nr[:], aim2[:], ALU.mult, ["nr", "aim2"], ["t0"])
          tt(ci[:], ci[:], t0[:], ALU.subtract, ["ci", "t0"], ["ci"])
          tt(ci[:], ci[:], den[:], ALU.mult, ["ci", "den"], ["ci"])
          bre2 = sbuf(p0, "bre2", [128, 32, 16]); bim2 = sbuf(p0, "bim2", [128, 32, 16])
          tb = sbuf(p0, "tb", [128, 32, 16])
          pload(bre2[:], bre2_d.ap(), "bre2"); pload(bim2[:], bim2_d.ap(), "bim2")

          def bc(v):
              return v.unsqueeze(2).to_broadcast([128, 32, 16])

          tt(bbr[:], bre2[:], bc(cr[:]), ALU.mult, ["bre2", "cr"], ["bbr"])
          tt(tb[:], bim2[:], bc(ci[:]), ALU.mult, ["bim2", "ci"], ["tb"])
          tt(bbr[:], bbr[:], tb[:], ALU.subtract, ["bbr", "tb"], ["bbr"])
          tt(bbi[:], bim2[:], bc(cr[:]), ALU.mult, ["bim2", "cr"], ["bbi"])
          tt(tb[:], bre2[:], bc(ci[:]), ALU.mult, ["bre2", "ci"], ["tb"])
          tt(bbi[:], bbi[:], tb[:], ALU.add, ["bbi", "tb"], ["bbi"])

          Xrb = sbuf(p0, "Xrb", [128, 32, 2, 16], BF16); Xib = sbuf(p0, "Xib", [128, 32, 2, 16], BF16)
          Mr = bre2; Mi = bim2
          Cx = sbuf(p0, "Cx", [128, 32, 2, 2, 16], BF16)
          diagD = sbuf(p0, "diagD", [128, 128])
          ident_bf = sbuf(p0, "ident_bf", [128, 128], BF16)
          h1 = sbuf(p0, "h1", [128, 32, 16]); h2 = sbuf(p0, "h2", [128, 32, 16])
          Ct = h1; tb2 = h2
          cp(ident_bf[:], ident[:], ["ident"], ["ident_bf"], eng="pool")
          memset(Xrb[:], 0.0, ["Xrb"]); memset(Xib[:], 0.0, ["Xib"])
          memset(Cx[:], 0.0, ["Cx"])
          for e in range(2):
              ps_ = slice(64 * e, 64 * e + 64)
              cp(Cx[ps_, :, 0, e, :], cre2[ps_], ["cre2", "Cx"], ["Cx"], eng="pool")
              ts(Cx[ps_, :, 1, e, :], cim2[ps_], -1.0, None, ALU.mult, None, ["cim2", "Cx"], ["Cx"], eng="pool")

          def bcp(v):
              return v.unsqueeze(2).to_broadcast([128, 32, 16])

          p0_debug = (any(n_ in debug for n_ in ("Bmat", "Cmat", "Kblk")) and not DEBUG.get("force_capture")) or stop == "p0"
          S.capture = not p0_debug
          b7bf = bank[7][:].bitcast(BF16)
          for m in range(T):
              tt(Mr[:], bbr[:], bcp(pwr[:, m, :]), ALU.mult, ["bbr", "pwr%d" % m], ["bre2"])
              tt(tb[:], bbi[:], bcp(pwi[:, m, :]), ALU.mult, ["bbi", "pwi%d" % m], ["tb"])
              tt(Mr[:], Mr[:], tb[:], ALU.subtract, ["bre2", "tb"], ["bre2"])
              tt(Mi[:], bbi[:], bcp(pwr[:, m, :]), ALU.mult, ["bbi", "pwr%d" % m], ["bim2"])
              tt(tb[:], bbr[:], bcp(pwi[:, m, :]), ALU.mult, ["bbr", "pwi%d" % m], ["tb"])
              tt(Mi[:], Mi[:], tb[:], ALU.add, ["bim2", "tb"], ["bim2"])
              for e in range(2):
                  ps_ = slice(64 * e, 64 * e + 64)
                  cp(Xrb[ps_, :, e, :], Mr[ps_], ["bre2", "Xrb"], ["Xrb"])
                  cp(Xib[ps_, :, e, :], Mi[ps_], ["bim2", "Xib"], ["Xib"])
              S.mark()
              j = T - 1 - m
              for b in range(8):
                  for part, X, xk_ in ((0, Xrb, "Xrb"), (1, Xib, "Xib")):
                      trn(b7bf[:, 128 * part:128 * part + 128], X[:, 4 * b:4 * b + 4].rearrange("p a e c -> p (a e c)"), ident_bf[:],
                          [xk_, "ident_bf"], [BK[7]])
                  cp(Bmat[:, b, j, :, :], b7bf[:, 0:256].rearrange("p (t c) -> p t c", t=2), [BK[7]], ["Bmat"], eng="act")
                  S.mark()
                  pb = bank[6]
                  S.op("dve", lambda e: e.memset(pb[:, 0:128], 0.0), writes=[BK[6]])
                  for q in range(4):
                      pr_ = 4 * b + q
                      for part, Xb, xk_ in ((0, Xrb, "Xrb"), (1, Xib, "Xib")):
                          mm(pb[32 * q:32 * q + 32, 32 * q:32 * q + 32], Xb[:, pr_].rearrange("p e c -> p (e c)"),
                             Cx[:, pr_, part].rearrange("p e c -> p (e c)"), part == 0, part == 1, [xk_, "Cx"], [BK[6]], tp=(0, 32 * q))
                  if m == 0:
                      ts(diagD[:], ident[:], dskip[:, b:b + 1], None, ALU.mult, None, ["ident", "dskip"], ["diagD"])
                      tt(Kblk[:, b, m, :], pb[:, 0:128], diagD[:], ALU.add, [BK[6], "diagD"], ["Kblk"])
                  else:
                      cp(Kblk[:, b, m, :], pb[:, 0:128], [BK[6]], ["Kblk"])
                  S.mark()
          S.capture = None
          if p0_debug:
              dbg("Bmat", Bmat[:], [128, 8, T, 2, 128], ["Bmat"])
              dbg("Kblk", Kblk[:], [128, 8, T, 128], ["Kblk"])
          P0_KEYS = ["mvec", "ang5", "mag5", "sn5", "cs5", "sc0_n1", "sc0_r1", "nr", "den", "t0", "cr", "ci", "bre2", "bim2", "tb",
                     "Xrb", "Xib", "Cx", "diagD", "h1", "h2", "ident_bf"]
      if stop == "p0":
          return

      def ht_load(src_ap, slot):
          xk = "xb%d%s" % (slot, HB["tag"])
          dma("sp", HB["xb"][slot][:], src_ap, [], [xk], xk)

      def ht_norm(slot):
          tg_ = HB["tag"]
          x_ = HB["xb"][slot]; xk = "xb%d%s" % (slot, tg_)
          act(HB["sqj"][:], x_[:], AF.Square, [xk], ["sqj" + tg_, "ssum%d" % slot], accum=ssum[slot][:])
          act(rstd[slot][:], ssum[slot][:], AF.Sqrt, ["ssum%d" % slot, "epsc"], ["rstd%d" % slot], bias=epsc[:], scale=1.0 / D_MODEL)
          recip(rstd[slot][:], rstd[slot][:], ["rstd%d" % slot], ["rstd%d" % slot])
          stt(x_[:], x_[:], rstd[slot][:, 0:1], HB["nw"][:], ALU.mult, ALU.mult, [xk, "rstd%d" % slot, "nw_row" + tg_], [xk])

      def ht_tr(slot, dst_fn, dst_keys):
          x_ = HB["xb"][slot]; xk = "xb%d%s" % (slot, HB["tag"])
          for h in range(4):
              for k4 in range(4):
                  kc = 4 * h + k4
                  trn(bank[h][:, k4 * 128:(k4 + 1) * 128], x_[:, kc * 128:(kc + 1) * 128], ident[:], [xk, "ident"], [BK[h]])
              cp(dst_fn(4 * h, 4 * h + 4), bank[h][:].rearrange("p (a b) -> p a b", b=128), [BK[h]], dst_keys, eng="act")

      with ExitStack() as p1:
          Wu = sbuf(p1, "Wu", [128, KC, 1024], BF16)
          for i in range(4):
              dma("pool", Wu[:, 4 * i:4 * i + 4, :],
                  w_in.ap()[512 * i:512 * (i + 1), C_U:C_U + 1024].rearrange("(kc p) c -> p kc c", p=128),
                  [], ["Wu%d" % i], "Wu%d" % i)
          DtW = [sbuf(p1, "Dt%d" % w, [128, 32, 2, 128], BF16) for w in range(2)]
          Dt = DtW[1]
          with ExitStack() as pd:
              jrevs = []
              for w, base in ((0, 255), (1, 127)):
                  jr_i = sbuf(pd, "jr_i%d" % w, [128, 1], I32); jrev = sbuf(pd, "jrev%d" % w, [128, 1])
                  S.op("pool", lambda e, jr_i=jr_i, base=base: e.iota(jr_i[:], pattern=[[0, 1]], base=base, channel_multiplier=-1),
                       writes=["jr_i%d" % w])
                  cp(jrev[:], jr_i[:], ["jr_i%d" % w], ["jrev%d" % w])
                  jrevs.append(jrev)
              dtr = sbuf(pd, "dtr", [128, 64])
              dma("sp", dtr[:], bass.AP(ldt_row, 0, [[0, 128], [1, 64]]), [], ["dtr"], "p_dtr")
              act(dtr[:], dtr[:], AF.Exp, ["dtr"], ["dtr"])
              ar_c = sbuf(pd, "ar_c", [128, 16, 64]); ai_c = sbuf(pd, "ai_c", [128, 16, 64])
              mgB = [sbuf(pd, "mg_c%d" % i, [128, 16, 64]) for i in range(2)]
              anB = [sbuf(pd, "an_c%d" % i, [128, 16, 64]) for i in range(2)]
              snB = [sbuf(pd, "sn_c%d" % i, [128, 16, 64]) for i in range(2)]
              csB = [sbuf(pd, "cs_c%d" % i, [128, 16, 64]) for i in range(2)]
              arB = [ar_c, sbuf(pd, "ar_c1", [128, 16, 64])]
              aiB = [ai_c, sbuf(pd, "ai_c1", [128, 16, 64])]

              def load_chunk(c):
                  a_, i_ = arB[c % 2], aiB[c % 2]
                  ka, ki_ = "ar_c%d" % (c % 2), "ai_c%d" % (c % 2)
                  dma("sp", a_[:].rearrange("p a b -> p (a b)"), bass.AP(are_row, 1024 * c, [[0, 128], [1, 1024]]), [], [ka], "p_" + ka)
                  dma("sp", i_[:].rearrange("p a b -> p (a b)"), bass.AP(aim_row, 1024 * c, [[0, 128], [1, 1024]]), [], [ki_], "p_" + ki_)
                  dtb = dtr[:, 16 * c:16 * c + 16].unsqueeze(2).to_broadcast([128, 16, 64])
                  tt(a_[:], a_[:], dtb, ALU.mult, [ka, "dtr"], [ka])
                  tt(i_[:], i_[:], dtb, ALU.mult, [ki_, "dtr"], [ki_])

              def stage_a(it):
                  c, w = it // 2, it % 2
                  b_ = it % 2
                  a_, i_ = arB[c % 2], aiB[c % 2]
                  ka, ki_ = "ar_c%d" % (c % 2), "ai_c%d" % (c % 2)
                  jk = "jrev%d" % w
                  act(mgB[b_][:], a_[:], AF.Exp, [ka, jk], ["mg_c%d" % b_], scale=jrevs[w][:, 0:1])
                  act(anB[b_][:], i_[:], AF.Identity, [ki_, jk], ["an_c%d" % b_], scale=jrevs[w][:, 0:1])
                  sincos(pd, "scd%d" % b_, anB[b_][:], [128, 16, 64], ["an_c%d" % b_], snB[b_][:], csB[b_][:],
                         ["sn_c%d" % b_], ["cs_c%d" % b_])

              def stage_b(it):
                  c, w = it // 2, it % 2
                  b_ = it % 2
                  dv = DtW[w][:, 8 * c:8 * c + 8, :, :].rearrange("p a t (e q) -> p a t e q", e=2)
                  mgv = mgB[b_][:].rearrange("p (a e) q -> p a e q", e=2)
                  tt(dv[:, :, 0, :, :], mgv, csB[b_][:].rearrange("p (a e) q -> p a e q", e=2), ALU.mult,
                     ["mg_c%d" % b_, "cs_c%d" % b_], ["Dt%d" % w])
                  tt(dv[:, :, 1, :, :], mgv, snB[b_][:].rearrange("p (a e) q -> p a e q", e=2), ALU.mult,
                     ["mg_c%d" % b_, "sn_c%d" % b_], ["Dt%d" % w])

              load_chunk(0)
              load_chunk(1)
              stage_a(0)
              for it in range(8):
                  if it + 1 < 8:
                      stage_a(it + 1)
                  stage_b(it)
                  if it % 2 == 1 and it // 2 + 2 < 4:
                      load_chunk(it // 2 + 2)
              S.retire(["jr_i0", "jr_i1", "jrev0", "jrev1", "dtr", "ar_c0", "ai_c0", "ar_c1", "ai_c1"]
                       + ["%s%d" % (n_, i_) for n_ in ("mg_c", "an_c", "sn_c", "cs_c") for i_ in range(2)]
                       + ["scd%d_%s" % (i_, n_) for i_ in range(2) for n_ in ("n1", "r1")])
          dbg("Dt", Dt[:], [128, 32, 2, 128], ["Dt1"])
          if stop == "p1a":
              return

          alloc_ht(p1, "P")
          hTp = [sbuf(p1, "hTp%d" % i, [128, KC, 128], BF16) for i in range(2)]
          upre = [sbuf(p1, "upre%d" % i, [128, NBATCH, 1024], BF16) for i in range(2)]
          NTP = NBATCH // 2
          Gb = [sbuf(p1, "G%d" % i, [128, 2, NTP, 32, 16]) for i in range(2)]
          Sr = sbuf(p1, "Sr", [128, 32, 16]); Si = sbuf(p1, "Si", [128, 32, 16])
          memset(Sr[:], 0.0, ["Sr"]); memset(Si[:], 0.0, ["Si"])
          nbat = NPRE // NBATCH
          n_pending = len([a_ for a_ in S.pending if a_ is not None])
          per_step = (n_pending + NPRE - 9) // max(1, NPRE - 8) + 1

          GQ = []

          def pop_g(k):
              for _ in range(k):
                  if GQ:
                      GQ.pop(0)()

          def u_mm(n):
              hs = n % 2
              ub_ = upre[(n // NBATCH) % 2]; uk_ = "upre%d" % ((n // NBATCH) % 2)
              tl_ = n % NBATCH
              for half in range(2):
                  for kc in range(KC):
                      mm(bank[4 + half][:], hTp[hs][:, kc, :], Wu[:, kc, 512 * half:512 * half + 512], kc == 0, kc == KC - 1,
                         ["hTp%d" % hs, "Wu%d" % (kc // 4)], [BK[4 + half]])
                  cp(ub_[:, tl_, 512 * half:512 * half + 512], bank[4 + half][:], [BK[4 + half]], [uk_])
                  pop_g(2)

          def queue_g(bt):
              ub_ = upre[bt % 2]; uk_ = "upre%d" % (bt % 2)
              G = Gb[bt % 2]; Gk = "G%d" % (bt % 2)

              def g_round(pr0, sl):
                  gk = BK[6 + sl]
                  for prl in range(4):
                      for part in range(2):
                          for tp in range(NTP):
                              gi = (prl * 2 + part) * NTP + tp
                              for w in range(2):
                                  mm(bank[6 + sl][:, 32 * gi:32 * gi + 32], DtW[w][:, pr0 + prl, part, :],
                                     ub_[:, 2 * tp + w, 32 * (pr0 + prl):32 * (pr0 + prl) + 32], w == 0, w == 1,
                                     ["Dt%d" % w, uk_], [gk])
                  gv = bank[6 + sl][:].rearrange("p (a t b c) -> p t b a c", a=4, t=2, b=NTP, c=32)
                  for e in range(2):
                      ps_ = slice(64 * e, 64 * e + 64)
                      for part in range(2):
                          cp(G[ps_, part, :, pr0:pr0 + 4, :], gv[ps_, part, :, :, 16 * e:16 * e + 16], [gk], [Gk], eng="act")

              def horner():
                  for tl in range(NTP):
                      tt(h1[:], Sr[:], bc(Pr[:]), ALU.mult, ["Sr", "Pr"], ["h1"], eng="pool")
                      tt(h2[:], Si[:], bc(Pi[:]), ALU.mult, ["Si", "Pi"], ["h2"], eng="pool")
                      tt(h1[:], h1[:], h2[:], ALU.subtract, ["h1", "h2"], ["h1"], eng="pool")
                      tt(h2[:], Sr[:], bc(Pi[:]), ALU.mult, ["Sr", "Pi"], ["h2"], eng="pool")
                      tt(Sr[:], h1[:], G[:, 0, tl, :, :], ALU.add, ["h1", Gk], ["Sr"], eng="pool")
                      tt(h1[:], Si[:], bc(Pr[:]), ALU.mult, ["Si", "Pr"], ["h1"], eng="pool")
                      tt(h1[:], h1[:], h2[:], ALU.add, ["h1", "h2"], ["h1"], eng="pool")
                      tt(Si[:], h1[:], G[:, 1, tl, :, :], ALU.add, ["h1", Gk], ["Si"], eng="pool")

              cnt_ = 0
              for pr0 in range(0, 32, 4):
                  GQ.append(lambda pr0=pr0, sl=cnt_ % 2: g_round(pr0, sl))
                  cnt_ += 1
              GQ.append(horner)

          def g_and_horner(bt):
              pop_g(len(GQ))
              queue_g(bt)

          ht_load(xpre.ap()[0:128, :], 0)
          ht_load(xpre.ap()[128:256, :], 1)
          for s_ in range(NPRE + 2):
              if s_ < NPRE:
                  ht_norm(s_ % NXB)
              if 1 <= s_ <= NPRE:
                  n = s_ - 1
                  ht_tr(n % NXB, lambda a, b_, hs=n % 2: hTp[hs][:, a:b_, :], ["hTp%d" % (n % 2)])
              if s_ + 2 < NPRE:
                  ht_load(xpre.ap()[(s_ + 2) * 128:(s_ + 3) * 128, :], (s_ + 2) % NXB)
              if 2 <= s_:
                  n = s_ - 2
                  u_mm(n)
                  if n % NBATCH == NBATCH - 1:
                      g_and_horner(n // NBATCH)
              if s_ >= 4:
                  S.replay(per_step)
          pop_g(len(GQ))
          S.replay(len(S.pending))
          HK1 = ["h1"]; HK2 = ["h2"]; KSR = ["Sr"]; KSI = ["Si"]
          tt(h1[:], Sr[:], bbr[:], ALU.mult, KSR + ["bbr"], HK1)
          tt(h2[:], Si[:], bbi[:], ALU.mult, KSI + ["bbi"], HK2)
          tt(h1[:], h1[:], h2[:], ALU.subtract, HK1 + HK2, HK1)
          S.op("dve", lambda e: e.tensor_reduce(out=sin_re[:], in_=h1[:], axis=AX.X, op=ALU.add), reads=HK1, writes=["sin_re"])
          tt(h1[:], Sr[:], bbi[:], ALU.mult, KSR + ["bbi"], HK1)
          tt(h2[:], Si[:], bbr[:], ALU.mult, KSI + ["bbr"], HK2)
          tt(h1[:], h1[:], h2[:], ALU.add, HK1 + HK2, HK1)
          S.op("dve", lambda e: e.tensor_reduce(out=sin_im[:], in_=h1[:], axis=AX.X, op=ALU.add), reads=HK1, writes=["sin_im"])
          tt(rs_re[:], sin_re[:], rho[:], ALU.mult, ["sin_re", "rho"], ["rs_re"])
          tt(rs_im[:], sin_im[:], rho[:], ALU.mult, ["sin_im", "rho"], ["rs_im"])
          cp(sinb_re[:], sin_re[:], ["sin_re"], ["sinb_re"])
          cp(sinb_im[:], sin_im[:], ["sin_im"], ["sinb_im"])
          dbg("sin_re", sin_re[:], [128, 32], ["sin_re"])
          dbg("sin_im", sin_im[:], [128, 32], ["sin_im"])
          S.retire(ht_keys())
          S.retire(["Wu0", "Wu1", "Wu2", "Wu3", "Dt0", "Dt1", "hTp0", "hTp1", "upre0", "upre1", "G0", "G1", "Sr", "Si"])
      S.retire(P0_KEYS)
      p0.close()
      if DEBUG.get("force_capture"):
          dbg("Bmat", Bmat[:], [128, 8, T, 2, 128], ["Bmat"])
          dbg("Kblk", Kblk[:], [128, 8, T, 128], ["Kblk"])
      if stop == "p1":
          return

      Cmat = sbuf(P, "Cmat", [128, 32, T, 2, 32], BF16)
      mixa = sbuf(P, "mixa", [128, 8, TOK], BF16)
      ybuf = sbuf(P, "ybuf", [128, 8, TOK], BF16)
      zsg = sbuf(P, "zsg", [128, 8, TOK], BF16)
      with ExitStack() as p2:
          hTm = sbuf(p2, "hTm", [128, KC, HALO + TOK], BF16)
          NWS = 3
          wsl = [sbuf(p2, "wsl%d" % i, [128, KC, 256], BF16) for i in range(NWS)]
          with ExitStack() as ph:
              alloc_ht(ph, "M")
              ht_load(xmain.ap()[0:128, :], 0)
              ht_load(xmain.ap()[128:256, :], 1)
              for s_ in range(NT + 2):
                  if s_ < NT + 1:
                      ht_norm(s_ % NXB)
                  if 1 <= s_:
                      tl = s_ - 1
                      ht_tr(tl % NXB, lambda a, b_, tl=tl: hTm[:, a:b_, tl * 128:(tl + 1) * 128], ["hTm%d" % tl])
                  if s_ + 2 < NT + 1:
                      ht_load(xmain.ap()[(s_ + 2) * 128:(s_ + 3) * 128, :], (s_ + 2) % NXB)
              S.retire(ht_keys())
          HT_ALL = ["hTm%d" % t_ for t_ in range(NT + 1)]
          HT_MAIN = HT_ALL[1:]

          wctr = [0]

          def load_cols(col0, ncols, dup=False):
              s = wctr[0] % NWS
              wctr[0] += 1
              k = "wsl%d" % s
              if dup:
                  for hh in range(2):
                      dma("pool", wsl[s][:, :, 64 * hh:64 * hh + 64],
                          w_in.ap()[:, col0:col0 + 64].rearrange("(kc p) c -> p kc c", p=128), [], [k], k + ("a" if hh == 0 else "b"))
              else:
                  dma("pool", wsl[s][:, :, 0:ncols], w_in.ap()[:, col0:col0 + ncols].rearrange("(kc p) c -> p kc c", p=128), [], [k], k + "a")
              return wsl[s], k

          def proj_fm(wt, wk, coff, banks, ntok0, ntoks):
              for (bi, t0_, n_) in banks:
                  hk_ = ["hTm%d" % t_ for t_ in range(t0_ // 128, (t0_ + n_ + 127) // 128)]
                  for kc in range(KC):
                      mm(bank[bi][:, 0:n_], wt[:, kc, coff:coff + 128], hTm[:, kc, t0_:t0_ + n_], kc == 0, kc == KC - 1,
                         [wk] + hk_, [BK[bi]])

          pa = ExitStack()
          kT2 = [sbuf(pa, "kT2_%d" % kv, [128, HALO + TOK], BF16) for kv in range(2)]
          v_sb = sbuf(pa, "v_sb", [128, NT + 1, 128], BF16)
          qf = sbuf(pa, "qf", [128, HALO + TOK]); sq = sbuf(pa, "sq", [128, HALO + TOK])

          def qk_norm(nb_, ntok, wcol, scale_imm, dst, dkeys):
              per = ntok // nb_
              for i in range(nb_):
                  sl = slice(i * per, (i + 1) * per)
                  act(sq[:, sl], bank[i][:, 0:per], AF.Square, [BK[i]], ["sq"])
                  cp(qf[:, sl], bank[i][:, 0:per], [BK[i]], ["qf"], eng="act")
              for i in range(nb_):
                  sl = slice(i * per, (i + 1) * per)
                  mm(bank[i][:, 0:per], blockones[:], sq[:, sl], True, True, ["blockones", "sq"], [BK[i]])
              for i in range(nb_):
                  sl = slice(i * per, (i + 1) * per)
                  act(sq[:, sl], bank[i][:, 0:per], AF.Sqrt, [BK[i], "epsc"], ["sq"], bias=epsc[:], scale=1.0 / 64.0)
              recip(sq[:, 0:ntok], sq[:, 0:ntok], ["sq"], ["sq"])
              tt(qf[:, 0:ntok], qf[:, 0:ntok], sq[:, 0:ntok], ALU.mult, ["qf", "sq"], ["qf"])
              ts(dst, qf[:, 0:ntok], wcol, scale_imm, ALU.mult, ALU.mult, ["qf", "qw2", "kw2"], dkeys)

          for kv in range(2):
              wt, wk = load_cols(C_K + 64 * kv, 64, dup=True)
              proj_fm(wt, wk, 0, [(0, 0, 384), (1, 384, 384), (2, 768, 384)], 0, HALO + TOK)
              qk_norm(3, HALO + TOK, kw2[:, 0:1], 1.0, kT2[kv][:], ["kT2_%d" % kv])
          wt, wk = load_cols(C_V, 128)
          for tl in range(NT + 1):
              bi = 6 + (tl % 2)
              for kc in range(KC):
                  mm(bank[bi][:, 0:128], hTm[:, kc, tl * 128:(tl + 1) * 128], wt[:, kc, 0:128], kc == 0, kc == KC - 1,
                     [wk, "hTm%d" % tl], [BK[bi]])
              cp(v_sb[:, tl, :], bank[bi][:, 0:128], [BK[bi]], ["v_sb"])
          dbg("kT2_0", kT2[0][:], [128, HALO + TOK], ["kT2_0"])
          dbg("v_sb", v_sb[:], [128, NT + 1, 128], ["v_sb"])

          d_i = sbuf(pa, "d_i", [128, 128], I32); dmat = sbuf(pa, "dmat", [128, 128])
          S.op("pool", lambda e: e.iota(d_i[:], pattern=[[1, 128]], base=0, channel_multiplier=-1), writes=["d_i"])
          cp(dmat[:], d_i[:], ["d_i"], ["dmat"])
          mcur = sbuf(pa, "mcur", [128, 128]); mprev = sbuf(pa, "mprev", [128, 128])
          dcl = sbuf(pa, "dcl", [128, 128]); dpl = sbuf(pa, "dpl", [128, 128])
          S.op("dve", lambda e: e.tensor_single_scalar(out=mcur[:], in_=dmat[:], scalar=0.0, op=ALU.is_ge), reads=["dmat"], writes=["mcur"])
          S.op("dve", lambda e: e.tensor_single_scalar(out=mprev[:], in_=dmat[:], scalar=0.0, op=ALU.is_lt), reads=["dmat"], writes=["mprev"])
          ts(dcl[:], dmat[:], 0.0, None, ALU.max, None, ["dmat"], ["dcl"])
          ts(dpl[:], dmat[:], 128.0, None, ALU.add, None, ["dmat"], ["dpl"])
          Etab = [sbuf(pa, "Etab%d" % i, [128, 2, 2, 128]) for i in range(2)]
          qT = [sbuf(pa, "qT%d" % i, [128, TOK], BF16) for i in range(2)]
          zaT = [sbuf(pa, "zaT%d" % i, [128, TOK], BF16) for i in range(2)]
          pexp = [sbuf(pa, "pexp%d" % i, [128, 512]) for i in range(2)]
          Pt = [[sbuf(pa, "Pt%d%d" % (e, c), [128, 512], BF16) for c in range(2)] for e in range(2)]
          rec = sbuf(pa, "rec", [128, 512]); onum = sbuf(pa, "onum", [128, 512])
          CtX = ybuf[:, 0, :].bitcast(F32).rearrange("p (a c) -> p a c", c=16)
          tb2X = ybuf[:, 1, :].bitcast(F32).rearrange("p (a c) -> p a c", c=16)
          memset(Cmat[:], 0.0, ["Cmat"])
          CMQ = []

          def cm_piece(i, part):
              m = i + 1
              if part == 0:
                  tt(CtX, cre2[:], bcp(pwr[:, m, :]), ALU.mult, ["cre2", "pwr%d" % m], ["ybuf0"], eng="pool")
                  tt(tb2X, cim2[:], bcp(pwi[:, m, :]), ALU.mult, ["cim2", "pwi%d" % m], ["ybuf1"], eng="pool")
                  tt(CtX, CtX, tb2X, ALU.subtract, ["ybuf0", "ybuf1"], ["ybuf0"], eng="pool")
                  for e in range(2):
                      ps_ = slice(64 * e, 64 * e + 64)
                      cp(Cmat[ps_, :, i, 0, 16 * e:16 * e + 16], CtX[ps_], ["ybuf0", "Cmat"], ["Cmat"], eng="pool")
              else:
                  tt(CtX, cre2[:], bcp(pwi[:, m, :]), ALU.mult, ["cre2", "pwi%d" % m], ["ybuf0"], eng="pool")
                  tt(tb2X, cim2[:], bcp(pwr[:, m, :]), ALU.mult, ["cim2", "pwr%d" % m], ["ybuf1"], eng="pool")
                  tt(CtX, CtX, tb2X, ALU.add, ["ybuf0", "ybuf1"], ["ybuf0"], eng="pool")
                  for e in range(2):
                      ps_ = slice(64 * e, 64 * e + 64)
                      ts(Cmat[ps_, :, i, 1, 16 * e:16 * e + 16], CtX[ps_], -1.0, None, ALU.mult, None, ["ybuf0", "Cmat"], ["Cmat"], eng="pool")

          for i_c in range(T):
              for part_c in range(2):
                  CMQ.append((i_c, part_c))

          WQ = {}

          def attn_prep(i):
              kv = i // 4
              s2 = i % 2
              for e in range(2):
                  h = 2 * i + e
                  slope = 2.0 ** (-8.0 * (h + 1) / 16.0)
                  act(Etab[s2][:, e, 0, :], dcl[:], AF.Exp, ["dcl"], ["Etab%d" % s2], scale=-slope)
                  tt(Etab[s2][:, e, 0, :], Etab[s2][:, e, 0, :], mcur[:], ALU.mult, ["Etab%d" % s2, "mcur"], ["Etab%d" % s2])
                  act(Etab[s2][:, e, 1, :], dpl[:], AF.Exp, ["dpl"], ["Etab%d" % s2], scale=-slope)
                  tt(Etab[s2][:, e, 1, :], Etab[s2][:, e, 1, :], mprev[:], ALU.mult, ["Etab%d" % s2, "mprev"], ["Etab%d" % s2])
              if i % 2 == 0:
                  WQ['q'] = load_cols(C_Q + 128 * i, 256)
                  if CMQ:
                      cm_piece(*CMQ.pop(0))
                  WQ['z'] = load_cols(C_ZA + 128 * i, 256)
                  if CMQ:
                      cm_piece(*CMQ.pop(0))
              coff = 128 * (i % 2)
              proj_fm(WQ['q'][0], WQ['q'][1], coff, [(0, HALO, 512), (1, HALO + 512, 512)], HALO, TOK)
              qk_norm(2, TOK, qw2[:, 0:1], 0.125, qT[s2][:], ["qT%d" % s2])
              proj_fm(WQ['z'][0], WQ['z'][1], coff, [(0, HALO, 512), (1, HALO + 512, 512)], HALO, TOK)
              for hb in range(2):
                  act(zaT[s2][:, 512 * hb:512 * hb + 512], bank[hb][:], AF.Silu, [BK[hb]], ["zaT%d" % s2])
              if i == 0:
                  dbg("qT0", qT[0][:], [128, TOK], ["qT0"])

          def attn_core(i):
              kv = i // 4
              s2 = i % 2
              for hq in range(2):
                  for e in range(2):
                      ps_ = slice(64 * e, 64 * e + 64)
                      for c in range(2):
                          bi = 2 + 2 * e + c
                          for qb in range(4):
                              qabs = 4 * hq + qb
                              tok0 = (HALO + 128 * qabs) if c == 0 else (128 * qabs)
                              mm(bank[bi][:, 128 * qb:128 * qb + 128], kT2[kv][ps_, tok0:tok0 + 128],
                                 qT[s2][ps_, 128 * qabs:128 * qabs + 128], True, True,
                                 ["kT2_%d" % kv, "qT%d" % s2], [BK[bi]], tp=(64 * e, 0))
                  for e in range(2):
                      for c in range(2):
                          bi = 2 + 2 * e + c
                          px = pexp[c]; pk = "pexp%d" % c
                          act(px[:], bank[bi][:], AF.Exp, [BK[bi]], [pk])
                          tt(Pt[e][c][:].rearrange("p (a b) -> p a b", b=128), px[:].rearrange("p (a b) -> p a b", b=128),
                             Etab[s2][:, e, c, :].unsqueeze(1).to_broadcast([128, 4, 128]), ALU.mult,
                             [pk, "Etab%d" % s2], ["Pt%d%d" % (e, c)])
                          if c == 1 and hq == 0:
                              ts(Pt[e][1][:, 0:128], Pt[e][1][:, 0:128], flag[:, 0:1], None, ALU.mult, None,
                                 ["Pt%d1" % e, "flag"], ["Pt%d1" % e])
                  for e in range(2):
                      ps_ = slice(64 * e, 64 * e + 64)
                      for qb in range(4):
                          qabs = 4 * hq + qb
                          cs_ = slice(128 * qb, 128 * qb + 128)
                          for c, blk in ((1, qabs), (0, qabs + 1)):
                              mm(bank[6][ps_, cs_], v_sb[:, blk, 64 * kv:64 * kv + 64], Pt[e][c][:, cs_], c == 1, c == 0,
                                 ["v_sb", "Pt%d%d" % (e, c)], [BK[6]], tp=(0, 64 * e))
                              mm(bank[7][ps_, cs_], ones_bf[:], Pt[e][c][:, cs_], c == 1, c == 0,
                                 ["ones_bf", "Pt%d%d" % (e, c)], [BK[7]], tp=(0, 64 * e))
                  ts(rec[:], bank[7][:], sinkx[:, i:i + 1], None, ALU.add, None, [BK[7], "sinkx"], ["rec"])
                  recip(rec[:], rec[:], ["rec"], ["rec"])
                  tt(onum[:], bank[6][:], rec[:], ALU.mult, [BK[6], "rec"], ["onum"])
                  tt(mixa[:, i, 512 * hq:512 * hq + 512], onum[:], zaT[s2][:, 512 * hq:512 * hq + 512], ALU.mult,
                     ["onum", "zaT%d" % s2], ["mixa%d" % i])

          attn_prep(0)
          for i in range(8):
              if i + 1 < 8:
                  attn_prep(i + 1)
              attn_core(i)
          while CMQ:
              cm_piece(*CMQ.pop(0))
          dbg("Cmat", Cmat[:], [128, 32, T, 2, 32], ["Cmat"])
          dbg("mixa", mixa[:], [128, 8, TOK], ["mixa%d" % i for i in range(8)])
          S.retire(["kT2_0", "kT2_1", "v_sb", "qf", "sq", "d_i", "dmat", "mcur", "mprev", "dcl", "dpl", "Etab0", "Etab1",
                    "qT0", "qT1", "zaT0", "zaT1", "pexp0", "pexp1", "Pt00", "Pt01", "Pt10", "Pt11", "rec", "onum"])
          pa.close()
          if stop == "pa":
              return

          with ExitStack() as p3:
              uT = [sbuf(p3, "uT%d" % i, [128, TOK], BF16) for i in range(2)]
              cosT = sbuf(p3, "cosT", [128, 4, NCH]); sinT = sbuf(p3, "sinT", [128, 4, NCH])
              coefT = sbuf(p3, "coefT", [128, 4, NCH])
              Btr = sbuf(p3, "Btr", [128, 4, NCH]); Bti = sbuf(p3, "Bti", [128, 4, NCH])
              rr = sbuf(p3, "rr", [128, 4, NCH]); ri = sbuf(p3, "ri", [128, 4, NCH])
              m1 = sbuf(p3, "m1", [128, 4, NCH]); m2 = sbuf(p3, "m2", [128, 4, NCH])
              psi = m1
              Spr = sbuf(p3, "Spr", [128, 4, NCH], BF16); Spi = sbuf(p3, "Spi", [128, 4, NCH], BF16)
              g1 = Btr[:].rearrange("p a b -> p (a b)"); g2 = Bti[:].rearrange("p a b -> p (a b)")
              _sc_cache["scm"] = (rr, ri)
              WU = {}

              def ssm_proj(b):
                  us = uT[b % 2]; uk = "uT%d" % (b % 2)
                  if b % 2 == 0:
                      WU["u"] = load_cols(C_U + 128 * b, 256)
                      WU["z"] = load_cols(C_ZS + 128 * b, 256)
                  coff = 128 * (b % 2)
                  proj_fm(WU["u"][0], WU["u"][1], coff, [(0, HALO, 512), (1, HALO + 512, 512)], HALO, TOK)
                  for hb in range(2):
                      cp(us[:, 512 * hb:512 * hb + 512], bank[hb][:], [BK[hb]], [uk], eng="act")
                  proj_fm(WU["z"][0], WU["z"][1], coff, [(0, HALO, 512), (1, HALO + 512, 512)], HALO, TOK)
                  for hb in range(2):
                      act(zsg[:, b, 512 * hb:512 * hb + 512], bank[hb][:], AF.Silu, [BK[hb]], ["zsg%d" % b])

              ssm_proj(0)
              for b in range(8):
                  us = uT[b % 2]; uk = "uT%d" % (b % 2)
                  for q in range(4):
                      pr_ = 4 * b + q
                      ts(psi[:, q, :], kk1[:], phi[:, pr_:pr_ + 1], None, ALU.mult, None, ["kk1", "phi"], ["m1"])
                      ts(coefT[:, q, :], kk1[:], 0.0, rho[:, pr_:pr_ + 1], ALU.mult, ALU.add, ["kk1", "rho"], ["coefT"])
                  memset(coefT[:, :, 0:1], 0.0, ["coefT"])
                  sincos(p3, "scm", psi[:], [128, 4, NCH], ["m1"], sinT[:], cosT[:], ["sinT"], ["cosT"], keys=("rr", "ri"))
                  for half in range(2):
                      yb2 = bank[2 + half]
                      uvT = us[:, 512 * half:512 * half + 512].rearrange("p (k i) -> p i k", i=T)
                      for m in range(T):
                          mm(yb2[:, 128 * m:128 * T], Kblk[:, b, m, :], uvT[:, 0:T - m, :], m == 0, False, ["Kblk", uk], [BK[2 + half]])
                  uv4 = us[:].rearrange("p (k i) -> p k i", i=T)
                  for part in range(2):
                      for j in range(T):
                          for q in range(4):
                              mm(bank[4 + q][:, 256 * part:256 * part + 256], Bmat[32 * q:32 * q + 32, b, j, part, :],
                                 uv4[32 * q:32 * q + 32, :, j], j == 0, j == T - 1, ["Bmat", uk], [BK[4 + q]], tp=(32 * q, 0))
                  for q in range(4):
                      cp(Btr[:, q, :], bank[4 + q][:, 0:256], [BK[4 + q]], ["Btr"], eng="act")
                      cp(Bti[:, q, :], bank[4 + q][:, 256:512], [BK[4 + q]], ["Bti"], eng="act")
                  tt(m1[:], Btr[:], cosT[:], ALU.mult, ["Btr", "cosT"], ["m1"])
                  tt(m2[:], Bti[:], sinT[:], ALU.mult, ["Bti", "sinT"], ["m2"])
                  tt(rr[:], m1[:], m2[:], ALU.add, ["m1", "m2"], ["rr"])
                  tt(m1[:], Bti[:], cosT[:], ALU.mult, ["Bti", "cosT"], ["m1"])
                  tt(m2[:], Btr[:], sinT[:], ALU.mult, ["Btr", "sinT"], ["m2"])
                  tt(ri[:], m1[:], m2[:], ALU.subtract, ["m1", "m2"], ["ri"])
                  tt(rr[:, :, 0], rr[:, :, 0], rs_re[:, 4 * b:4 * b + 4], ALU.add, ["rr", "rs_re"], ["rr"])
                  tt(ri[:, :, 0], ri[:, :, 0], rs_im[:, 4 * b:4 * b + 4], ALU.add, ["ri", "rs_im"], ["ri"])
                  fl = lambda t_: t_[:].rearrange("p a b -> p (a b)")
                  S.op("dve", lambda e: e.tensor_tensor_scan(out=fl(Btr), data0=fl(coefT), data1=fl(rr), initial=0.0,
                                                             op0=ALU.mult, op1=ALU.add), reads=["coefT", "rr"], writes=["Btr"])
                  S.op("dve", lambda e: e.tensor_tensor_scan(out=fl(Bti), data0=fl(coefT), data1=fl(ri), initial=0.0,
                                                             op0=ALU.mult, op1=ALU.add), reads=["coefT", "ri"], writes=["Bti"])
                  tt(m1[:], Btr[:], cosT[:], ALU.mult, ["Btr", "cosT"], ["m1"])
                  tt(m2[:], Bti[:], sinT[:], ALU.mult, ["Bti", "sinT"], ["m2"])
                  tt(Spr[:, :, 1:NCH], m1[:, :, 0:NCH - 1], m2[:, :, 0:NCH - 1], ALU.subtract, ["m1", "m2"], ["Spr"])
                  tt(m1[:], Btr[:], sinT[:], ALU.mult, ["Btr", "sinT"], ["m1"])
                  tt(m2[:], Bti[:], cosT[:], ALU.mult, ["Bti", "cosT"], ["m2"])
                  tt(Spi[:, :, 1:NCH], m1[:, :, 0:NCH - 1], m2[:, :, 0:NCH - 1], ALU.add, ["m1", "m2"], ["Spi"])
                  cp(Spr[:, :, 0], sinb_re[:, 4 * b:4 * b + 4], ["sinb_re", "Spr"], ["Spr"])
                  cp(Spi[:, :, 0], sinb_im[:, 4 * b:4 * b + 4], ["sinb_im", "Spi"], ["Spi"])
                  if b + 1 < 8:
                      ssm_proj(b + 1)
                  for half in range(2):
                      yb2 = bank[2 + half]
                      for q in range(4):
                          pr_ = 4 * b + q
                          for i_ in range(T):
                              for part, Sp, sk in ((0, Spr, "Spr"), (1, Spi, "Spi")):
                                  mm(yb2[32 * q:32 * q + 32, 128 * i_:128 * i_ + 128], Cmat[:, pr_, i_, part, :], Sp[:, q, 128 * half:128 * half + 128],
                                     False, (i_ == T - 1 and part == 1), ["Cmat", sk], [BK[2 + half]], tp=(0, 32 * q))
                  for half in range(2):
                      sl = slice(512 * half, 512 * half + 512)
                      yb = bank[2 + half][:].rearrange("p (i k) -> p k i", i=T)
                      g1v = g1[:, sl].rearrange("p (k i) -> p k i", i=T)
                      g2v = g2[:, sl].rearrange("p (k i) -> p k i", i=T)
                      act(g1v, yb, AF.Square, [BK[2 + half]], ["Btr"])
                      ts(g1[:, sl], g1[:, sl], 0.044715, 1.0, ALU.mult, ALU.add, ["Btr"], ["Btr"])
                      tt(g1v, g1v, yb, ALU.mult, ["Btr", BK[2 + half]], ["Btr"])
                      act(g2[:, sl], g1[:, sl], AF.Sigmoid, ["Btr"], ["Bti"], scale=1.5957691216057308)
                      tt(ybuf[:, b, sl].rearrange("p (k i) -> p k i", i=T), g2v, yb, ALU.mult, ["Bti", BK[2 + half]], ["ybuf%d" % b])
                      if b == 0:
                          pass
              dbg("ybuf", ybuf[:], [128, 8, TOK], ["ybuf%d" % b for b in range(8)])
              S.retire(["uT0", "uT1", "cosT", "sinT", "coefT", "Btr", "Bti", "rr", "ri", "m1", "m2", "Spr", "Spi"])
          S.retire(HT_ALL + ["wsl%d" % i for i in range(NWS)])
          if stop == "p3":
              return

      with ExitStack() as p4:
          mixs = sbuf(p4, "mixs", [128, 8, TOK], BF16)
          wg = [sbuf(p4, "wg%d" % i, [128, 8, 128], BF16) for i in range(3)]
          sg = sbuf(p4, "sg", [128, TOK]); tg = sbuf(p4, "tg", [128, TOK])
          YB = ["ybuf%d" % b for b in range(8)]
          wo = [sbuf(p4, "wo%d" % i, [128, KC, 512], BF16) for i in range(2)]
          xr = [sbuf(p4, "xr%d" % i, [128, 512]) for i in range(3)]
          ot = [sbuf(p4, "ot%d" % i, [128, 512]) for i in range(3)]

          def load_wo(c):
              s_ = c % 2
              for hh in range(2):
                  dma("pool", wo[s_][:, 8 * hh:8 * hh + 8, :],
                      w_out.ap()[1024 * hh:1024 * hh + 1024, 512 * c:512 * c + 512].rearrange("(kc p) c -> p kc c", p=128),
                      [], ["wo%d_%d" % (s_, hh)], "wo%d_%d" % (s_, hh))

          for f in range(8):
              if f == 3:
                  load_wo(0)
              if f == 6:
                  load_wo(1)
              s = f % 3
              dma("pool", wg[s][:], w_glu.ap()[:, 128 * f:128 * f + 128].rearrange("(kc p) c -> p kc c", p=128), [], ["wg%d" % s], "wg%d" % s)
              b0 = 2 * (f % 2)
              for half in range(2):
                  for kc in range(8):
                      mm(bank[b0 + half][:], wg[s][:, kc, :], ybuf[:, kc, 512 * half:512 * half + 512], kc == 0, kc == 7,
                         ["wg%d" % s] + YB, [BK[b0 + half]])
              for half in range(2):
                  sl = slice(512 * half, 512 * half + 512)
                  act(sg[:, sl], bank[b0 + half][:], AF.Sigmoid, [BK[b0 + half], "bglu"], ["sg"], bias=bglu[:, f:f + 1], scale=1.0)
              tt(tg[:], sg[:], ybuf[:, f, :], ALU.mult, ["sg", "ybuf%d" % f], ["tg"])
              tt(mixs[:, f, :], tg[:], zsg[:, f, :], ALU.mult, ["tg", "zsg%d" % f], ["mixs%d" % f])
          dbg("mixs", mixs[:], [128, 8, TOK], ["mixs%d" % f for f in range(8)])
          MIXA = ["mixa%d" % i for i in range(8)]; MIXS = ["mixs%d" % i for i in range(8)]
          cnt = 0
          for c in range(4):
              s = c % 2
              if c >= 2:
                  load_wo(c)
              for tl in range(NT):
                  bi = cnt % 8
                  xs = cnt % 3
                  cnt += 1
                  dma("sp", xr[xs][:], xmain.ap()[HALO + 128 * tl:HALO + 128 * tl + 128, 512 * c:512 * c + 512], [], ["xr%d" % xs], "xr%d" % xs)
                  for kc in range(KC):
                      lhs = mixa[:, kc, 128 * tl:128 * tl + 128] if kc < 8 else mixs[:, kc - 8, 128 * tl:128 * tl + 128]
                      mm(bank[bi][:], lhs, wo[s][:, kc, :], kc == 0, kc == KC - 1,
                         ["wo%d_%d" % (s, kc // 8)] + (MIXA if kc < 8 else MIXS), [BK[bi]])
                  tt(ot[xs][:], bank[bi][:], xr[xs][:], ALU.add, [BK[bi], "xr%d" % xs], ["ot%d" % xs])
                  fin.append(dma("sp", out_d.ap()[128 * tl:128 * tl + 128, 512 * c:512 * c + 512], ot[xs][:], ["ot%d" % xs], [], "ot%d" % xs))


    body()
    for st_ in _p0h:
        st_.close()
    S.emit(top, final_wait_ops=fin)
    top.close()
    return nc, dbg_out


def make_in_maps(inputs):
    f32 = np.float32
    x = np.asarray(inputs["x"], f32)[0]
    g = lambda k: np.asarray(inputs[k], f32)
    a_re, a_im, ldt = g("ssm_a_re"), g("ssm_a_im"), g("ssm_log_dt")

    def st2(a):
        return np.ascontiguousarray(a.reshape(32, 2, 64).transpose(1, 2, 0).reshape(128, 32))

    def b2(a):
        return np.ascontiguousarray(a.reshape(32, 2, 64, 16).transpose(1, 2, 0, 3).reshape(128, 32, 16))

    def c2(a):
        return np.ascontiguousarray(a.reshape(32, 2, 16, 64).transpose(1, 3, 0, 2).reshape(128, 32, 16))

    shared = {
        "w_in": g("w_in"), "w_glu": g("w_glu"), "w_out": g("w_out"),
        "normw_row": g("norm_w").reshape(1, D_MODEL),
        "qw2": np.ascontiguousarray(np.tile(g("q_norm_w"), 2).reshape(128, 1)),
        "kw2": np.ascontiguousarray(np.tile(g("k_norm_w"), 2).reshape(128, 1)),
        "sinkcol": np.ascontiguousarray(g("attn_sinks").reshape(8, 2).T.repeat(64, axis=0)),
        "are2": st2(a_re), "aim2": st2(a_im),
        "ldt2": st2(np.broadcast_to(ldt[:, None], (64, 64))),
        "bre2": b2(g("ssm_b_re")), "bim2": b2(g("ssm_b_im")),
        "cre2": c2(g("ssm_c_re")), "cim2": c2(g("ssm_c_im")),
        "are_row": a_re.reshape(1, 4096), "aim_row": a_im.reshape(1, 4096), "ldt_row": ldt.reshape(1, 64),
        "dskip": np.ascontiguousarray(g("ssm_d").reshape(8, 128).T),
        "bglu": np.ascontiguousarray(g("b_glu").reshape(8, 128).T),
    }
    maps = []
    for c in range(NCORES):
        t0 = c * TOK
        xm = np.zeros((HALO + TOK, D_MODEL), f32)
        xm[HALO:] = x[t0:t0 + TOK]
        if c > 0:
            xm[:HALO] = x[t0 - HALO:t0]
        xp = np.zeros((NPRE * 128, D_MODEL), f32)
        if c > 0:
            xp[NPRE * 128 - t0:] = x[:t0]
        m = dict(shared)
        m["xpre"] = xp
        m["xmain"] = xm
        m["flag"] = np.full((128, 1), 0.0 if c == 0 else 1.0, f32)
        maps.append(m)
    return maps


_NC_CACHE = {}


def kernel(**inputs):
    dbg = tuple(sorted(DEBUG.get("names", ())))
    if dbg not in _NC_CACHE:
        _NC_CACHE[dbg] = build_nc(debug=dbg, stop=DEBUG.get("stop"))
    nc, dbg_out = _NC_CACHE[dbg]
    maps = make_in_maps(inputs)
    cores = DEBUG.get("cores", list(range(NCORES)))
    res = run_bass_kernel_spmd(nc, [maps[c] for c in cores], core_ids=list(range(len(cores))))
    if dbg:
        DEBUG["results"] = res.results
    out = np.concatenate([r["out"] for r in res.results], axis=0)
    return out.reshape(1, -1, D_MODEL).astype(np.float32)
```
